# Optimizing a Trainium2 kernel written in Bass

```python
import math
import jax, jax.numpy as jnp
from jax import lax
import numpy as np

D_MODEL = 1024
BATCH = 4
SEQ = 8192
DEPTH = 1

MEM_LEN = 256
HEAD_DIM = 64
NSA_HEADS = 8
NSA_KV_HEADS = 2
NSA_GROUP = NSA_HEADS // NSA_KV_HEADS
SB_HEADS = 8
MIX_WIDTH = (NSA_HEADS + SB_HEADS) * HEAD_DIM
CMP_LEN = 32
CMP_STRIDE = 16
CMP_HIDDEN = 256
SEL_BLOCK = 64
SEL_TOPK = 16
WINDOW = 512
Q_BLOCK = 128
ROT_DIM = HEAD_DIM // 4
ROPE_THETA = 500000.0
MEM_HEADS = 4
D_FF = 2816
NORM_EPS = 1e-6
NEG = -1e30
FORCE_SCORE = 1e4

KV_W = NSA_KV_HEADS * HEAD_DIM
IN_SPLITS = [NSA_HEADS * HEAD_DIM, KV_W, KV_W, KV_W, KV_W, KV_W, KV_W, NSA_HEADS * 3,
             SB_HEADS * HEAD_DIM, SB_HEADS * HEAD_DIM, SB_HEADS * HEAD_DIM]
IN_COLS = sum(IN_SPLITS)

kernel_name = "hymba_nsa_stickbreak_macaron_memory"


def rms_norm(x, g):
    x32 = x.astype(jnp.float32)
    y = x32 * lax.rsqrt(jnp.mean(x32 * x32, axis=-1, keepdims=True) + NORM_EPS)
    return (y * g.astype(jnp.float32)).astype(x.dtype)


def rope_angles(pos):
    freqs = ROPE_THETA ** (-jnp.arange(0, ROT_DIM, 2, dtype=jnp.float32) / ROT_DIM)
    return pos.astype(jnp.float32)[..., None] * freqs


def apply_rope(x, ang):
    half = ROT_DIM // 2
    x1, x2, xp = x[..., :half], x[..., half:ROT_DIM], x[..., ROT_DIM:]
    c = jnp.cos(ang).astype(x.dtype)
    s = jnp.sin(ang).astype(x.dtype)
    return jnp.concatenate([x1 * c - x2 * s, x2 * c + x1 * s, xp], axis=-1)


def masked_softmax(s, mask):
    s = jnp.where(mask, s.astype(jnp.float32), NEG)
    m = jnp.max(s, axis=-1, keepdims=True)
    p = jnp.where(mask, jnp.exp(s - m), 0.0)
    return p / jnp.maximum(jnp.sum(p, axis=-1, keepdims=True), 1e-30)


def swiglu(h, wg, wu, wd):
    return (jax.nn.silu(h @ wg) * (h @ wu)) @ wd


def overlap_matrix(n_cmp, n_sel):
    cs = np.arange(n_cmp)[:, None] * CMP_STRIDE
    ss = np.arange(n_sel)[None, :] * SEL_BLOCK
    ov = np.minimum(cs + CMP_LEN, ss + SEL_BLOCK) - np.maximum(cs, ss)
    return jnp.asarray(np.clip(ov, 0, None).astype(np.float32) / CMP_LEN)


def compress(t, pe, w1, w2):
    B, H, T, dk = t.shape
    ratio = CMP_LEN // CMP_STRIDE
    n_chunks = T // CMP_STRIDE
    n_cmp = n_chunks - ratio + 1
    chunks = t.reshape(B, H, n_chunks, CMP_STRIDE, dk)
    blocks = jnp.concatenate([chunks[:, :, r:r + n_cmp] for r in range(ratio)], axis=3)
    blocks = blocks + pe
    flat = blocks.reshape(B, H, n_cmp, CMP_LEN * dk)
    return jax.nn.gelu(flat @ w1) @ w2


def nsa_mixer(q, k_cmp, v_cmp, k_sel, v_sel, k_win, v_win, gates, pos,
              pe_k, w1_k, w2_k, pe_v, w1_v, w2_v, g_q, g_kc, g_ks, g_kw):
    B, T = q.shape[:2]
    dk = HEAD_DIM
    scale = dk ** -0.5
    ang = rope_angles(pos)
    q = q.reshape(B, T, NSA_KV_HEADS, NSA_GROUP, dk).transpose(0, 2, 3, 1, 4)
    q = apply_rope(rms_norm(q, g_q), ang[:, None, None])

    def heads(t):
        return t.reshape(B, T, NSA_KV_HEADS, dk).transpose(0, 2, 1, 3)

    n_cmp = (T - CMP_LEN) // CMP_STRIDE + 1
    cmp_end = jnp.arange(n_cmp) * CMP_STRIDE + CMP_LEN - 1
    ang_c = rope_angles(pos[:, cmp_end])
    kc = apply_rope(rms_norm(compress(heads(k_cmp), pe_k, w1_k, w2_k), g_kc), ang_c[:, None])
    vc = compress(heads(v_cmp), pe_v, w1_v, w2_v)
    n_sel = T // SEL_BLOCK
    top_n = min(SEL_TOPK, n_sel)
    ks_blocks = apply_rope(rms_norm(heads(k_sel), g_ks), ang[:, None]).reshape(B, NSA_KV_HEADS, n_sel, SEL_BLOCK, dk)
    vs_blocks = heads(v_sel).reshape(B, NSA_KV_HEADS, n_sel, SEL_BLOCK, dk)
    w_ov = overlap_matrix(n_cmp, n_sel)
    pad = ((0, 0), (0, 0), (WINDOW, 0), (0, 0))
    kw_pad = jnp.pad(apply_rope(rms_norm(heads(k_win), g_kw), ang[:, None]), pad)
    vw_pad = jnp.pad(heads(v_win), pad)
    gates = jax.nn.sigmoid(gates.reshape(B, T, NSA_KV_HEADS, NSA_GROUP, 3).transpose(0, 2, 3, 1, 4))

    bi = jnp.arange(B)[:, None, None, None]
    hi = jnp.arange(NSA_KV_HEADS)[None, :, None, None]
    jblk = jnp.arange(n_sel)

    def block_fn(i):
        s0 = i * Q_BLOCK
        tq = s0 + jnp.arange(Q_BLOCK)
        qb = lax.dynamic_slice_in_dim(q, s0, Q_BLOCK, axis=3)
        gb = lax.dynamic_slice_in_dim(gates, s0, Q_BLOCK, axis=3)
        m_c = cmp_end[None, :] <= tq[:, None]
        p_c = masked_softmax(jnp.einsum('bhgqd,bhnd->bhgqn', qb, kc) * scale, m_c)
        o_c = jnp.einsum('bhgqn,bhnd->bhgqd', p_c, vc)
        imp = jnp.einsum('bhgqn,nj->bhqj', p_c, w_ov)
        cur = tq // SEL_BLOCK
        valid = jblk[None, :] * SEL_BLOCK <= tq[:, None]
        forced = (jblk[None, :] == 0) | (jblk[None, :] == cur[:, None]) | (jblk[None, :] == cur[:, None] - 1)
        score = jnp.where(valid & forced, FORCE_SCORE, jnp.where(valid, imp, -1.0))
        _, idx = lax.top_k(score, top_n)
        ksb = ks_blocks[bi, hi, idx].reshape(B, NSA_KV_HEADS, Q_BLOCK, top_n * SEL_BLOCK, dk)
        vsb = vs_blocks[bi, hi, idx].reshape(B, NSA_KV_HEADS, Q_BLOCK, top_n * SEL_BLOCK, dk)
        kpos = (idx[..., None] * SEL_BLOCK + jnp.arange(SEL_BLOCK)).reshape(B, NSA_KV_HEADS, Q_BLOCK, -1)
        m_s = (kpos <= tq[:, None])[:, :, None]
        p_s = masked_softmax(jnp.einsum('bhgqd,bhqkd->bhgqk', qb, ksb) * scale, m_s)
        o_s = jnp.einsum('bhgqk,bhqkd->bhgqd', p_s, vsb)
        kwb = lax.dynamic_slice_in_dim(kw_pad, s0, Q_BLOCK + WINDOW, axis=2)
        vwb = lax.dynamic_slice_in_dim(vw_pad, s0, Q_BLOCK + WINDOW, axis=2)
        kpos_w = s0 - WINDOW + jnp.arange(Q_BLOCK + WINDOW)
        diff = tq[:, None] - kpos_w[None, :]
        m_w = (kpos_w[None, :] >= 0) & (diff >= 0) & (diff < WINDOW)
        p_w = masked_softmax(jnp.einsum('bhgqd,bhkd->bhgqk', qb, kwb) * scale, m_w)
        o_w = jnp.einsum('bhgqk,bhkd->bhgqd', p_w, vwb)
        o = gb[..., 0:1] * o_c + gb[..., 1:2] * o_s + gb[..., 2:3] * o_w
        return o.astype(q.dtype)

    out = lax.map(block_fn, jnp.arange(T // Q_BLOCK))
    return out.transpose(1, 0, 4, 2, 3, 5).reshape(B, T, NSA_HEADS * dk)


def stick_breaking(q, k, v):
    B, T = q.shape[:2]
    dk = HEAD_DIM
    scale = dk ** -0.5

    def heads(t):
        return t.reshape(B, T, SB_HEADS, dk).transpose(0, 2, 1, 3)

    q, k, v = heads(q), heads(k), heads(v)
    kpos = jnp.arange(T)

    def block_fn(i):
        s0 = i * Q_BLOCK
        tq = s0 + jnp.arange(Q_BLOCK)
        qb = lax.dynamic_slice_in_dim(q, s0, Q_BLOCK, axis=2)
        z = jnp.einsum('bhqd,bhkd->bhqk', qb, k).astype(jnp.float32) * scale
        mask = kpos[None, :] < tq[:, None]
        log_rem = jnp.where(mask, jax.nn.log_sigmoid(-z), 0.0)
        suffix = lax.cumsum(log_rem, axis=3, reverse=True) - log_rem
        a = jnp.where(mask, jnp.exp(jax.nn.log_sigmoid(z) + suffix), 0.0)
        return jnp.einsum('bhqk,bhkd->bhqd', a, v).astype(q.dtype)

    out = lax.map(block_fn, jnp.arange(T // Q_BLOCK))
    return out.transpose(1, 0, 3, 2, 4).reshape(B, T, SB_HEADS * dk)


def memory_cross_attention(h, mem_n, wq, wk, wv, wo, g_q, g_k):
    B, T = h.shape[:2]
    M = mem_n.shape[1]
    q = rms_norm((h @ wq).reshape(B, T, MEM_HEADS, HEAD_DIM), g_q)
    k = rms_norm((mem_n @ wk).reshape(B, M, MEM_HEADS, HEAD_DIM), g_k)
    v = (mem_n @ wv).reshape(B, M, MEM_HEADS, HEAD_DIM)
    s = jnp.einsum('bqhd,bkhd->bhqk', q, k).astype(jnp.float32) * HEAD_DIM ** -0.5
    p = jax.nn.softmax(s, axis=-1)
    o = jnp.einsum('bhqk,bkhd->bqhd', p, v).astype(h.dtype).reshape(B, T, MEM_HEADS * HEAD_DIM)
    return o @ wo


def setup_inputs(seed: int = 0) -> dict:
    key = jax.random.key(seed)
    ks = iter(jax.random.split(key, 40))

    def dense(shape, fan_in):
        return jax.random.normal(next(ks), (DEPTH,) + shape, jnp.float32) * fan_in ** -0.5

    def gain(n):
        return 1.0 + 0.05 * jax.random.normal(next(ks), (DEPTH, n), jnp.float32)

    x = jax.random.normal(next(ks), (BATCH, SEQ, D_MODEL), jnp.float32)
    mem = jax.random.normal(next(ks), (BATCH, MEM_LEN, D_MODEL), jnp.float32)
    start = jax.random.randint(next(ks), (BATCH, 1), 0, 4096, dtype=jnp.int32)
    positions = (start + jnp.arange(SEQ, dtype=jnp.int32)[None, :]).astype(jnp.int32)
    return {
        "x": x, "mem": mem, "positions": positions,
        "ffn1_norm": gain(D_MODEL),
        "ffn1_wg": dense((D_MODEL, D_FF), D_MODEL),
        "ffn1_wu": dense((D_MODEL, D_FF), D_MODEL),
        "ffn1_wd": dense((D_FF, D_MODEL), D_FF),
        "mix_norm": gain(D_MODEL),
        "w_in": dense((D_MODEL, IN_COLS), D_MODEL),
        "nsa_q_norm": gain(HEAD_DIM),
        "nsa_kc_norm": gain(HEAD_DIM),
        "nsa_ks_norm": gain(HEAD_DIM),
        "nsa_kw_norm": gain(HEAD_DIM),
        "cmp_pos_k": 0.1 * jax.random.normal(next(ks), (DEPTH, CMP_LEN, HEAD_DIM), jnp.float32),
        "cmp_w1_k": dense((CMP_LEN * HEAD_DIM, CMP_HIDDEN), CMP_LEN * HEAD_DIM),
        "cmp_w2_k": dense((CMP_HIDDEN, HEAD_DIM), CMP_HIDDEN),
        "cmp_pos_v": 0.1 * jax.random.normal(next(ks), (DEPTH, CMP_LEN, HEAD_DIM), jnp.float32),
        "cmp_w1_v": dense((CMP_LEN * HEAD_DIM, CMP_HIDDEN), CMP_LEN * HEAD_DIM),
        "cmp_w2_v": dense((CMP_HIDDEN, HEAD_DIM), CMP_HIDDEN),
        "w_out": dense((MIX_WIDTH, D_MODEL), MIX_WIDTH),
        "mem_x_norm": gain(D_MODEL),
        "mem_kv_norm": gain(D_MODEL),
        "mem_wq": dense((D_MODEL, MEM_HEADS * HEAD_DIM), D_MODEL),
        "mem_wk": dense((D_MODEL, MEM_HEADS * HEAD_DIM), D_MODEL),
        "mem_wv": dense((D_MODEL, MEM_HEADS * HEAD_DIM), D_MODEL),
        "mem_q_norm": gain(HEAD_DIM),
        "mem_k_norm": gain(HEAD_DIM),
        "mem_wo": dense((MEM_HEADS * HEAD_DIM, D_MODEL), MEM_HEADS * HEAD_DIM),
        "ffn2_norm": gain(D_MODEL),
        "ffn2_wg": dense((D_MODEL, D_FF), D_MODEL),
        "ffn2_wu": dense((D_MODEL, D_FF), D_MODEL),
        "ffn2_wd": dense((D_FF, D_MODEL), D_FF),
    }


def reference(x, mem, positions, ffn1_norm, ffn1_wg, ffn1_wu, ffn1_wd, mix_norm, w_in,
              nsa_q_norm, nsa_kc_norm, nsa_ks_norm, nsa_kw_norm,
              cmp_pos_k, cmp_w1_k, cmp_w2_k, cmp_pos_v, cmp_w1_v, cmp_w2_v, w_out,
              mem_x_norm, mem_kv_norm, mem_wq, mem_wk, mem_wv, mem_q_norm, mem_k_norm, mem_wo,
              ffn2_norm, ffn2_wg, ffn2_wu, ffn2_wd):
    split_at = [int(c) for c in np.cumsum(IN_SPLITS)[:-1]]
    for l in range(DEPTH):
        x = x + 0.5 * swiglu(rms_norm(x, ffn1_norm[l]), ffn1_wg[l], ffn1_wu[l], ffn1_wd[l])
        h = rms_norm(x, mix_norm[l])
        (q_n, kc, vc, ksl, vsl, kwn, vwn, gts, q_s, k_s, v_s) = jnp.split(h @ w_in[l], split_at, axis=-1)
        o_nsa = nsa_mixer(q_n, kc, vc, ksl, vsl, kwn, vwn, gts, positions,
                          cmp_pos_k[l], cmp_w1_k[l], cmp_w2_k[l], cmp_pos_v[l], cmp_w1_v[l], cmp_w2_v[l],
                          nsa_q_norm[l], nsa_kc_norm[l], nsa_ks_norm[l], nsa_kw_norm[l])
        o_sb = stick_breaking(q_s, k_s, v_s)
        x = x + jnp.concatenate([o_nsa, o_sb], axis=-1) @ w_out[l]
        x = x + memory_cross_attention(rms_norm(x, mem_x_norm[l]), rms_norm(mem, mem_kv_norm[l]),
                                       mem_wq[l], mem_wk[l], mem_wv[l], mem_wo[l],
                                       mem_q_norm[l], mem_k_norm[l])
        x = x + 0.5 * swiglu(rms_norm(x, ffn2_norm[l]), ffn2_wg[l], ffn2_wu[l], ffn2_wd[l])
    return x
```

```python
import os
import numpy as np
import ml_dtypes
from contextlib import ExitStack
import concourse.bass as bass
import concourse.mybir as mybir
from concourse.bass_utils import run_bass_kernel_spmd

F32 = mybir.dt.float32
BF = mybir.dt.bfloat16
I32 = mybir.dt.int32
AF = mybir.ActivationFunctionType
ALU = mybir.AluOpType
AX = mybir.AxisListType
NPBF = ml_dtypes.bfloat16

T = 8192
D = 1024
FF = 2816
NFC = 22
NT = 64
NOWN = 32
EPS = 1e-6
BIG = 30000.0
TWO_PI = 6.283185307179586
PI = 3.141592653589793
C_QN, C_KC, C_VC, C_KS, C_VS, C_KW, C_VW, C_GT, C_QS, C_KSB, C_VSB = 0, 512, 640, 768, 896, 1024, 1152, 1280, 1304, 1816, 2328

COMPUTE = ("tensor", "scalar", "vector", "gpsimd")


class Op:
    __slots__ = ("eng", "fn", "deps", "is_dma", "semkey", "token", "has_dep", "idx")

    def __init__(self, eng, fn, is_dma=False, semkey=None):
        self.eng = eng
        self.fn = fn
        self.deps = []
        self.is_dma = is_dma
        self.semkey = semkey
        self.token = None
        self.has_dep = False
        self.idx = -1


class Prog:
    def __init__(self, nc, stack):
        self.nc = nc
        self.ops = []
        self.emitted = 0
        self.state = {}
        self.eng_sem = {}
        self.eng_cnt = {}
        for e in COMPUTE:
            self.eng_sem[e] = stack.enter_context(nc.semaphore("sem_" + e))
            self.eng_cnt[e] = 0
        self.dma_sem = {}
        self.dma_cnt = {}
        self.stack = stack
        self.waited = {e: {} for e in ("sync",) + COMPUTE}
        self.last_op = {}
        self.pending_dma = []
        self.last_dma = {}
        self.n_instr = 0

    def add(self, eng, fn, r=(), w=(), dma=False, semkey=None):
        op = Op(eng, fn, is_dma=dma, semkey=semkey)
        op.idx = len(self.ops)
        deps = {}
        for k in r:
            st = self.state.get(k)
            if st is not None:
                for d in st[0]:
                    deps[d.idx] = (d, "raw")
                if isinstance(k, tuple) and k[0] == "ps":
                    for d in st[1]:
                        if d.eng != eng and d.idx not in deps:
                            deps[d.idx] = (d, "rar")
        for k in w:
            st = self.state.get(k)
            if st is not None:
                for d in st[0]:
                    if d.idx not in deps:
                        deps[d.idx] = (d, "waw")
                for d in st[1]:
                    if d.idx not in deps:
                        deps[d.idx] = (d, "war")
        for d, kind in deps.values():
            if (not d.is_dma) and (not dma) and d.eng == eng:
                if eng == "tensor":
                    continue
            op.deps.append(d)
            d.has_dep = True
        for k in r:
            st = self.state.setdefault(k, [[], []])
            st[1].append(op)
        for k in w:
            self.state[k] = [[op], []]
        self.ops.append(op)
        if dma:
            prev = self.last_dma.get(semkey)
            if prev is not None and prev not in op.deps:
                op.deps.append(prev)
            self.last_dma[semkey] = op
            op.has_dep = True
            self.pending_dma.append(op)
        else:
            self.last_op[eng] = op
        return op

    def dma(self, eng, out, in_, r=(), w=(), semkey=None):
        assert semkey is not None
        return self.add(eng, lambda e: e.dma_start(out=out, in_=in_), r=r, w=w, dma=True, semkey=semkey)

    def barrier(self):
        lasts = list(self.last_op.values())
        dmas = list(self.pending_dma)
        self.pending_dma = []
        for e in ("sync",) + COMPUTE:
            op = Op(e, None)
            op.idx = len(self.ops)
            for d in lasts + dmas:
                if d.eng == e and not d.is_dma:
                    continue
                op.deps.append(d)
                d.has_dep = True
            self.ops.append(op)
        self.state = {}

    def flush(self):
        nc = self.nc
        for op in self.ops[self.emitted:]:
            e = getattr(nc, op.eng)
            wt = self.waited[op.eng]
            for d in op.deps:
                sem, val = d.token
                key = id(sem)
                if wt.get(key, 0) >= val:
                    continue
                e.wait_ge(sem, val)
                self.n_instr += 1
                wt[key] = val
            if op.fn is None:
                continue
            ins = op.fn(e)
            self.n_instr += 1
            if op.is_dma:
                sk = op.semkey
                if sk not in self.dma_sem:
                    self.dma_sem[sk] = self.stack.enter_context(nc.semaphore("dsem_%d" % len(self.dma_sem)))
                    self.dma_cnt[sk] = 0
                ins.then_inc(self.dma_sem[sk], 16)
                self.dma_cnt[sk] += 16
                op.token = (self.dma_sem[sk], self.dma_cnt[sk])
            elif op.has_dep:
                self.eng_cnt[op.eng] += 1
                ins.then_inc(self.eng_sem[op.eng], 1)
                op.token = (self.eng_sem[op.eng], self.eng_cnt[op.eng])
        self.emitted = len(self.ops)


INPUT_SPECS = [
    ("x", [T, D], F32), ("mem", [256, D], F32), ("pos_cat", [128, 100], I32), ("freqs", [128, 8], F32),
    ("wg1", [D, FF], F32), ("wu1", [D, FF], F32), ("wd1", [FF, D], F32),
    ("wg2", [D, FF], F32), ("wu2", [D, FF], F32), ("wd2", [FF, D], F32),
    ("w_in", [D, 2840], F32), ("w_out", [D, D], F32),
    ("mwq", [D, 256], F32), ("mwk", [D, 256], F32), ("mwv", [D, 256], F32), ("mwo", [256, D], F32),
    ("cw1k", [2048, 256], F32), ("cw1v", [2048, 256], F32), ("cw2k", [256, 64], F32), ("cw2v", [256, 64], F32),
    ("pekT", [64, 32], F32), ("pevT", [64, 32], F32),
    ("g_ffn1", [128, D], F32), ("g_mix", [128, D], F32), ("g_memx", [128, D], F32), ("g_memkv", [128, D], F32),
    ("g_ffn2", [128, D], F32),
    ("g_q", [128, 64], F32), ("g_kc", [128, 64], F32), ("g_ksw", [128, 128], F32),
    ("g_mq", [128, 64], F32), ("g_mk", [128, 64], F32),
    ("sab", [128, 2], F32), ("maskc", [128, 8 * 128], BF), ("A2", [NOWN, 128, 128], F32), ("A13", [NOWN, 128, 128], F32),
    ("cmsel", [128, 2 * 128], BF), ("wmask", [128, 6 * 128], BF), ("sbmask", [128, 8 * 512], BF),
    ("ident", [128, 128], BF), ("uneg", [128, 128], BF), ("negones", [128, 128], BF),
    ("expd2", [64, T], BF), ("wov", [128, 4 * 128], BF),
]


def build_nc(debug=False, stop_after=99, quick=False):
    nc = bass.Bass("TRN2", target_bir_lowering=False)
    IN = {}
    for name, shape, dt in INPUT_SPECS:
        IN[name] = nc.dram_tensor(name, shape, dt, kind="ExternalInput").ap()
    out_d = nc.dram_tensor("out", [NOWN * 128, D], F32, kind="ExternalOutput").ap()
    DBG = {}

    def dbg_out(name, shape, dt=F32):
        DBG[name] = nc.dram_tensor("dbg_" + name, shape, dt, kind="ExternalOutput").ap()
        return DBG[name]

    x1_all = nc.dram_tensor("x1_all", [T, D], F32).ap()
    x1o_d = nc.dram_tensor("x1o_d", [NOWN * 128, D], F32).ap()
    x3o_d = nc.dram_tensor("x3o_d", [NOWN * 128, D], F32).ap()
    ksT_d = nc.dram_tensor("ksT_d", [128, T], BF).ap()
    kwT_d = nc.dram_tensor("kwT_d", [128, T], BF).ap()
    vs_d = nc.dram_tensor("vs_d", [T, 130], BF).ap()
    vw_d = nc.dram_tensor("vw_d", [T, 130], BF).ap()
    ksbT_d = nc.dram_tensor("ksbT_d", [128, 4, T], BF).ap()
    vsb_d = nc.dram_tensor("vsb_d", [T, 512], BF).ap()
    qsbT_d = nc.dram_tensor("qsbT_d", [128, 4, NOWN * 128], BF).ap()
    qnT_d = nc.dram_tensor("qnT_d", [128, NOWN, 512], BF).ap()
    oT_d = nc.dram_tensor("oT_d", [128, 8, NOWN * 128], BF).ap()

    with ExitStack() as top:
        P = Prog(nc, top)

        _cnt = [0]

        def sbt(st, name, shape, dt):
            _cnt[0] += 1
            return st.enter_context(nc.sbuf_tensor("s%d_%s" % (_cnt[0], name), shape, dt))

        PSALL = top.enter_context(nc.psum_tensor("psall", [128, 8, 512], F32))
        PS = [PSALL[:, i, :] for i in range(8)]

        def psk(b):
            return ("ps", b)

        V = lambda fn, r=(), w=(): P.add("vector", fn, r, w)
        A = lambda fn, r=(), w=(): P.add("scalar", fn, r, w)
        G = lambda fn, r=(), w=(): P.add("gpsimd", fn, r, w)
        TE = lambda fn, r=(), w=(): P.add("tensor", fn, r, w)

        ident = sbt(top, "ident", [128, 128], BF)
        gates = sbt(top, "gates", [128, NOWN, 24], F32)
        cosT = sbt(top, "cosT", [128, 100, 8], F32)
        sinT = sbt(top, "sinT", [128, 100, 8], F32)
        KCT = sbt(top, "KCT", [128, 512], BF)
        WVC = sbt(top, "WVC", [128, 4, 2, 200], BF)
        sab = sbt(top, "sab", [128, 2], F32)
        sabb = sbt(top, "sabb", [128, 2], BF)
        epsb = sbt(top, "epsb", [128, 1], F32)

        P.dma("sync", ident[:], IN["ident"], w=["ident"], semkey="c0")
        P.dma("sync", sab[:], IN["sab"], w=["sab"], semkey="c1")
        V(lambda e: e.tensor_copy(sabb[:], sab[:]), r=["sab"], w=["sabb"])
        V(lambda e: e.memset(epsb[:], 0.0), w=["epsb"])

        def evac_copy(i, out, in_, r, w):
            if i % 2 == 0:
                return A(lambda e: e.copy(out, in_), r=r, w=w)
            return V(lambda e: e.tensor_copy(out, in_), r=r, w=w)

        with ExitStack() as ph:
            posi = sbt(ph, "posi", [128, 100], I32)
            posf = sbt(ph, "posf", [128, 100], F32)
            frq = sbt(ph, "frq", [128, 8], F32)
            ang = sbt(ph, "ang", [128, 100, 8], F32)
            tmpa = sbt(ph, "tmpa", [128, 100, 8], F32)
            ki = sbt(ph, "ki", [128, 100, 8], I32)
            kf = sbt(ph, "kf", [128, 100, 8], F32)
            P.dma("sync", posi[:], IN["pos_cat"], w=["posi"], semkey="c0")
            P.dma("sync", frq[:], IN["freqs"], w=["frq"], semkey="c1")
            V(lambda e: e.tensor_copy(posf[:], posi[:]), r=["posi"], w=["posf"])
            V(lambda e: e.tensor_tensor(ang[:], posf[:].unsqueeze(2).to_broadcast([128, 100, 8]),
                                        frq[:].unsqueeze(1).to_broadcast([128, 100, 8]), op=ALU.mult),
              r=["posf", "frq"], w=["ang"])

            def sin_of(dst, shift):
                V(lambda e: e.tensor_scalar(tmpa[:], ang[:], shift, None, op0=ALU.add), r=["ang"], w=["tmpa"])
                V(lambda e: e.tensor_scalar(kf[:], tmpa[:], 1.0 / TWO_PI, None, op0=ALU.mult), r=["tmpa"], w=["kf"])
                V(lambda e: e.tensor_copy(ki[:], kf[:]), r=["kf"], w=["ki"])
                V(lambda e: e.tensor_copy(kf[:], ki[:]), r=["ki"], w=["kf"])
                V(lambda e: e.scalar_tensor_tensor(tmpa[:], kf[:], -TWO_PI, tmpa[:], op0=ALU.mult, op1=ALU.add),
                  r=["kf", "tmpa"], w=["tmpa"])
                V(lambda e: e.tensor_scalar(kf[:], tmpa[:], 0.0, TWO_PI, op0=ALU.is_lt, op1=ALU.mult), r=["tmpa"], w=["kf"])
                V(lambda e: e.tensor_tensor(tmpa[:], tmpa[:], kf[:], op=ALU.add), r=["tmpa", "kf"], w=["tmpa"])
                V(lambda e: e.tensor_scalar(tmpa[:], tmpa[:], TWO_PI, 0.0, op0=ALU.min, op1=ALU.max), r=["tmpa"], w=["tmpa"])
                A(lambda e: e.activation(dst, tmpa[:], AF.Sin, scale=-1.0, bias=PI), r=["tmpa"], w=["trig"])

            sin_of(sinT[:], 0.0)
            sin_of(cosT[:], PI / 2)
            P.barrier()
            P.flush()

        def cs_all(j):
            return cosT[:, j, :], sinT[:, j, :]

        def cs_own(i):
            return cosT[:, 64 + i, :], sinT[:, 64 + i, :]

        def cs_cmp(c):
            return cosT[:, 96 + c, :], sinT[:, 96 + c, :]

        def rms_stats(st_tiles, src_list, skeys, tag):
            junk, ss, rstd = st_tiles
            n = len(src_list)
            V(lambda e: e.memset(ss[:, 0:n], 0.0), w=[tag + "ss"])
            for j, (src, sk) in enumerate(zip(src_list, skeys)):
                A(lambda e, j=j, src=src: e.activation(junk[:], src, AF.Square, accum_out=ss[:, j:j + 1]),
                  r=[sk, tag + "ss"], w=[tag + "junk", tag + "ss"])
            V(lambda e: e.tensor_scalar(ss[:, 0:n], ss[:, 0:n], 1.0 / D, EPS, op0=ALU.mult, op1=ALU.add),
              r=[tag + "ss"], w=[tag + "ss"])
            A(lambda e: e.activation(rstd[:, 0:n], ss[:, 0:n], AF.Ln), r=[tag + "ss"], w=[tag + "rstd"])
            A(lambda e: e.activation(rstd[:, 0:n], rstd[:, 0:n], AF.Exp, scale=-0.5), r=[tag + "rstd"], w=[tag + "rstd"])

        def head_norm_rope(tm, src, H, gain, cs, dst, rk, wk, tag):
            sq, ss, y, t1, t2 = tm
            A(lambda e: e.activation(sq[:, 0:H, :], src, AF.Square), r=rk, w=[tag + "sq"])
            V(lambda e: e.tensor_reduce(ss[:, 0:H], sq[:, 0:H, :], axis=AX.X, op=ALU.add), r=[tag + "sq"], w=[tag + "ss"])
            V(lambda e: e.tensor_scalar(ss[:, 0:H], ss[:, 0:H], 1.0 / 64, EPS, op0=ALU.mult, op1=ALU.add),
              r=[tag + "ss"], w=[tag + "ss"])
            A(lambda e: e.activation(ss[:, 0:H], ss[:, 0:H], AF.Ln), r=[tag + "ss"], w=[tag + "ss"])
            A(lambda e: e.activation(ss[:, 0:H], ss[:, 0:H], AF.Exp, scale=-0.5), r=[tag + "ss"], w=[tag + "ss"])
            V(lambda e: e.tensor_tensor(y[:, 0:H, :], src, ss[:, 0:H].unsqueeze(2).to_broadcast([128, H, 64]), op=ALU.mult),
              r=list(rk) + [tag + "ss"], w=[tag + "y"])
            V(lambda e: e.tensor_tensor(y[:, 0:H, :], y[:, 0:H, :], gain, op=ALU.mult), r=[tag + "y", "gains"], w=[tag + "y"])
            if cs is None:
                V(lambda e: e.tensor_copy(dst, y[:, 0:H, :]), r=[tag + "y"], w=wk)
                return
            c, s = cs
            cb = c.unsqueeze(1).to_broadcast([128, H, 8])
            sb_ = s.unsqueeze(1).to_broadcast([128, H, 8])
            V(lambda e: e.tensor_tensor(t1[:, 0:H, :], y[:, 0:H, 0:8], cb, op=ALU.mult), r=[tag + "y", "trig"], w=[tag + "t1"])
            V(lambda e: e.tensor_tensor(t2[:, 0:H, :], y[:, 0:H, 8:16], sb_, op=ALU.mult), r=[tag + "y", "trig"], w=[tag + "t2"])
            V(lambda e: e.tensor_tensor(dst[:, :, 0:8], t1[:, 0:H, :], t2[:, 0:H, :], op=ALU.subtract),
              r=[tag + "t1", tag + "t2"], w=wk)
            V(lambda e: e.tensor_tensor(t1[:, 0:H, :], y[:, 0:H, 8:16], cb, op=ALU.mult), r=[tag + "y", "trig"], w=[tag + "t1"])
            V(lambda e: e.tensor_tensor(t2[:, 0:H, :], y[:, 0:H, 0:8], sb_, op=ALU.mult), r=[tag + "y", "trig"], w=[tag + "t2"])
            V(lambda e: e.tensor_tensor(dst[:, :, 8:16], t1[:, 0:H, :], t2[:, 0:H, :], op=ALU.add),
              r=[tag + "t1", tag + "t2"], w=wk)
            V(lambda e: e.tensor_copy(dst[:, :, 16:64], y[:, 0:H, 16:64]), r=[tag + "y"], w=wk)

        class _Dup2:
            def __init__(self, tab, base):
                self.tab, self.base = tab, base

            def __getitem__(self, key):
                _, a, _ = key
                return self.tab[:, self.base + a // 2, :]

        def norm_rope_b(tmb, src, A_, B_, gain, cos3, sin3, dst, rk, wk, tag):
            sq, ss, y, t1, t2 = tmb[0:5]
            H = A_ * B_
            fl = lambda ap: ap
            A(lambda e: e.activation(sq[:, 0:H, :], fl(src), AF.Square), r=rk, w=[tag + "sq"])
            V(lambda e: e.tensor_reduce(ss[:, 0:H], sq[:, 0:H, :], axis=AX.X, op=ALU.add), r=[tag + "sq"], w=[tag + "ss"])
            V(lambda e: e.tensor_scalar(ss[:, 0:H], ss[:, 0:H], 1.0 / 64, EPS, op0=ALU.mult, op1=ALU.add), r=[tag + "ss"], w=[tag + "ss"])
            A(lambda e: e.activation(ss[:, 0:H], ss[:, 0:H], AF.Ln), r=[tag + "ss"], w=[tag + "ss"])
            A(lambda e: e.activation(ss[:, 0:H], ss[:, 0:H], AF.Exp, scale=-0.5), r=[tag + "ss"], w=[tag + "ss"])
            V(lambda e: e.tensor_tensor(y[:, 0:H, :], fl(src), ss[:, 0:H].unsqueeze(2).to_broadcast([128, H, 64]), op=ALU.mult),
              r=list(rk) + [tag + "ss"], w=[tag + "y"])
            V(lambda e: e.tensor_tensor(y[:, 0:H, :], y[:, 0:H, :], gain, op=ALU.mult), r=[tag + "y", "gains"], w=[tag + "y"])
            t3, t4 = tmb[5], tmb[6]
            sl = lambda tt, a: tt[:, a * B_:(a + 1) * B_, :]
            cbs = [cos3[:, a, :].unsqueeze(1).to_broadcast([128, B_, 8]) for a in range(A_)]
            sbs = [sin3[:, a, :].unsqueeze(1).to_broadcast([128, B_, 8]) for a in range(A_)]
            for a in range(A_):
                V(lambda e, a=a: e.tensor_tensor(sl(t1, a), sl(y, a)[:, :, 0:8], cbs[a], op=ALU.mult), r=[tag + "y", "trig"], w=[(tag + "t1", a)])
            for a in range(A_):
                V(lambda e, a=a: e.tensor_tensor(sl(t2, a), sl(y, a)[:, :, 8:16], sbs[a], op=ALU.mult), r=[tag + "y", "trig"], w=[(tag + "t2", a)])
            for a in range(A_):
                V(lambda e, a=a: e.tensor_tensor(sl(t3, a), sl(y, a)[:, :, 8:16], cbs[a], op=ALU.mult), r=[tag + "y", "trig"], w=[(tag + "t3", a)])
            for a in range(A_):
                V(lambda e, a=a: e.tensor_tensor(sl(t4, a), sl(y, a)[:, :, 0:8], sbs[a], op=ALU.mult), r=[tag + "y", "trig"], w=[(tag + "t4", a)])
            for a in range(A_):
                V(lambda e, a=a, da=dst(a): e.tensor_tensor(da[:, :, 0:8], sl(t1, a), sl(t2, a), op=ALU.subtract), r=[(tag + "t1", a), (tag + "t2", a)], w=wk)
            for a in range(A_):
                V(lambda e, a=a, da=dst(a): e.tensor_tensor(da[:, :, 8:16], sl(t3, a), sl(t4, a), op=ALU.add), r=[(tag + "t3", a), (tag + "t4", a)], w=wk)
            for a in range(A_):
                V(lambda e, a=a, da=dst(a): e.tensor_copy(da[:, :, 16:64], sl(y, a)[:, :, 16:64]), r=[tag + "y"], w=wk)

        def load_w_bf(dst3, src2, nk, key):
            h = (nk + 1) // 2
            v = src2.rearrange("(k p) n -> p k n", p=128)
            P.dma("gpsimd", dst3[:, 0:h, :], v[:, 0:h, :], w=[key + "A"], semkey=key + "A")
            if nk > h:
                P.dma("gpsimd", dst3[:, h:nk, :], v[:, h:nk, :], w=[key + "B"], semkey=key + "B")

        def wkey(key, k, nk):
            return key + ("A" if k < (nk + 1) // 2 else "B")

        def ffn_phase(tag, src_d, dst_d, nblk, wg_d, wu_d, wd_d, g_d, wg_pre=None):
            with ExitStack() as ph:
                wg = wg_pre if wg_pre is not None else sbt(ph, "wg", [128, 8, FF], BF)
                wu = sbt(ph, "wu", [128, 8, FF], BF)
                wd = sbt(ph, "wd", [128, NFC, D], BF)
                gbc = sbt(ph, "gbc", [128, D], F32)
                xt = sbt(ph, "xt", [128, 4, D], F32)
                hb = [sbt(ph, "hb%d" % j, [128, D], BF) for j in range(2)]
                hT = sbt(ph, "hT", [128, 8, 512], BF)
                aT = sbt(ph, "aT", [128, NFC, 512], BF)
                sg = [sbt(ph, "sg%d" % j, [128, 512], F32) for j in range(2)]
                junk = sbt(ph, "junk", [128, D], BF)
                ss = sbt(ph, "ss", [128, 4], F32)
                rstd = sbt(ph, "rstd", [128, 4], F32)
                if wg_pre is None:
                    load_w_bf(wg, wg_d, 8, "wg")
                load_w_bf(wu, wu_d, 8, "wu")
                load_w_bf(wd, wd_d, NFC, "wd")
                P.dma("sync", gbc[:], g_d, w=["gbc"], semkey="c0")
                for n in range(nblk):
                    if n == 0:
                        for j in range(4):
                            P.dma("sync", xt[:, j, :], src_d[(n * 4 + j) * 128:(n * 4 + j + 1) * 128, :], w=[("xt", j)], semkey=("xt", j))
                    rms_stats((junk, ss, rstd), [xt[:, j, :] for j in range(4)], [("xt", j) for j in range(4)], "f")
                    for j in range(4):
                        hbj = hb[j % 2]
                        V(lambda e, j=j, hbj=hbj: e.scalar_tensor_tensor(hbj[:], xt[:, j, :], rstd[:, j:j + 1], gbc[:], op0=ALU.mult, op1=ALU.mult),
                          r=[("xt", j), "frstd", "gbc"], w=[("hb", j % 2)])
                        for half in range(2):
                            bank = 4 + half
                            for kk in range(4):
                                kc = half * 4 + kk
                                TE(lambda e, hbj=hbj, kc=kc, kk=kk, bank=bank: e.matmul(PS[bank][:, kk * 128:(kk + 1) * 128], hbj[:, kc * 128:(kc + 1) * 128], ident[:], start=True, stop=True),
                                   r=[("hb", j % 2), "ident"], w=[psk(bank)])
                            evac_copy(j + half, hT[:, half * 4:half * 4 + 4, j * 128:(j + 1) * 128],
                                      PS[bank][:].rearrange("p (k t) -> p k t", k=4),
                                      r=[psk(bank)], w=[("hT", half, j)])
                    hT_keys = [("hT", h_, j_) for h_ in range(2) for j_ in range(4)]
                    for fc in range(NFC):
                        bg, bu = fc % 2, 2 + fc % 2
                        for kc in range(8):
                            TE(lambda e, fc=fc, kc=kc, bg=bg: e.matmul(PS[bg][:], wg[:, kc, fc * 128:(fc + 1) * 128], hT[:, kc, :], start=(kc == 0), stop=(kc == 7)),
                               r=hT_keys + [wkey("wg", kc, 8)], w=[psk(bg)])
                        for kc in range(8):
                            TE(lambda e, fc=fc, kc=kc, bu=bu: e.matmul(PS[bu][:], wu[:, kc, fc * 128:(fc + 1) * 128], hT[:, kc, :], start=(kc == 0), stop=(kc == 7)),
                               r=hT_keys + [wkey("wu", kc, 8)], w=[psk(bu)])
                        sgf = sg[fc % 2]
                        A(lambda e, sgf=sgf, bg=bg: e.activation(sgf[:], PS[bg][:], AF.Silu), r=[psk(bg)], w=[("sg", fc % 2)])
                        V(lambda e, sgf=sgf, bu=bu, fc=fc: e.tensor_tensor(aT[:, fc, :], sgf[:], PS[bu][:], op=ALU.mult),
                          r=[("sg", fc % 2), psk(bu)], w=[("aT", fc)])
                    for j in range(4):
                        for half in range(2):
                            bank = 6 + half
                            for fc in range(NFC):
                                TE(lambda e, fc=fc, j=j, half=half, bank=bank: e.matmul(PS[bank][:], aT[:, fc, j * 128:(j + 1) * 128], wd[:, fc, half * 512:(half + 1) * 512], start=(fc == 0), stop=(fc == NFC - 1)),
                                   r=[("aT", fc), wkey("wd", fc, NFC)], w=[psk(bank)])
                            V(lambda e, j=j, half=half, bank=bank: e.scalar_tensor_tensor(xt[:, j, half * 512:(half + 1) * 512], PS[bank][:], 0.5, xt[:, j, half * 512:(half + 1) * 512], op0=ALU.mult, op1=ALU.add),
                              r=[psk(bank), ("xt", j)], w=[("xt", j)])
                        P.dma("sync", dst_d[(n * 4 + j) * 128:(n * 4 + j + 1) * 128, :], xt[:, j, :], r=[("xt", j)], w=[("dst", n, j)], semkey=("st", j))
                        if n + 1 < nblk:
                            P.dma("sync", xt[:, j, :], src_d[((n + 1) * 4 + j) * 128:((n + 1) * 4 + j + 1) * 128, :], w=[("xt", j)], semkey=("xt", j))
                P.barrier()
                P.flush()

        _dn = [0]

        def dump(name, src, shape, dt=F32, r=()):
            if not debug:
                return
            d = dbg_out(name, shape, dt)
            _dn[0] += 1
            P.dma("sync", d, src, r=r, semkey=("dbg", _dn[0] % 4))

        def dump_done():
            if debug:
                P.barrier()
                P.flush()

        if not quick:
            ffn_phase("f1", IN["x"], x1_all, 16, IN["wg1"], IN["wu1"], IN["wd1"], IN["g_ffn1"])
        else:
            P.dma("sync", x1_all[0:2048, :], IN["x"][0:2048, :], semkey="c0")
            P.barrier()
            P.flush()
        dump("x1", x1_all[0:1024, :], [1024, D])
        dump("x1b", x1_all[T - 512:T, :], [512, D])
        dump("cosT", cosT[:].rearrange("p a b -> p (a b)"), [128, 800])
        dump("sinT", sinT[:].rearrange("p a b -> p (a b)"), [128, 800])
        dump_done()
        if stop_after <= 1:
            return nc, DBG, None

        with ExitStack() as phkv:
          kcT = sbt(phkv, "kcT", [128, T + 16], BF)
          vcT = sbt(phkv, "vcT", [128, T + 16], BF)
          gq = sbt(phkv, "gq", [128, 64], F32)
          gkc = sbt(phkv, "gkc", [128, 64], F32)
          gksw = sbt(phkv, "gksw", [128, 2, 64], F32)
          tm = (sbt(phkv, "nsq", [128, 8, 64], F32), sbt(phkv, "nss", [128, 8], F32), sbt(phkv, "ny", [128, 8, 64], F32),
                sbt(phkv, "nt1", [128, 8, 8], F32), sbt(phkv, "nt2", [128, 8, 8], F32))
          with ExitStack() as ph:
            win = sbt(ph, "win", [128, 8, 2840], BF)
            gbc = sbt(ph, "gbc", [128, D], F32)
            xt = sbt(ph, "xt", [128, 4, D], F32)
            hb = [sbt(ph, "hb%d" % j, [128, D], BF) for j in range(4)]
            hbo = [sbt(ph, "hbo%d" % j, [128, D], BF) for j in range(2)]
            hT = sbt(ph, "hT", [128, 8, 512], BF)
            hoT = sbt(ph, "hoT", [128, 8, 256], BF)
            x1o = [sbt(ph, "x1o%d" % j, [128, D], F32) for j in range(2)]
            junk = sbt(ph, "junk", [128, D], BF)
            ss4 = sbt(ph, "ss4", [128, 4], F32)
            rstd = sbt(ph, "rstd", [128, 4], F32)
            kstg = sbt(ph, "kstg", [128, 2, 2, 64], BF)
            kTs = sbt(ph, "kTs", [128, 2, 512], BF)
            vstg = sbt(ph, "vstg", [128, 4, 2, 2, 65], BF)
            ksbs = sbt(ph, "ksbs", [128, 4, 512], BF)
            vsbs = sbt(ph, "vsbs", [128, 4, 512], BF)
            qstg = sbt(ph, "qstg", [128, 2, 4, 2, 64], BF)
            kraw = sbt(ph, "kraw", [128, 4, 4, 64], F32)
            kst16 = sbt(ph, "kst16", [128, 4, 4, 64], BF)
            qraw = sbt(ph, "qraw", [128, 2, 8, 64], F32)
            gk16 = sbt(ph, "gk16", [128, 4, 4, 64], F32)
            gq16 = sbt(ph, "gq16", [128, 16, 64], F32)
            graw = sbt(ph, "graw", [128, 2, 24], F32)
            tmb = (sbt(ph, "bsq", [128, 16, 64], F32), sbt(ph, "bss", [128, 16], F32), sbt(ph, "by", [128, 16, 64], F32),
                   sbt(ph, "bt1", [128, 16, 8], F32), sbt(ph, "bt2", [128, 16, 8], F32),
                   sbt(ph, "bt3", [128, 16, 8], F32), sbt(ph, "bt4", [128, 16, 8], F32))
            qnTs = sbt(ph, "qnTs", [128, 2, 512], BF)
            qsbs = sbt(ph, "qsbs", [128, 4, 256], BF)
            load_w_bf(win, IN["w_in"], 8, "win")
            P.dma("sync", gbc[:], IN["g_mix"], w=["gbc"], semkey="c0")
            P.dma("sync", gq[:], IN["g_q"], w=["gains"], semkey="c1")
            P.dma("sync", gkc[:], IN["g_kc"], w=["gains"], semkey="c2")
            P.dma("sync", gksw[:].rearrange("p a d -> p (a d)"), IN["g_ksw"], w=["gains"], semkey="c3")
            V(lambda e: e.memset(vstg[:], 1.0), w=["vstg"])
            for j_ in range(0 if os.environ.get("NOG16") else 4):
                for gi_ in range(2):
                    V(lambda e, j_=j_, gi_=gi_: e.tensor_copy(gk16[:, j_, 2 * gi_:2 * gi_ + 2, :], gksw[:, gi_, :].unsqueeze(1).to_broadcast([128, 2, 64])), r=["gains"], w=["gk16"])
            V(lambda e: e.tensor_copy(gq16[:], gq[:].unsqueeze(1).to_broadcast([128, 16, 64])), r=["gains"], w=["gq16"])
            V(lambda e: e.memset(kcT[:, T:T + 16], 0.0), w=["kcTpad"])
            V(lambda e: e.memset(vcT[:, T:T + 16], 0.0), w=["vcTpad"])
            if quick:
                V(lambda e: e.memset(kcT[:, 2048:T], 0.0), w=["kcTq"])
                V(lambda e: e.memset(vcT[:, 2048:T], 0.0), w=["vcTq"])
            win_keys = ["winA", "winB"]
            nblk1b = int(quick) if quick else 16
            for n in range(nblk1b):
                if n == 0:
                    for j in range(4):
                        P.dma("sync", xt[:, j, :], x1_all[(n * 4 + j) * 128:(n * 4 + j + 1) * 128, :], w=[("xt", j)], semkey=("xt", j))
                rms_stats((junk, ss4, rstd), [xt[:, j, :] for j in range(4)], [("xt", j) for j in range(4)], "m")
                for j in range(4):
                    V(lambda e, j=j: e.scalar_tensor_tensor(hb[j][:], xt[:, j, :], rstd[:, j:j + 1], gbc[:], op0=ALU.mult, op1=ALU.mult),
                      r=[("xt", j), "mrstd", "gbc"], w=[("hb", j)])
                    for half in range(2):
                        bank = 4 + half
                        for kk in range(4):
                            kc = half * 4 + kk
                            TE(lambda e, j=j, kc=kc, kk=kk, bank=bank: e.matmul(PS[bank][:, kk * 128:(kk + 1) * 128], hb[j][:, kc * 128:(kc + 1) * 128], ident[:], start=True, stop=True),
                               r=[("hb", j), "ident"], w=[psk(bank)])
                        evac_copy(j + half, hT[:, half * 4:half * 4 + 4, j * 128:(j + 1) * 128],
                                  PS[bank][:].rearrange("p (k t) -> p k t", k=4), r=[psk(bank)], w=[("hT", half, j)])
                hT_keys = [("hT", h_, j_) for h_ in range(2) for j_ in range(4)]
                for q in range(2):
                    i_own = n * 2 + q
                    V(lambda e, q=q: e.tensor_scalar(x1o[q][:], xt[:, 2 * q, :], sab[:, 0:1], None, op0=ALU.mult),
                      r=[("xt", 2 * q), "sab"], w=[("x1o", q)])
                    V(lambda e, q=q: e.scalar_tensor_tensor(x1o[q][:], xt[:, 2 * q + 1, :], sab[:, 1:2], x1o[q][:], op0=ALU.mult, op1=ALU.add),
                      r=[("xt", 2 * q + 1), "sab", ("x1o", q)], w=[("x1o", q)])
                    P.dma("sync", x1o_d[i_own * 128:(i_own + 1) * 128, :], x1o[q][:], r=[("x1o", q)], w=[("x1od", i_own)], semkey=("x1o", q))
                    V(lambda e, q=q: e.tensor_scalar(hbo[q][:], hb[2 * q][:], sab[:, 0:1], None, op0=ALU.mult),
                      r=[("hb", 2 * q), "sab"], w=[("hbo", q)])
                    V(lambda e, q=q: e.scalar_tensor_tensor(hbo[q][:], hb[2 * q + 1][:], sab[:, 1:2], hbo[q][:], op0=ALU.mult, op1=ALU.add),
                      r=[("hb", 2 * q + 1), "sab", ("hbo", q)], w=[("hbo", q)])
                    for half in range(2):
                        bank = 6 + half
                        for kk in range(4):
                            kc = half * 4 + kk
                            TE(lambda e, q=q, kc=kc, kk=kk, bank=bank: e.matmul(PS[bank][:, kk * 128:(kk + 1) * 128], hbo[q][:, kc * 128:(kc + 1) * 128], ident[:], start=True, stop=True),
                               r=[("hbo", q), "ident"], w=[psk(bank)])
                        evac_copy(q + half, hoT[:, half * 4:half * 4 + 4, q * 128:(q + 1) * 128],
                                  PS[bank][:].rearrange("p (k t) -> p k t", k=4), r=[psk(bank)], w=[("hoT", half, q)])
                hoT_keys = [("hoT", h_, q_) for h_ in range(2) for q_ in range(2)]
                if n + 1 < nblk1b:
                    for j in range(4):
                        P.dma("sync", xt[:, j, :], x1_all[((n + 1) * 4 + j) * 128:((n + 1) * 4 + j + 1) * 128, :], w=[("xt", j)], semkey=("xt", j))
                tok0 = n * 512
                for which, col, dstT in ((0, C_KC, kcT), (1, C_VC, vcT)):
                    bank = which
                    for kc in range(8):
                        TE(lambda e, kc=kc, col=col, bank=bank: e.matmul(PS[bank][:], win[:, kc, col:col + 128], hT[:, kc, :], start=(kc == 0), stop=(kc == 7)),
                           r=hT_keys + win_keys, w=[psk(bank)])
                    evac_copy(which, dstT[:, tok0:tok0 + 512], PS[bank][:], r=[psk(bank)], w=[("cT", which, n)])
                for pr in range(4):
                    bank = 2 + pr % 2
                    for kc in range(8):
                        TE(lambda e, kc=kc, pr=pr, bank=bank: e.matmul(PS[bank][:], win[:, kc, C_KSB + pr * 128:C_KSB + (pr + 1) * 128], hT[:, kc, :], start=(kc == 0), stop=(kc == 7)),
                           r=hT_keys + win_keys, w=[psk(bank)])
                    evac_copy(pr, ksbs[:, pr, :], PS[bank][:], r=[psk(bank)], w=["ksbs"])
                P.dma("sync", ksbT_d[:, :, tok0:tok0 + 512], ksbs[:], r=["ksbs"], w=[("ksbd", n)], semkey="ksbs")
                for j in range(4):
                    bank = j % 2
                    for kc in range(8):
                        TE(lambda e, kc=kc, j=j, bank=bank: e.matmul(PS[bank][:], hT[:, kc, j * 128:(j + 1) * 128], win[:, kc, C_KS:C_KS + 512], start=(kc == 0), stop=(kc == 7)),
                           r=hT_keys + win_keys, w=[psk(bank)])
                    psv = PS[bank][:].rearrange("p (a h d) -> p a h d", a=4, h=2)
                    A(lambda e, j=j, psv=psv: e.copy(vstg[:, j, 0, :, 0:64], psv[:, 1, :, :]), r=[psk(bank)], w=["vstg"])
                    A(lambda e, j=j, psv=psv: e.copy(vstg[:, j, 1, :, 0:64], psv[:, 3, :, :]), r=[psk(bank)], w=["vstg"])
                    if not os.environ.get("NOKRAW"):
                        V(lambda e, j=j, bank=bank: e.tensor_copy(kraw[:, j, 0:2, :].rearrange("p h d -> p (h d)"), PS[bank][:, 0:128]), r=[psk(bank)], w=[("kraw", j)])
                        V(lambda e, j=j, bank=bank: e.tensor_copy(kraw[:, j, 2:4, :].rearrange("p h d -> p (h d)"), PS[bank][:, 256:384]), r=[psk(bank), ("kraw", j)], w=[("kraw", j)])
                    bank2 = 6 + j % 2
                    for kc in range(8):
                        TE(lambda e, kc=kc, j=j, bank2=bank2: e.matmul(PS[bank2][:], hT[:, kc, j * 128:(j + 1) * 128], win[:, kc, C_VSB:C_VSB + 512], start=(kc == 0), stop=(kc == 7)),
                           r=hT_keys + win_keys, w=[psk(bank2)])
                    evac_copy(j, vsbs[:, j, :], PS[bank2][:], r=[psk(bank2)], w=["vsbs"])
                if os.environ.get("KSKIP"):
                    continue
                norm_rope_b(tmb, kraw[:].rearrange("p a b d -> p (a b) d"), 4, 4, gk16[:].rearrange("p a b d -> p (a b) d"), cosT[:, n * 4:n * 4 + 4, :], sinT[:, n * 4:n * 4 + 4, :],
                            lambda a: kst16[:, a, :, :], [("kraw", j_) for j_ in range(4)], ["kst16"], "b")
                for j in range(4):
                    for gi in range(2):
                        TE(lambda e, gi=gi, j=j: e.matmul(PS[2 + gi][:, j * 128:(j + 1) * 128], kst16[:, j, 2 * gi:2 * gi + 2, :].rearrange("p h d -> p (h d)"), ident[:], start=True, stop=True),
                           r=["kst16", "ident"], w=[psk(2 + gi)])
                for gi in range(2):
                    evac_copy(gi, kTs[:, gi, :], PS[2 + gi][:], r=[psk(2 + gi)], w=["kTs"])
                P.dma("sync", ksT_d[:, tok0:tok0 + 512], kTs[:, 0, :], r=["kTs"], w=[("ksd", n)], semkey="kTs")
                P.dma("sync", kwT_d[:, tok0:tok0 + 512], kTs[:, 1, :], r=["kTs"], w=[("kwd", n)], semkey="kTs2")
                P.dma("sync", vs_d[tok0:tok0 + 512, :].rearrange("(j p) c -> p j c", p=128), vstg[:, :, 0, :, :].rearrange("p j h c -> p j (h c)"),
                      r=["vstg"], w=[("vsd", n)], semkey="vstg")
                P.dma("sync", vw_d[tok0:tok0 + 512, :].rearrange("(j p) c -> p j c", p=128), vstg[:, :, 1, :, :].rearrange("p j h c -> p j (h c)"),
                      r=["vstg"], w=[("vwd", n)], semkey="vstg2")
                P.dma("sync", vsb_d[tok0:tok0 + 512, :].rearrange("(j p) c -> p j c", p=128), vsbs[:], r=["vsbs"], w=[("vsbd", n)], semkey="vsbs")
                if os.environ.get("QSKIP"):
                    continue
                for q in range(2):
                    bank = 4 + q
                    for kc in range(8):
                        TE(lambda e, kc=kc, q=q, bank=bank: e.matmul(PS[bank][:], hoT[:, kc, q * 128:(q + 1) * 128], win[:, kc, C_QN:C_QN + 512], start=(kc == 0), stop=(kc == 7)),
                           r=hoT_keys + win_keys, w=[psk(bank)])
                    A(lambda e, q=q, bank=bank: e.copy(qraw[:, q, :, :].rearrange("p h d -> p (h d)"), PS[bank][:]), r=[psk(bank)], w=[("qraw", q)])
                    for kc in range(8):
                        TE(lambda e, kc=kc, q=q: e.matmul(PS[1][:, q * 32:q * 32 + 24], hoT[:, kc, q * 128:(q + 1) * 128], win[:, kc, C_GT:C_GT + 24], start=(kc == 0), stop=(kc == 7)),
                           r=hoT_keys + win_keys, w=[psk(1)])
                cq = cosT[:, 64 + n * 2:64 + n * 2 + 2, :].unsqueeze(2).to_broadcast([128, 2, 2, 8]).rearrange("p q k d -> p (q k) d") if False else None
                norm_rope_b(tmb, qraw[:].rearrange("p q h d -> p (q h) d"), 4, 4, gq16[:], _Dup2(cosT, 64 + n * 2), _Dup2(sinT, 64 + n * 2),
                            lambda a: qstg[:, a // 2, :, a % 2, :], [("qraw", 0), ("qraw", 1)], ["qstg"], "b")
                for q in range(2):
                    for g in range(4):
                        TE(lambda e, g=g, q=q: e.matmul(PS[0][:, g * 128:(g + 1) * 128], qstg[:, q, g, :, :].rearrange("p k d -> p (k d)"), ident[:], start=True, stop=True),
                           r=["qstg", "ident"], w=[psk(0)])
                    evac_copy(q, qnTs[:, q, :], PS[0][:], r=[psk(0)], w=["qnTs"])
                for q in range(2):
                    A(lambda e, q=q: e.activation(graw[:, q, :], PS[1][:, q * 32:q * 32 + 24], AF.Exp, scale=-1.0), r=[psk(1)], w=["graw"])
                V(lambda e: e.tensor_scalar(graw[:], graw[:], 1.0, None, op0=ALU.add), r=["graw"], w=["graw"])
                V(lambda e, n=n: e.reciprocal(gates[:, n * 2:n * 2 + 2, :], graw[:]), r=["graw"], w=["gates"])
                P.dma("sync", qnT_d[:, n * 2:n * 2 + 2, :], qnTs[:], r=["qnTs"], w=[("qnd", n)], semkey="qnTs")
                for pr in range(4):
                    bank = 2 + pr % 2
                    for kc in range(8):
                        TE(lambda e, kc=kc, pr=pr, bank=bank: e.matmul(PS[bank][:, 0:256], win[:, kc, C_QS + pr * 128:C_QS + (pr + 1) * 128], hoT[:, kc, :], start=(kc == 0), stop=(kc == 7)),
                           r=hoT_keys + win_keys, w=[psk(bank)])
                    V(lambda e, pr=pr, bank=bank: e.tensor_scalar(qsbs[:, pr, :], PS[bank][:, 0:256], 0.125, None, op0=ALU.mult), r=[psk(bank)], w=["qsbs"])
                P.dma("sync", qsbT_d[:, :, n * 256:(n + 1) * 256], qsbs[:], r=["qsbs"], w=[("qsbd", n)], semkey="qsbs")
            P.barrier()
            P.flush()

          if True:
            with ExitStack() as pc:
                w1 = sbt(pc, "w1", [128, 32, 256], BF)
                w2 = sbt(pc, "w2", [128, 2, 64], BF)
                peT = sbt(pc, "peT", [128, 32], F32)
                kpe = sbt(pc, "kpe", [128, 32, 512], BF)
                gel = sbt(pc, "gel", [128, 2, 2, 512], BF)
                gx2 = sbt(pc, "gx2", [128, 512], F32)
                gu = sbt(pc, "gu", [128, 512], F32)
                cstg = sbt(pc, "cstg", [128, 2, 64], BF)
                wovs = sbt(pc, "wovs", [128, 4, 128], BF)
                P.dma("sync", wovs[:].rearrange("p c j -> p (c j)"), IN["wov"], w=["wovs"], semkey="c0")
                V(lambda e: e.memset(WVC[:], 1.0), w=["WVC"])
                for hk in range(2):
                    V(lambda e, hk=hk: e.tensor_copy(WVC[:, :, hk, 0:128], wovs[:]), r=["wovs", "WVC"], w=["WVC"])
                for which, (w1_d, w2_d, pe_d, srcT) in enumerate(((IN["cw1k"], IN["cw2k"], IN["pekT"], kcT), (IN["cw1v"], IN["cw2v"], IN["pevT"], vcT))):
                    w1v = w1_d.rearrange("(r d) n -> d r n", d=64)
                    P.dma("gpsimd", w1[0:64, :, :], w1v, w=["w1a"], semkey="w1a")
                    P.dma("gpsimd", w1[64:128, :, :], w1v, w=["w1b"], semkey="w1b")
                    P.dma("gpsimd", w2[:], w2_d.rearrange("(c p) d -> p c d", p=128), w=["w2"], semkey="w2")
                    P.dma("sync", peT[0:64, :], pe_d, w=["peTa"], semkey="c1")
                    P.dma("sync", peT[64:128, :], pe_d, w=["peTb"], semkey="c2")
                    src_win = bass.AP(srcT, 0, [[T + 16, 128], [1, 32], [16, 512]])
                    V(lambda e, src_win=src_win: e.tensor_tensor(kpe[:], src_win, peT[:].unsqueeze(2).to_broadcast([128, 32, 512]), op=ALU.add),
                      r=["peTa", "peTb"], w=["kpe"])
                    for hk in range(2):
                        lo, hi = hk * 64, hk * 64 + 64
                        for hc in range(2):
                            bank = (hk * 2 + hc) % 4
                            for r_ in range(32):
                                TE(lambda e, r_=r_, hc=hc, lo=lo, hi=hi, bank=bank: e.matmul(PS[bank][:], w1[lo:hi, r_, hc * 128:(hc + 1) * 128], kpe[lo:hi, r_, :], start=(r_ == 0), stop=(r_ == 31)),
                                   r=["w1a", "w1b", "kpe"], w=[psk(bank)])
                            A(lambda e, bank=bank: e.activation(gx2[:], PS[bank][:], AF.Square), r=[psk(bank)], w=["gx2"])
                            V(lambda e: e.tensor_scalar(gx2[:], gx2[:], 0.044715, 1.0, op0=ALU.mult, op1=ALU.add), r=["gx2"], w=["gx2"])
                            V(lambda e, bank=bank: e.tensor_tensor(gu[:], gx2[:], PS[bank][:], op=ALU.mult), r=["gx2", psk(bank)], w=["gu"])
                            A(lambda e: e.activation(gu[:], gu[:], AF.Tanh, scale=0.7978845608028654), r=["gu"], w=["gu"])
                            V(lambda e: e.tensor_scalar(gu[:], gu[:], 0.5, 0.5, op0=ALU.mult, op1=ALU.add), r=["gu"], w=["gu"])
                            V(lambda e, hk=hk, hc=hc, bank=bank: e.tensor_tensor(gel[:, hk, hc, :], gu[:], PS[bank][:], op=ALU.mult),
                              r=["gu", psk(bank)], w=[("gel", hk, hc)])
                    for cc in range(4):
                        bank = 4 + cc % 2
                        for hk in range(2):
                            for hc in range(2):
                                TE(lambda e, cc=cc, hk=hk, hc=hc, bank=bank: e.matmul(PS[bank][:, hk * 64:(hk + 1) * 64], gel[:, hk, hc, cc * 128:(cc + 1) * 128], w2[:, hc, :], start=(hc == 0), stop=(hc == 1)),
                                   r=[("gel", hk, hc), "w2"], w=[psk(bank)])
                        pv = PS[bank][:, 0:128].rearrange("p (h d) -> p h d", h=2)
                        if which == 0:
                            head_norm_rope(tm, pv, 2, gkc[:].unsqueeze(1).to_broadcast([128, 2, 64]), cs_cmp(cc), cstg[:], [psk(bank)], ["cstg"], "k")
                            TE(lambda e, cc=cc: e.matmul(PS[6][:, cc * 128:(cc + 1) * 128], cstg[:].rearrange("p h d -> p (h d)"), ident[:], start=True, stop=True),
                               r=["cstg", "ident"], w=[psk(6)])
                        else:
                            V(lambda e, cc=cc, pv=pv: e.tensor_copy(WVC[:, cc, :, 128:192], pv), r=[psk(bank), "WVC"], w=["WVC"])
                    if which == 0:
                        V(lambda e: e.tensor_copy(KCT[:], PS[6][:]), r=[psk(6)], w=["KCT"])
                P.barrier()
                P.flush()
        dump("ksT", ksT_d, [128, T], BF)
        dump("kwT", kwT_d, [128, T], BF)
        dump("vs", vs_d, [T, 130], BF)
        dump("vw", vw_d, [T, 130], BF)
        dump("ksbT", ksbT_d.rearrange("p a t -> p (a t)"), [128, 4 * T], BF)
        dump("vsb", vsb_d, [T, 512], BF)
        dump("qsbT", qsbT_d.rearrange("p a t -> p (a t)"), [128, 4 * NOWN * 128], BF)
        dump("qnT", qnT_d.rearrange("p a t -> p (a t)"), [128, NOWN * 512], BF)
        dump("x1o", x1o_d[0:512, :], [512, D])
        dump("gates", gates[:].rearrange("p a b -> p (a b)"), [128, NOWN * 24])
        dump("KCT", KCT[:], [128, 512], BF)
        dump("WVC", WVC[:].rearrange("p a b c -> p (a b c)"), [128, 1600], BF)
        dump_done()
        if stop_after <= 2:
            return nc, DBG, None

        with ExitStack() as ph:
            KSA = [sbt(ph, "KSA0", [128, T], BF), sbt(ph, "KSA1", [128, T], BF)]
            RQ = sbt(ph, "RQ", [128, 2, 512], BF)
            KWT = sbt(ph, "KWT", [128, T], BF)
            VS = sbt(ph, "VS", [128, NT, 130], BF)
            VW = sbt(ph, "VW", [128, NT, 130], BF)
            QNT = sbt(ph, "QNT", [128, NOWN, 512], BF)
            mkc = sbt(ph, "mkc", [128, 8, 128], BF)
            cms = sbt(ph, "cms", [128, 2, 128], BF)
            wms = sbt(ph, "wms", [128, 6, 128], BF)
            cms4 = sbt(ph, "cms4", [128, 2, 4, 128], BF)
            wms4 = sbt(ph, "wms4", [128, 6, 4, 128], BF)
            A2t = [sbt(ph, "A2t%d" % k, [128, 128], F32) for k in range(2)]
            A13t = [sbt(ph, "A13t%d" % k, [128, 128], F32) for k in range(2)]
            Eb = [sbt(ph, "Eb%d" % k, [128, 512], BF) for k in range(3)]
            imp = sbt(ph, "imp", [128, 128], F32)
            sc = sbt(ph, "sc", [128, 128], F32)
            sc2 = sbt(ph, "sc2", [128, 128], F32)
            m8 = sbt(ph, "m8", [128, 8], F32)
            m8b = sbt(ph, "m8b", [128, 8], F32)
            nsel = sbt(ph, "nsel", [128, 128], BF)
            nselT = sbt(ph, "nselT", [128, 4, 128], BF)
            den = sbt(ph, "den", [128, 16], F32)
            oacc = sbt(ph, "oacc", [128, 2, 4, 64], F32)
            obf = sbt(ph, "obf", [128, 512], BF)
            oTs = sbt(ph, "oTs", [128, 4, 128], BF)
            P.dma("sync", KSA[0][0:64, :], ksT_d[0:64, :], w=[("KSA", 0)], semkey=("l", 0))
            P.dma("sync", KSA[0][64:128, :], IN["expd2"], w=[("KSA", 0)], semkey=("l", 0))
            P.dma("sync", KSA[1][64:128, :], ksT_d[64:128, :], w=[("KSA", 1)], semkey=("l", 5))
            P.dma("sync", KSA[1][0:64, :], IN["expd2"], w=[("KSA", 1)], semkey=("l", 5))
            P.dma("sync", KWT[:], kwT_d, w=["KWT"], semkey=("l", 1))
            for q8 in range(8):
                P.dma("sync", VS[:, q8 * 8:(q8 + 1) * 8, :], vs_d[q8 * 1024:(q8 + 1) * 1024, :].rearrange("(j p) c -> p j c", p=128), w=[("VS", q8)], semkey=("lv", q8 % 4))
                P.dma("sync", VW[:, q8 * 8:(q8 + 1) * 8, :], vw_d[q8 * 1024:(q8 + 1) * 1024, :].rearrange("(j p) c -> p j c", p=128), w=[("VW", q8)], semkey=("lw", q8 % 4))
            P.dma("sync", QNT[:], qnT_d, w=["QNT"], semkey=("l", 4))
            P.dma("sync", mkc[:].rearrange("p a b -> p (a b)"), IN["maskc"], w=["mkc"], semkey=("l", 6))
            P.dma("sync", cms[:].rearrange("p a b -> p (a b)"), IN["cmsel"], w=["cms"], semkey=("l", 7))
            P.dma("sync", wms[:].rearrange("p a b -> p (a b)"), IN["wmask"], w=["wms"], semkey=("l", 8))
            V(lambda e: e.tensor_copy(cms4[:], cms[:].unsqueeze(2).to_broadcast([128, 2, 4, 128])), r=["cms"], w=["cms4"])
            V(lambda e: e.tensor_copy(wms4[:], wms[:].unsqueeze(2).to_broadcast([128, 6, 4, 128])), r=["wms"], w=["wms4"])
            PSC = [PS[2][:].rearrange("p (g c) -> p g c", g=2), PS[3][:].rearrange("p (g c) -> p g c", g=2)]

            def pcmp(g):
                return PSC[g // 2][:, g % 2, :]

            PSS = PS[4][:].rearrange("p (g c) -> p g c", g=4)
            PSW = PS[5][:].rearrange("p (g c) -> p g c", g=4)
            ecnt = [0]

            def stream(tiles, q_rhs, pv_rhs_fn, pacc_fn, ncol, tag, firsts=(0,)):
                n = len(tiles)

                SB3 = (0, 1, 6)

                def emit_score(t):
                    b = SB3[t % 3]
                    mm = tiles[t]["score"]
                    for k_, (lh, rh, keys) in enumerate(mm):
                        TE(lambda e, lh=lh, rh=rh, b=b, k_=k_, L=len(mm): e.matmul(PS[b][:], lh, rh, start=(k_ == 0), stop=(k_ == L - 1)),
                           r=keys, w=[psk(b)])

                emit_score(0)
                if n > 1:
                    emit_score(1)
                for t in range(n):
                    if t + 2 < n:
                        emit_score(t + 2)
                    b = SB3[t % 3]
                    ei = ecnt[0] % 3
                    ecnt[0] += 1
                    E = Eb[ei]
                    A(lambda e, E=E, b=b: e.activation(E[:], PS[b][:], AF.Exp, scale=0.125), r=[psk(b)], w=[("E", ei)])
                    mk = tiles[t].get("mask")
                    if mk is not None:
                        V(lambda e, E=E, mk=mk: e.tensor_tensor(E[:].rearrange("p (g q) -> p g q", g=4), E[:].rearrange("p (g q) -> p g q", g=4), mk, op=ALU.mult),
                          r=[("E", ei), "mkc"], w=[("E", ei)])
                    rhs, rkeys = pv_rhs_fn(t)
                    for g in range(4):
                        TE(lambda e, E=E, g=g, rhs=rhs, t=t: e.matmul(pacc_fn(g)[:, 0:ncol], E[:, g * 128:(g + 1) * 128], rhs, start=(t == 0 and g in firsts), stop=(t == n - 1)),
                           r=[("E", ei)] + rkeys, w=[(tag, g)])

            for i in range(1 if quick else NOWN):
                k2 = i % 2
                P.dma("sync", A2t[k2][:], IN["A2"][i], w=[("A2t", k2)], semkey=("A2t", k2))
                P.dma("sync", A13t[k2][:], IN["A13"][i], w=[("A13t", k2)], semkey=("A13t", k2))
                for hk in range(2):
                    lo, hi = hk * 64, hk * 64 + 64
                    qr = QNT[lo:hi, i, :]
                    ncc = i // 8 + 1
                    tiles = []
                    for cc in range(ncc):
                        tiles.append({"score": [(KCT[lo:hi, cc * 128:(cc + 1) * 128], qr, ["KCT", "QNT"])],
                                      "mask": (mkc[:, i % 8, :].unsqueeze(1).to_broadcast([128, 4, 128]) if cc == ncc - 1 else None)})
                    stream(tiles, qr, lambda t, hk=hk: (WVC[:, t, hk, 0:193], ["WVC"]), pcmp, 193, "pc", firsts=(0, 2))
                    for g in range(4):
                        pc = pcmp(g)
                        gi = (hk * 4 + g) * 3
                        V(lambda e, pc=pc, g=g: e.tensor_scalar(den[:, g:g + 1], pc[:, 192:193], 1e-30, None, op0=ALU.max), r=[("pc", g)], w=[("den", g)])
                        V(lambda e, g=g: e.reciprocal(den[:, g:g + 1], den[:, g:g + 1]), r=[("den", g)], w=[("den", g)])
                        if g == 0:
                            V(lambda e, pc=pc, g=g: e.tensor_scalar(imp[:], pc[:, 0:128], den[:, g:g + 1], None, op0=ALU.mult), r=[("pc", g), ("den", g)], w=["imp"])
                        else:
                            V(lambda e, pc=pc, g=g: e.scalar_tensor_tensor(imp[:], pc[:, 0:128], den[:, g:g + 1], imp[:], op0=ALU.mult, op1=ALU.add),
                              r=[("pc", g), ("den", g), "imp"], w=["imp"])
                        V(lambda e, g=g, gi=gi, i=i: e.tensor_tensor(den[:, 4 + g:5 + g], den[:, g:g + 1], gates[:, i, gi:gi + 1], op=ALU.mult), r=[("den", g), "gates"], w=[("den", 4 + g)])
                        V(lambda e, pc=pc, g=g, hk=hk: e.tensor_scalar(oacc[:, hk, g, :], pc[:, 128:192], den[:, 4 + g:5 + g], None, op0=ALU.mult),
                          r=[("pc", g), ("den", 4 + g)], w=[("oacc", hk, g)])
                    V(lambda e, k2=k2: e.tensor_tensor(sc[:], imp[:], A2t[k2][:], op=ALU.mult), r=["imp", ("A2t", k2)], w=["sc"])
                    V(lambda e, k2=k2: e.tensor_tensor(sc[:], sc[:], A13t[k2][:], op=ALU.add), r=["sc", ("A13t", k2)], w=["sc"])
                    V(lambda e: e.max(out=m8[:], in_=sc[:]), r=["sc"], w=["m8"])
                    V(lambda e: e.match_replace(out=sc2[:], in_to_replace=m8[:], in_values=sc[:], imm_value=-2.0), r=["m8", "sc"], w=["sc2"])
                    V(lambda e: e.max(out=m8b[:], in_=sc2[:]), r=["sc2"], w=["m8b"])
                    V(lambda e: e.tensor_scalar(sc2[:], sc[:], m8b[:, 7:8], None, op0=ALU.is_ge), r=["sc", "m8b"], w=["sc2"])
                    V(lambda e: e.tensor_scalar(nsel[:], sc2[:], -1.0, BIG, op0=ALU.add, op1=ALU.mult), r=["sc2"], w=["nsel"])
                    tiles = []
                    kts = []
                    for rel in range(6):
                        kt = 2 * i - 4 + rel
                        if kt < 0:
                            continue
                        kts.append(kt)
                        tiles.append({"score": [(KWT[lo:hi, kt * 128:(kt + 1) * 128], qr, ["KWT", "QNT"]),
                                                (ident[:], wms4[:, rel, :, :].rearrange("p g q -> p (g q)"), ["ident", "wms4"])]})
                    stream(tiles, qr, lambda t, hk=hk, kts=kts: (VW[:, kts[t], hk * 65:(hk + 1) * 65], [("VW", kts[t] // 8)]), lambda g: PSW[:, g, :], 65, "pwin")
                    olo = 64 - lo
                    for h2 in range(2):
                        TE(lambda e, h2=h2, olo=olo: e.matmul(PS[7][olo:olo + 64, h2 * 128:(h2 + 1) * 128], nsel[:, h2 * 64:(h2 + 1) * 64], ident[:], start=True, stop=True),
                           r=["nsel", "ident"], w=[psk(7)])
                    V(lambda e, lo=lo, hi=hi, i=i: e.tensor_copy(RQ[lo:hi, :, :], QNT[lo:hi, i, :].unsqueeze(1).to_broadcast([64, 2, 512])), r=["QNT"], w=["RQ"])
                    for h2 in range(2):
                        V(lambda e, h2=h2, olo=olo: e.tensor_copy(RQ[olo:olo + 64, h2, :].rearrange("p (g q) -> p g q", g=4), PS[7][olo:olo + 64, h2 * 128:(h2 + 1) * 128].unsqueeze(1).to_broadcast([64, 4, 128])),
                          r=[psk(7), "RQ"], w=["RQ"])
                    if debug and i in (0, 1, 4, 15, 31):
                        dump("sel_%d_%d" % (i, hk), sc2[:], [128, 128], r=["sc2"])
                        dump("imp_%d_%d" % (i, hk), imp[:], [128, 128], r=["imp"])
                    for g in range(4):
                        gi = (hk * 4 + g) * 3 + 1 + 1
                        c0 = 8 + 1 * 4 + g
                        V(lambda e, g=g, c0=c0: e.tensor_scalar(den[:, c0:c0 + 1], PSW[:, g, 64:65], 1e-30, None, op0=ALU.max), r=[("pwin", g)], w=[("den", c0)])
                        V(lambda e, c0=c0: e.reciprocal(den[:, c0:c0 + 1], den[:, c0:c0 + 1]), r=[("den", c0)], w=[("den", c0)])
                        V(lambda e, c0=c0, gi=gi, i=i: e.tensor_tensor(den[:, c0:c0 + 1], den[:, c0:c0 + 1], gates[:, i, gi:gi + 1], op=ALU.mult), r=[("den", c0), "gates"], w=[("den", c0)])
                        V(lambda e, g=g, c0=c0, hk=hk: e.scalar_tensor_tensor(oacc[:, hk, g, :], PSW[:, g, 0:64], den[:, c0:c0 + 1], oacc[:, hk, g, :], op0=ALU.mult, op1=ALU.add),
                          r=[("pwin", g), ("den", c0), ("oacc", hk, g)], w=[("oacc", hk, g)])
                    tiles = []
                    for kt in range(2 * i + 2):
                        mm = [(KSA[hk][:, kt * 128:(kt + 1) * 128], RQ[:, kt // 32, :], [("KSA", hk), "RQ"])]
                        if kt >= 2 * i:
                            mm.append((ident[:], cms4[:, kt - 2 * i, :, :].rearrange("p g q -> p (g q)"), ["ident", "cms4"]))
                        tiles.append({"score": mm})
                    stream(tiles, qr, lambda t, hk=hk: (VS[:, t, hk * 65:(hk + 1) * 65], [("VS", t // 8)]), lambda g: PSS[:, g, :], 65, "psel")
                    for g in range(4):
                        gi = (hk * 4 + g) * 3 + 1 + 0
                        c0 = 8 + 0 * 4 + g
                        V(lambda e, g=g, c0=c0: e.tensor_scalar(den[:, c0:c0 + 1], PSS[:, g, 64:65], 1e-30, None, op0=ALU.max), r=[("psel", g)], w=[("den", c0)])
                        V(lambda e, c0=c0: e.reciprocal(den[:, c0:c0 + 1], den[:, c0:c0 + 1]), r=[("den", c0)], w=[("den", c0)])
                        V(lambda e, c0=c0, gi=gi, i=i: e.tensor_tensor(den[:, c0:c0 + 1], den[:, c0:c0 + 1], gates[:, i, gi:gi + 1], op=ALU.mult), r=[("den", c0), "gates"], w=[("den", c0)])
                        V(lambda e, g=g, c0=c0, hk=hk: e.scalar_tensor_tensor(oacc[:, hk, g, :], PSS[:, g, 0:64], den[:, c0:c0 + 1], oacc[:, hk, g, :], op0=ALU.mult, op1=ALU.add),
                          r=[("psel", g), ("den", c0), ("oacc", hk, g)], w=[("oacc", hk, g)])
                okeys = [("oacc", hk_, g_) for hk_ in range(2) for g_ in range(4)]
                V(lambda e: e.tensor_copy(obf[:], oacc[:].rearrange("p a b c -> p (a b c)")), r=okeys, w=["obf"])
                for c in range(4):
                    TE(lambda e, c=c: e.matmul(PS[7][:, c * 128:(c + 1) * 128], obf[:, c * 128:(c + 1) * 128], ident[:], start=True, stop=True), r=["obf", "ident"], w=[psk(7)])
                V(lambda e: e.tensor_copy(oTs[:].rearrange("p a b -> p (a b)"), PS[7][:]), r=[psk(7)], w=["oTs"])
                P.dma("sync", oT_d[:, 0:4, i * 128:(i + 1) * 128], oTs[:], r=["oTs"], w=[("oTd", i)], semkey="oTs")
            P.barrier()
            P.flush()
        dump("oT_nsa", oT_d[:, 0:4, :].rearrange("p a t -> p (a t)"), [128, 4 * NOWN * 128], BF)
        dump_done()
        if stop_after <= 3:
            return nc, DBG, None

        with ExitStack() as ph:
            KSB = sbt(ph, "KSB", [128, 4, T], BF)
            VSB = sbt(ph, "VSB", [128, NT, 512], BF)
            QSB = sbt(ph, "QSB", [128, 4, NOWN * 128], BF)
            uneg = sbt(ph, "uneg", [128, 128], BF)
            nones = sbt(ph, "nones", [128, 128], BF)
            sbm = sbt(ph, "sbm", [128, 8, 512], BF)
            e1 = sbt(ph, "e1", [128, 2, 512], F32)
            spb = [sbt(ph, "spb%d" % k, [128, 2, 512], BF) for k in range(2)]
            aTb = [sbt(ph, "aTb%d" % k, [128, 2, 512], BF) for k in range(2)]
            Rb = sbt(ph, "Rb", [128, 2, 512], BF)
            osb = sbt(ph, "osb", [128, 512], BF)
            for pr in range(4):
                P.dma("sync", KSB[:, pr, :], ksbT_d[:, pr, :], w=[("KSB", pr)], semkey=("lw", pr))
            for q8 in range(8):
                P.dma("sync", VSB[:, q8 * 8:(q8 + 1) * 8, :], vsb_d[q8 * 1024:(q8 + 1) * 1024, :].rearrange("(j p) c -> p j c", p=128), w=[("VSB", q8)], semkey=("lv", q8 % 4))
            P.dma("sync", QSB[:], qsbT_d, w=["QSB"], semkey=("l", 8))
            P.dma("sync", uneg[:], IN["uneg"], w=["uneg"], semkey=("l", 9))
            P.dma("sync", nones[:], IN["negones"], w=["nones"], semkey=("l", 10))
            P.dma("sync", sbm[:].rearrange("p a b -> p (a b)"), IN["sbmask"], w=["sbm"], semkey=("l", 11))
            PSP = [PSALL[:, 0:2, :], PSALL[:, 2:4, :], PSALL[:, 4:6, :]]
            sbq = quick and not os.environ.get("SBFULL")
            for pr in range(1 if sbq else 4):
                for g8 in range(2 if sbq else 8):
                    nkt = 8 * (g8 + 1)
                    kts = list(range(nkt - 1, -1, -1))

                    def c0_of(kt, g8=g8):
                        r = kt - 8 * g8
                        return 0 if r <= 0 else 128 * ((r - 1 + 1) // 2)

                    V(lambda e: e.memset(Rb[:], 0.0), w=["Rb"])

                    def S1(t, kts=kts, pr=pr, g8=g8):
                        kt = kts[t]
                        c0 = c0_of(kt)
                        for hh in range(2):
                            lo, hi = hh * 64, hh * 64 + 64
                            b = (t % 3) * 2 + hh
                            TE(lambda e, lo=lo, hi=hi, b=b, kt=kt, pr=pr, g8=g8, c0=c0: e.matmul(PS[b][:, c0:512], KSB[lo:hi, pr, kt * 128:(kt + 1) * 128], QSB[lo:hi, pr, g8 * 512 + c0:(g8 + 1) * 512], start=True, stop=False),
                               r=[("KSB", pr), "QSB"], w=[("psp", t % 3)])

                    def S2(t, kts=kts, g8=g8):
                        kt = kts[t]
                        c0 = c0_of(kt)
                        k2, k3 = t % 2, t % 3
                        A(lambda e, k3=k3, c0=c0: e.activation(e1[:, :, c0:512], PSP[k3][:, :, c0:512], AF.Exp), r=[("psp", k3)], w=["e1"])
                        A(lambda e, k2=k2, c0=c0: e.activation(spb[k2][:, :, c0:512], e1[:, :, c0:512], AF.Ln, bias=1.0), r=["e1"], w=[("spb", k2)])
                        if kt >= 8 * g8:
                            mk = sbm[:, kt - 8 * g8, c0:512].unsqueeze(1).to_broadcast([128, 2, 512 - c0])
                            V(lambda e, k2=k2, mk=mk, c0=c0: e.tensor_tensor(spb[k2][:, :, c0:512], spb[k2][:, :, c0:512], mk, op=ALU.mult), r=[("spb", k2), "sbm"], w=[("spb", k2)])

                    def S3(t, nkt=nkt, kts=kts):
                        k2, k3 = t % 2, t % 3
                        c0 = c0_of(kts[t])
                        for hh in range(2):
                            b = k3 * 2 + hh
                            TE(lambda e, b=b, k2=k2, hh=hh, t=t, c0=c0: e.matmul(PS[b][:, c0:512], uneg[:], spb[k2][:, hh, c0:512], start=False, stop=(t == 0)), r=[("spb", k2), "uneg"], w=[("psp", k3)])
                            if t > 0:
                                TE(lambda e, b=b, hh=hh, c0=c0: e.matmul(PS[b][:, c0:512], nones[:], Rb[:, hh, c0:512], start=False, stop=True), r=["Rb", "nones"], w=[("psp", k3)])
                        if t + 1 < nkt:
                            V(lambda e, k2=k2, c0=c0: e.tensor_tensor(Rb[:, :, c0:512], Rb[:, :, c0:512], spb[k2][:, :, c0:512], op=ALU.add), r=["Rb", ("spb", k2)], w=["Rb"])

                    def S4(t, kts=kts, g8=g8):
                        kt = kts[t]
                        c0 = c0_of(kt)
                        k2, k3 = t % 2, t % 3
                        A(lambda e, k2=k2, k3=k3, c0=c0: e.activation(aTb[k2][:, :, c0:512], PSP[k3][:, :, c0:512], AF.Exp), r=[("psp", k3)], w=[("aTb", k2)])
                        if kt >= 8 * g8:
                            mk = sbm[:, kt - 8 * g8, c0:512].unsqueeze(1).to_broadcast([128, 2, 512 - c0])
                            V(lambda e, k2=k2, mk=mk, c0=c0: e.tensor_tensor(aTb[k2][:, :, c0:512], aTb[k2][:, :, c0:512], mk, op=ALU.mult), r=[("aTb", k2), "sbm"], w=[("aTb", k2)])

                    def S5(t, kts=kts, pr=pr, nkt=nkt):
                        kt = kts[t]
                        c0 = c0_of(kt)
                        k2 = t % 2
                        for hh in range(2):
                            h = pr * 2 + hh
                            TE(lambda e, hh=hh, k2=k2, kt=kt, h=h, t=t, nkt=nkt, c0=c0: e.matmul(PS[6 + hh][hh * 64:hh * 64 + 64, c0:512], VSB[:, kt, h * 64:(h + 1) * 64], aTb[k2][:, hh, c0:512], start=(t == 0), stop=(t == nkt - 1)),
                               r=[("aTb", k2), ("VSB", kt // 8)], w=[psk(6 + hh)])

                    S1(0)
                    S1(1)
                    S2(0)
                    S3(0)
                    for s_ in range(nkt):
                        if s_ + 2 < nkt:
                            S1(s_ + 2)
                        if s_ + 1 < nkt:
                            S2(s_ + 1)
                            S3(s_ + 1)
                        S4(s_)
                        S5(s_)
                    for hh in range(2):
                        evac_copy(hh, osb[hh * 64:hh * 64 + 64, :], PS[6 + hh][hh * 64:hh * 64 + 64, :], r=[psk(6 + hh)], w=["osb"])
                    P.dma("sync", oT_d[:, 4 + pr, g8 * 512:(g8 + 1) * 512], osb[:], r=["osb"], w=[("oTd", pr, g8)], semkey="osb")
            P.barrier()
            P.flush()
        dump("oT_sb", oT_d[:, 4:8, :].rearrange("p a t -> p (a t)"), [128, 4 * NOWN * 128], BF)
        dump_done()
        if stop_after <= 4:
            return nc, DBG, None

        pre2 = top.enter_context(ExitStack())
        wg2_pre = sbt(pre2, "wg2pre", [128, 8, FF], BF)
        with ExitStack() as ph:
            wout = sbt(ph, "wout", [128, 8, D], BF)
            mwq = sbt(ph, "mwq", [128, 8, 256], BF)
            mwk = sbt(ph, "mwk", [128, 8, 256], BF)
            mwv = sbt(ph, "mwv", [128, 8, 256], BF)
            mwo = sbt(ph, "mwo", [128, 2, D], BF)
            gbc = sbt(ph, "gbc", [128, D], F32)
            gkv = sbt(ph, "gkv", [128, D], F32)
            gmq = sbt(ph, "gmq", [128, 64], F32)
            gmk = sbt(ph, "gmk", [128, 64], F32)
            mt_ = sbt(ph, "mt", [128, 2, D], F32)
            mb = [sbt(ph, "mb%d" % k, [128, D], BF) for k in range(2)]
            memT = sbt(ph, "memT", [128, 8, 256], BF)
            KMT = sbt(ph, "KMT", [128, 2, 256], BF)
            VM = sbt(ph, "VM", [128, 2, 4, 65], BF)
            kmstg = sbt(ph, "kmstg", [128, 4, 64], BF)
            xt = sbt(ph, "xt", [128, 4, D], F32)
            oTb = sbt(ph, "oTb", [128, 8, 512], BF)
            hb = [sbt(ph, "hb%d" % k, [128, D], BF) for k in range(2)]
            hT = sbt(ph, "hT", [128, 8, 512], BF)
            qstg = sbt(ph, "qstg", [128, 4, 64], BF)
            qmT = sbt(ph, "qmT", [128, 2, 512], BF)
            Em = [sbt(ph, "Em%d" % k, [128, 512], BF) for k in range(2)]
            om = sbt(ph, "om", [128, 4, 256], BF)
            omT = sbt(ph, "omT", [128, 2, 512], BF)
            rd = sbt(ph, "rd", [128, 16], F32)
            junk = sbt(ph, "junk", [128, D], BF)
            ss4 = sbt(ph, "ss4", [128, 4], F32)
            rstd = sbt(ph, "rstd", [128, 4], F32)
            tm = (sbt(ph, "nsq", [128, 8, 64], F32), sbt(ph, "nss", [128, 8], F32), sbt(ph, "ny", [128, 8, 64], F32),
                  sbt(ph, "nt1", [128, 8, 8], F32), sbt(ph, "nt2", [128, 8, 8], F32))
            load_w_bf(wout, IN["w_out"], 8, "wout")
            load_w_bf(mwq, IN["mwq"], 8, "mwq")
            load_w_bf(mwk, IN["mwk"], 8, "mwk")
            load_w_bf(mwv, IN["mwv"], 8, "mwv")
            load_w_bf(mwo, IN["mwo"], 2, "mwo")
            load_w_bf(wg2_pre, IN["wg2"], 8, "wg")
            P.dma("sync", gbc[:], IN["g_memx"], w=["gbc"], semkey="c0")
            P.dma("sync", gkv[:], IN["g_memkv"], w=["gkv"], semkey="c1")
            P.dma("sync", gmq[:], IN["g_mq"], w=["gains"], semkey="c2")
            P.dma("sync", gmk[:], IN["g_mk"], w=["gains"], semkey="c3")
            P.dma("sync", mt_[:], IN["mem"].rearrange("(j p) d -> p j d", p=128), w=["mt"], semkey="c4")
            V(lambda e: e.memset(VM[:], 1.0), w=["VM"])
            rms_stats((junk, ss4, rstd), [mt_[:, j, :] for j in range(2)], ["mt", "mt"], "mm")
            for j in range(2):
                V(lambda e, j=j: e.scalar_tensor_tensor(mb[j][:], mt_[:, j, :], rstd[:, j:j + 1], gkv[:], op0=ALU.mult, op1=ALU.mult), r=["mt", "mmrstd", "gkv"], w=[("mb", j)])
                for half in range(2):
                    bank = 4 + half
                    for kk in range(4):
                        kc = half * 4 + kk
                        TE(lambda e, j=j, kc=kc, kk=kk, bank=bank: e.matmul(PS[bank][:, kk * 128:(kk + 1) * 128], mb[j][:, kc * 128:(kc + 1) * 128], ident[:], start=True, stop=True),
                           r=[("mb", j), "ident"], w=[psk(bank)])
                    evac_copy(j + half, memT[:, half * 4:half * 4 + 4, j * 128:(j + 1) * 128], PS[bank][:].rearrange("p (k t) -> p k t", k=4), r=[psk(bank)], w=[("memT", half, j)])
            mT_keys = [("memT", h_, j_) for h_ in range(2) for j_ in range(2)]
            for j in range(2):
                for kc in range(8):
                    TE(lambda e, j=j, kc=kc: e.matmul(PS[0][:, 0:256], memT[:, kc, j * 128:(j + 1) * 128], mwk[:, kc, :], start=(kc == 0), stop=(kc == 7)), r=mT_keys + ["mwkA", "mwkB"], w=[psk(0)])
                head_norm_rope(tm, PS[0][:, 0:256].rearrange("p (h d) -> p h d", h=4), 4, gmk[:].unsqueeze(1).to_broadcast([128, 4, 64]), None, kmstg[:], [psk(0)], ["kmstg"], "k")
                for pr in range(2):
                    TE(lambda e, pr=pr: e.matmul(PS[1][:, pr * 128:(pr + 1) * 128], kmstg[:, 2 * pr:2 * pr + 2, :].rearrange("p h d -> p (h d)"), ident[:], start=True, stop=True), r=["kmstg", "ident"], w=[psk(1)])
                V(lambda e, j=j: e.tensor_copy(KMT[:, :, j * 128:(j + 1) * 128], PS[1][:, 0:256].rearrange("p (a t) -> p a t", a=2)), r=[psk(1)], w=["KMT"])
                for kc in range(8):
                    TE(lambda e, j=j, kc=kc: e.matmul(PS[2][:, 0:256], memT[:, kc, j * 128:(j + 1) * 128], mwv[:, kc, :], start=(kc == 0), stop=(kc == 7)), r=mT_keys + ["mwvA", "mwvB"], w=[psk(2)])
                V(lambda e, j=j: e.tensor_copy(VM[:, j, :, 0:64], PS[2][:, 0:256].rearrange("p (h d) -> p h d", h=4)), r=[psk(2), "VM"], w=["VM"])
            for n in range(8):
                for j in range(4):
                    P.dma("sync", xt[:, j, :], x1o_d[(n * 4 + j) * 128:(n * 4 + j + 1) * 128, :], w=[("xt", j)], semkey=("xt", j))
                P.dma("sync", oTb[:], oT_d[:, :, n * 512:(n + 1) * 512], w=["oTb"], semkey="oTb")
                for j in range(4):
                    for half in range(2):
                        bank = 6 + half
                        for c in range(8):
                            TE(lambda e, j=j, half=half, c=c, bank=bank: e.matmul(PS[bank][:], oTb[:, c, j * 128:(j + 1) * 128], wout[:, c, half * 512:(half + 1) * 512], start=(c == 0), stop=(c == 7)),
                               r=["oTb", "woutA", "woutB"], w=[psk(bank)])
                        V(lambda e, j=j, half=half, bank=bank: e.tensor_tensor(xt[:, j, half * 512:(half + 1) * 512], xt[:, j, half * 512:(half + 1) * 512], PS[bank][:], op=ALU.add),
                          r=[psk(bank), ("xt", j)], w=[("xt", j)])
                if debug and n == 0:
                    for j in range(4):
                        dump("x2_%d" % j, xt[:, j, :], [128, D], r=[("xt", j)])
                rms_stats((junk, ss4, rstd), [xt[:, j, :] for j in range(4)], [("xt", j) for j in range(4)], "x")
                for j in range(4):
                    hbj = hb[j % 2]
                    V(lambda e, j=j, hbj=hbj: e.scalar_tensor_tensor(hbj[:], xt[:, j, :], rstd[:, j:j + 1], gbc[:], op0=ALU.mult, op1=ALU.mult), r=[("xt", j), "xrstd", "gbc"], w=[("hb", j % 2)])
                    for half in range(2):
                        bank = 4 + half
                        for kk in range(4):
                            kc = half * 4 + kk
                            TE(lambda e, hbj=hbj, kc=kc, kk=kk, bank=bank: e.matmul(PS[bank][:, kk * 128:(kk + 1) * 128], hbj[:, kc * 128:(kc + 1) * 128], ident[:], start=True, stop=True),
                               r=[("hb", j % 2), "ident"], w=[psk(bank)])
                        evac_copy(j + half, hT[:, half * 4:half * 4 + 4, j * 128:(j + 1) * 128], PS[bank][:].rearrange("p (k t) -> p k t", k=4), r=[psk(bank)], w=[("hT", half, j)])
                hT_keys = [("hT", h_, j_) for h_ in range(2) for j_ in range(4)]
                for j in range(4):
                    for kc in range(8):
                        TE(lambda e, j=j, kc=kc: e.matmul(PS[0][:, 0:256], hT[:, kc, j * 128:(j + 1) * 128], mwq[:, kc, :], start=(kc == 0), stop=(kc == 7)), r=hT_keys + ["mwqA", "mwqB"], w=[psk(0)])
                    head_norm_rope(tm, PS[0][:, 0:256].rearrange("p (h d) -> p h d", h=4), 4, gmq[:].unsqueeze(1).to_broadcast([128, 4, 64]), None, qstg[:], [psk(0)], ["qstg"], "k")
                    for pr in range(2):
                        TE(lambda e, pr=pr: e.matmul(PS[1][:, pr * 128:(pr + 1) * 128], qstg[:, 2 * pr:2 * pr + 2, :].rearrange("p h d -> p (h d)"), ident[:], start=True, stop=True), r=["qstg", "ident"], w=[psk(1)])
                    V(lambda e, j=j: e.tensor_copy(qmT[:, :, j * 128:(j + 1) * 128], PS[1][:, 0:256].rearrange("p (a t) -> p a t", a=2)), r=[psk(1)], w=[("qmT", j)])
                qm_keys = [("qmT", j_) for j_ in range(4)]
                PSO = [PS[4][:].rearrange("p (h c) -> p h c", h=4), PS[5][:].rearrange("p (h c) -> p h c", h=4),
                       PS[6][:].rearrange("p (h c) -> p h c", h=4), PS[7][:].rearrange("p (h c) -> p h c", h=4)]
                cnt_e = 0
                for h in range(4):
                    pr, lo = h // 2, (h % 2) * 64
                    for mt2 in range(2):
                        b = 2 + cnt_e % 2
                        E = Em[cnt_e % 2]
                        ek = ("Em", cnt_e % 2)
                        cnt_e += 1
                        TE(lambda e, pr=pr, lo=lo, mt2=mt2, b=b: e.matmul(PS[b][:], KMT[lo:lo + 64, pr, mt2 * 128:(mt2 + 1) * 128], qmT[lo:lo + 64, pr, :], start=True, stop=True), r=["KMT"] + qm_keys, w=[psk(b)])
                        A(lambda e, E=E, b=b: e.activation(E[:], PS[b][:], AF.Exp, scale=0.125), r=[psk(b)], w=[ek])
                        for j in range(4):
                            TE(lambda e, E=E, j=j, h=h, mt2=mt2: e.matmul(PSO[j][:, h, 0:65], E[:, j * 128:(j + 1) * 128], VM[:, mt2, h, :], start=(mt2 == 0), stop=(mt2 == 1)), r=[ek, "VM"], w=[psk(4 + j)])
                for j in range(4):
                    for h in range(4):
                        V(lambda e, j=j, h=h: e.reciprocal(rd[:, j * 4 + h:j * 4 + h + 1], PSO[j][:, h, 64:65]), r=[psk(4 + j)], w=[("rd", j, h)])
                        V(lambda e, j=j, h=h: e.tensor_scalar(om[:, j, h * 64:(h + 1) * 64], PSO[j][:, h, 0:64], rd[:, j * 4 + h:j * 4 + h + 1], None, op0=ALU.mult), r=[psk(4 + j), ("rd", j, h)], w=[("om", j)])
                for j in range(4):
                    for c in range(2):
                        TE(lambda e, j=j, c=c: e.matmul(PS[0][:, c * 128:(c + 1) * 128], om[:, j, c * 128:(c + 1) * 128], ident[:], start=True, stop=True), r=[("om", j), "ident"], w=[psk(0)])
                    V(lambda e, j=j: e.tensor_copy(omT[:, :, j * 128:(j + 1) * 128], PS[0][:, 0:256].rearrange("p (a t) -> p a t", a=2)), r=[psk(0)], w=[("omT", j)])
                omT_keys = [("omT", j_) for j_ in range(4)]
                for j in range(4):
                    for half in range(2):
                        bank = 2 + half
                        for c in range(2):
                            TE(lambda e, j=j, half=half, c=c, bank=bank: e.matmul(PS[bank][:], omT[:, c, j * 128:(j + 1) * 128], mwo[:, c, half * 512:(half + 1) * 512], start=(c == 0), stop=(c == 1)),
                               r=omT_keys + ["mwoA", "mwoB"], w=[psk(bank)])
                        V(lambda e, j=j, half=half, bank=bank: e.tensor_tensor(xt[:, j, half * 512:(half + 1) * 512], xt[:, j, half * 512:(half + 1) * 512], PS[bank][:], op=ALU.add),
                          r=[psk(bank), ("xt", j)], w=[("xt", j)])
                    P.dma("sync", x3o_d[(n * 4 + j) * 128:(n * 4 + j + 1) * 128, :], xt[:, j, :], r=[("xt", j)], w=[("x3d", n, j)], semkey=("st", j))
            P.barrier()
            P.flush()
        dump("x3", x3o_d[0:512, :], [512, D])
        dump_done()
        if stop_after <= 5:
            return nc, DBG, None

        ffn_phase("f2", x3o_d, out_d, 8, IN["wg2"], IN["wu2"], IN["wd2"], IN["g_ffn2"], wg_pre=wg2_pre)
        print("n_instr", P.n_instr, "n_ops", len(P.ops), "dma_sems", len(P.dma_sem))
        return nc, DBG, None


def _consts():
    c = {}
    c["freqs"] = np.broadcast_to((500000.0 ** (-np.arange(0, 16, 2, dtype=np.float32) / 16)).astype(np.float32)[None, :], (128, 8)).copy()
    c["ident"] = np.eye(128, dtype=np.float32).astype(NPBF)
    jj, ss_ = np.meshgrid(np.arange(128), np.arange(128), indexing="ij")
    c["uneg"] = (-(jj >= ss_).astype(np.float32)).astype(NPBF)
    c["negones"] = (-np.ones((128, 128), np.float32)).astype(NPBF)
    ex = np.zeros((64, 64, 128), np.float32)
    for kt in range(64):
        ex[(2 * kt) % 64, kt, 0:64] = 1.0
        ex[(2 * kt + 1) % 64, kt, 64:128] = 1.0
    c["expd2"] = ex.reshape(64, 64 * 128).astype(NPBF)
    n = np.arange(512)[:, None]
    j = np.arange(128)[None, :]
    ov = np.minimum(n * 16 + 32, j * 64 + 64) - np.maximum(n * 16, j * 64)
    ov = np.clip(ov, 0, None).astype(np.float32) / 32.0
    ov[511, :] = 0.0
    c["wov"] = ov.reshape(4, 128, 128).transpose(1, 0, 2).reshape(128, 512).astype(NPBF)
    return c


def _core_tables(p):
    t = {}
    sab = np.zeros((128, 2), np.float32)
    sab[:, p] = 1.0
    t["sab"] = sab
    nl = np.arange(128)[:, None]
    ql = np.arange(128)[None, :]
    mc = np.zeros((128, 8, 128), np.float32)
    for a in range(8):
        mc[:, a, :] = (16 * nl + 31 <= 256 * a + 128 * p + ql)
    t["maskc"] = mc.reshape(128, 1024).astype(NPBF)
    A2 = np.zeros((NOWN, 128, 128), np.float32)
    A13 = np.zeros((NOWN, 128, 128), np.float32)
    jb = np.arange(128)[None, :]
    for i in range(NOWN):
        tq = 128 * (2 * i + p) + np.arange(128)[:, None]
        cur = tq // 64
        valid = (jb * 64 <= tq)
        big = np.zeros((128, 128), np.float32)
        big = np.where(jb == 0, 1e4, big)
        big = np.where(jb == cur - 1, 2e4, big)
        big = np.where(jb == cur, 3e4, big)
        forced = big > 0
        A2[i] = (valid & ~forced)
        A13[i] = np.where(valid & forced, big, np.where(valid, 0.0, -1.0))
    t["A2"] = A2
    t["A13"] = A13
    kl = np.arange(128)[:, None]
    causal = np.where(kl <= ql, 0.0, -BIG).astype(np.float32)
    cm = np.zeros((128, 2, 128), np.float32)
    if p == 0:
        cm[:, 0, :] = causal
        cm[:, 1, :] = -BIG
    else:
        cm[:, 0, :] = 0.0
        cm[:, 1, :] = causal
    t["cmsel"] = cm.reshape(128, 256).astype(NPBF)
    wm = np.zeros((128, 6, 128), np.float32)
    for rel in range(6):
        dd = 128 * (p + 4 - rel) + ql - kl
        wm[:, rel, :] = np.where((dd >= 0) & (dd < 512), 0.0, -BIG)
    t["wmask"] = wm.reshape(128, 768).astype(NPBF)
    sm = np.zeros((128, 8, 4, 128), np.float32)
    for r in range(8):
        for m in range(4):
            sm[:, r, m, :] = (128 * (r - 2 * m - p) + kl < ql)
    t["sbmask"] = sm.reshape(128, 8 * 512).astype(NPBF)
    return t


def _prep_inputs(inp):
    consts = _consts()
    f = lambda k: np.ascontiguousarray(np.asarray(inp[k])[0], dtype=np.float32)
    rep = lambda v, n=128: np.ascontiguousarray(np.broadcast_to(np.asarray(v, np.float32)[None, :], (n, v.shape[-1])))
    shared = {
        "wg1": f("ffn1_wg"), "wu1": f("ffn1_wu"), "wd1": f("ffn1_wd"),
        "wg2": f("ffn2_wg"), "wu2": f("ffn2_wu"), "wd2": f("ffn2_wd"),
        "w_in": f("w_in"), "w_out": f("w_out"),
        "mwq": f("mem_wq"), "mwk": f("mem_wk"), "mwv": f("mem_wv"), "mwo": f("mem_wo"),
        "cw1k": f("cmp_w1_k"), "cw1v": f("cmp_w1_v"), "cw2k": f("cmp_w2_k"), "cw2v": f("cmp_w2_v"),
        "pekT": np.ascontiguousarray(f("cmp_pos_k").T), "pevT": np.ascontiguousarray(f("cmp_pos_v").T),
        "g_ffn1": rep(f("ffn1_norm")), "g_mix": rep(f("mix_norm")), "g_memx": rep(f("mem_x_norm")),
        "g_memkv": rep(f("mem_kv_norm")), "g_ffn2": rep(f("ffn2_norm")),
        "g_q": rep(f("nsa_q_norm")), "g_kc": rep(f("nsa_kc_norm")),
        "g_ksw": np.ascontiguousarray(np.concatenate([rep(f("nsa_ks_norm")), rep(f("nsa_kw_norm"))], axis=1)),
        "g_mq": rep(f("mem_q_norm")), "g_mk": rep(f("mem_k_norm")),
    }
    shared.update(consts)
    x = np.asarray(inp["x"], np.float32)
    mem = np.asarray(inp["mem"], np.float32)
    pos = np.asarray(inp["positions"]).astype(np.int32)
    tabs = [_core_tables(0), _core_tables(1)]
    in_maps = []
    for c in range(8):
        b, p = c // 2, c % 2
        m = dict(shared)
        m.update(tabs[p])
        m["x"] = np.ascontiguousarray(x[b])
        m["mem"] = np.ascontiguousarray(mem[b])
        pt = pos[b].reshape(64, 128)
        own = pt[p::2]
        cmp_pos = np.zeros(512, np.int32)
        cmp_pos[:511] = pos[b][np.arange(511) * 16 + 31]
        m["pos_cat"] = np.ascontiguousarray(np.concatenate([pt.T, own.T, cmp_pos.reshape(4, 128).T], axis=1).astype(np.int32))
        in_maps.append(m)
    return in_maps


def kernel(**inputs):
    in_maps = _prep_inputs(inputs)
    nc, _, _ = build_nc()
    res = run_bass_kernel_spmd(nc, in_maps, core_ids=list(range(8)))
    out = np.zeros((4, T, D), np.float32)
    for c in range(8):
        b, p = c // 2, c % 2
        o = np.asarray(res.results[c]["out"]).reshape(NOWN, 128, D)
        out[b].reshape(64, 128, D)[p::2] = o
    return out
```

```python
import os
import numpy as np
import ml_dtypes
from contextlib import ExitStack
import concourse.bass as bass
import concourse.mybir as mybir
from concourse.bass_utils import run_bass_kernel_spmd

F32 = mybir.dt.float32
BF = mybir.dt.bfloat16
I32 = mybir.dt.int32
AF = mybir.ActivationFunctionType
ALU = mybir.AluOpType
AX = mybir.AxisListType
NPBF = ml_dtypes.bfloat16

T = 8192
D = 1024
FF = 2816
NFC = 22
NT = 64
NOWN = 32
EPS = 1e-6
BIG = 30000.0
TWO_PI = 6.283185307179586
PI = 3.141592653589793
C_QN, C_KC, C_VC, C_KS, C_VS, C_KW, C_VW, C_GT, C_QS, C_KSB, C_VSB = 0, 512, 640, 768, 896, 1024, 1152, 1280, 1304, 1816, 2328

COMPUTE = ("tensor", "scalar", "vector", "gpsimd")


class Op:
    __slots__ = ("eng", "fn", "deps", "is_dma", "semkey", "token", "has_dep", "idx")

    def __init__(self, eng, fn, is_dma=False, semkey=None):
        self.eng = eng
        self.fn = fn
        self.deps = []
        self.is_dma = is_dma
        self.semkey = semkey
        self.token = None
        self.has_dep = False
        self.idx = -1


class Prog:
    def __init__(self, nc, stack):
        self.nc = nc
        self.ops = []
        self.emitted = 0
        self.state = {}
        self.eng_sem = {}
        self.eng_cnt = {}
        for e in COMPUTE:
            self.eng_sem[e] = stack.enter_context(nc.semaphore("sem_" + e))
            self.eng_cnt[e] = 0
        self.dma_sem = {}
        self.dma_cnt = {}
        self.stack = stack
        self.waited = {e: {} for e in ("sync",) + COMPUTE}
        self.last_op = {}
        self.pending_dma = []
        self.last_dma = {}
        self.n_instr = 0

    def add(self, eng, fn, r=(), w=(), dma=False, semkey=None):
        op = Op(eng, fn, is_dma=dma, semkey=semkey)
        op.idx = len(self.ops)
        deps = {}
        for k in r:
            st = self.state.get(k)
            if st is not None:
                for d in st[0]:
                    deps[d.idx] = (d, "raw")
                if isinstance(k, tuple) and k[0] == "ps":
                    for d in st[1]:
                        if d.eng != eng and d.idx not in deps:
                            deps[d.idx] = (d, "rar")
        for k in w:
            st = self.state.get(k)
            if st is not None:
                for d in st[0]:
                    if d.idx not in deps:
                        deps[d.idx] = (d, "waw")
                for d in st[1]:
                    if d.idx not in deps:
                        deps[d.idx] = (d, "war")
        for d, kind in deps.values():
            if (not d.is_dma) and (not dma) and d.eng == eng:
                if eng == "tensor":
                    continue
            op.deps.append(d)
            d.has_dep = True
        for k in r:
            st = self.state.setdefault(k, [[], []])
            st[1].append(op)
        for k in w:
            self.state[k] = [[op], []]
        self.ops.append(op)
        if dma:
            prev = self.last_dma.get(semkey)
            if prev is not None and prev not in op.deps:
                op.deps.append(prev)
            self.last_dma[semkey] = op
            op.has_dep = True
            self.pending_dma.append(op)
        else:
            self.last_op[eng] = op
        return op

    def dma(self, eng, out, in_, r=(), w=(), semkey=None):
        assert semkey is not None
        return self.add(eng, lambda e: e.dma_start(out=out, in_=in_), r=r, w=w, dma=True, semkey=semkey)

    def barrier(self):
        lasts = list(self.last_op.values())
        dmas = list(self.pending_dma)
        self.pending_dma = []
        for e in ("sync",) + COMPUTE:
            op = Op(e, None)
            op.idx = len(self.ops)
            for d in lasts + dmas:
                if d.eng == e and not d.is_dma:
                    continue
                op.deps.append(d)
                d.has_dep = True
            self.ops.append(op)
        self.state = {}

    def flush(self):
        nc = self.nc
        for op in self.ops[self.emitted:]:
            e = getattr(nc, op.eng)
            wt = self.waited[op.eng]
            for d in op.deps:
                sem, val = d.token
                key = id(sem)
                if wt.get(key, 0) >= val:
                    continue
                e.wait_ge(sem, val)
                self.n_instr += 1
                wt[key] = val
            if op.fn is None:
                continue
            ins = op.fn(e)
            self.n_instr += 1
            if op.is_dma:
                sk = op.semkey
                if sk not in self.dma_sem:
                    self.dma_sem[sk] = self.stack.enter_context(nc.semaphore("dsem_%d" % len(self.dma_sem)))
                    self.dma_cnt[sk] = 0
                ins.then_inc(self.dma_sem[sk], 16)
                self.dma_cnt[sk] += 16
                op.token = (self.dma_sem[sk], self.dma_cnt[sk])
            elif op.has_dep:
                self.eng_cnt[op.eng] += 1
                ins.then_inc(self.eng_sem[op.eng], 1)
                op.token = (self.eng_sem[op.eng], self.eng_cnt[op.eng])
        self.emitted = len(self.ops)


INPUT_SPECS = [
    ("x", [T, D], F32), ("mem", [256, D], F32), ("pos_cat", [128, 100], I32), ("freqs", [128, 8], F32),
    ("wg1", [D, FF], F32), ("wu1", [D, FF], F32), ("wd1", [FF, D], F32),
    ("wg2", [D, FF], F32), ("wu2", [D, FF], F32), ("wd2", [FF, D], F32),
    ("w_in", [D, 2840], F32), ("w_out", [D, D], F32),
    ("mwq", [D, 256], F32), ("mwk", [D, 256], F32), ("mwv", [D, 256], F32), ("mwo", [256, D], F32),
    ("cw1k", [2048, 256], F32), ("cw1v", [2048, 256], F32), ("cw2k", [256, 64], F32), ("cw2v", [256, 64], F32),
    ("pekT", [64, 32], F32), ("pevT", [64, 32], F32),
    ("g_ffn1", [128, D], F32), ("g_mix", [128, D], F32), ("g_memx", [128, D], F32), ("g_memkv", [128, D], F32),
    ("g_ffn2", [128, D], F32),
    ("g_q", [128, 64], F32), ("g_kc", [128, 64], F32), ("g_ksw", [128, 128], F32),
    ("g_mq", [128, 64], F32), ("g_mk", [128, 64], F32),
    ("sab", [128, 2], F32), ("maskc", [128, 8 * 128], BF), ("A2", [NOWN, 128, 128], F32), ("A13", [NOWN, 128, 128], F32),
    ("cmsel", [128, 2 * 128], BF), ("wmask", [128, 6 * 128], BF), ("sbmask", [128, 8 * 512], BF),
    ("ident", [128, 128], BF), ("uneg", [128, 128], BF), ("negones", [128, 128], BF),
    ("expd2", [64, T], BF), ("wov", [128, 4 * 128], BF),
]


def build_nc(debug=False, stop_after=99, quick=False):
    nc = bass.Bass("TRN2", target_bir_lowering=False)
    IN = {}
    for name, shape, dt in INPUT_SPECS:
        IN[name] = nc.dram_tensor(name, shape, dt, kind="ExternalInput").ap()
    out_d = nc.dram_tensor("out", [NOWN * 128, D], F32, kind="ExternalOutput").ap()
    DBG = {}

    def dbg_out(name, shape, dt=F32):
        DBG[name] = nc.dram_tensor("dbg_" + name, shape, dt, kind="ExternalOutput").ap()
        return DBG[name]

    x1_all = nc.dram_tensor("x1_all", [T, D], F32).ap()
    x1o_d = nc.dram_tensor("x1o_d", [NOWN * 128, D], F32).ap()
    x3o_d = nc.dram_tensor("x3o_d", [NOWN * 128, D], F32).ap()
    ksT_d = nc.dram_tensor("ksT_d", [128, T], BF).ap()
    kwT_d = nc.dram_tensor("kwT_d", [128, T], BF).ap()
    vs_d = nc.dram_tensor("vs_d", [T, 130], BF).ap()
    vw_d = nc.dram_tensor("vw_d", [T, 130], BF).ap()
    ksbT_d = nc.dram_tensor("ksbT_d", [128, 4, T], BF).ap()
    vsb_d = nc.dram_tensor("vsb_d", [T, 512], BF).ap()
    qsbT_d = nc.dram_tensor("qsbT_d", [128, 4, NOWN * 128], BF).ap()
    qnT_d = nc.dram_tensor("qnT_d", [128, NOWN, 512], BF).ap()
    oT_d = nc.dram_tensor("oT_d", [128, 8, NOWN * 128], BF).ap()

    with ExitStack() as top:
        P = Prog(nc, top)

        _cnt = [0]

        def sbt(st, name, shape, dt):
            _cnt[0] += 1
            return st.enter_context(nc.sbuf_tensor("s%d_%s" % (_cnt[0], name), shape, dt))

        PSALL = top.enter_context(nc.psum_tensor("psall", [128, 8, 512], F32))
        PS = [PSALL[:, i, :] for i in range(8)]

        def psk(b):
            return ("ps", b)

        V = lambda fn, r=(), w=(): P.add("vector", fn, r, w)
        A = lambda fn, r=(), w=(): P.add("scalar", fn, r, w)
        G = lambda fn, r=(), w=(): P.add("gpsimd", fn, r, w)
        TE = lambda fn, r=(), w=(): P.add("tensor", fn, r, w)

        ident = sbt(top, "ident", [128, 128], BF)
        gates = sbt(top, "gates", [128, NOWN, 24], F32)
        cosT = sbt(top, "cosT", [128, 100, 8], F32)
        sinT = sbt(top, "sinT", [128, 100, 8], F32)
        KCT = sbt(top, "KCT", [128, 512], BF)
        WVC = sbt(top, "WVC", [128, 4, 2, 200], BF)
        sab = sbt(top, "sab", [128, 2], F32)
        sabb = sbt(top, "sabb", [128, 2], BF)
        epsb = sbt(top, "epsb", [128, 1], F32)

        P.dma("sync", ident[:], IN["ident"], w=["ident"], semkey="c0")
        P.dma("sync", sab[:], IN["sab"], w=["sab"], semkey="c1")
        V(lambda e: e.tensor_copy(sabb[:], sab[:]), r=["sab"], w=["sabb"])
        V(lambda e: e.memset(epsb[:], 0.0), w=["epsb"])

        def evac_copy(i, out, in_, r, w):
            if i % 2 == 0:
                return A(lambda e: e.copy(out, in_), r=r, w=w)
            return V(lambda e: e.tensor_copy(out, in_), r=r, w=w)

        with ExitStack() as ph:
            posi = sbt(ph, "posi", [128, 100], I32)
            posf = sbt(ph, "posf", [128, 100], F32)
            frq = sbt(ph, "frq", [128, 8], F32)
            ang = sbt(ph, "ang", [128, 100, 8], F32)
            tmpa = sbt(ph, "tmpa", [128, 100, 8], F32)
            ki = sbt(ph, "ki", [128, 100, 8], I32)
            kf = sbt(ph, "kf", [128, 100, 8], F32)
            P.dma("sync", posi[:], IN["pos_cat"], w=["posi"], semkey="c0")
            P.dma("sync", frq[:], IN["freqs"], w=["frq"], semkey="c1")
            V(lambda e: e.tensor_copy(posf[:], posi[:]), r=["posi"], w=["posf"])
            V(lambda e: e.tensor_tensor(ang[:], posf[:].unsqueeze(2).to_broadcast([128, 100, 8]),
                                        frq[:].unsqueeze(1).to_broadcast([128, 100, 8]), op=ALU.mult),
              r=["posf", "frq"], w=["ang"])

            def sin_of(dst, shift):
                V(lambda e: e.tensor_scalar(tmpa[:], ang[:], shift, None, op0=ALU.add), r=["ang"], w=["tmpa"])
                V(lambda e: e.tensor_scalar(kf[:], tmpa[:], 1.0 / TWO_PI, None, op0=ALU.mult), r=["tmpa"], w=["kf"])
                V(lambda e: e.tensor_copy(ki[:], kf[:]), r=["kf"], w=["ki"])
                V(lambda e: e.tensor_copy(kf[:], ki[:]), r=["ki"], w=["kf"])
                V(lambda e: e.scalar_tensor_tensor(tmpa[:], kf[:], -TWO_PI, tmpa[:], op0=ALU.mult, op1=ALU.add),
                  r=["kf", "tmpa"], w=["tmpa"])
                V(lambda e: e.tensor_scalar(kf[:], tmpa[:], 0.0, TWO_PI, op0=ALU.is_lt, op1=ALU.mult), r=["tmpa"], w=["kf"])
                V(lambda e: e.tensor_tensor(tmpa[:], tmpa[:], kf[:], op=ALU.add), r=["tmpa", "kf"], w=["tmpa"])
                V(lambda e: e.tensor_scalar(tmpa[:], tmpa[:], TWO_PI, 0.0, op0=ALU.min, op1=ALU.max), r=["tmpa"], w=["tmpa"])
                A(lambda e: e.activation(dst, tmpa[:], AF.Sin, scale=-1.0, bias=PI), r=["tmpa"], w=["trig"])

            sin_of(sinT[:], 0.0)
            sin_of(cosT[:], PI / 2)
            P.barrier()
            P.flush()

        def cs_all(j):
            return cosT[:, j, :], sinT[:, j, :]

        def cs_own(i):
            return cosT[:, 64 + i, :], sinT[:, 64 + i, :]

        def cs_cmp(c):
            return cosT[:, 96 + c, :], sinT[:, 96 + c, :]

        def rms_stats(st_tiles, src_list, skeys, tag):
            junk, ss, rstd = st_tiles
            n = len(src_list)
            V(lambda e: e.memset(ss[:, 0:n], 0.0), w=[tag + "ss"])
            for j, (src, sk) in enumerate(zip(src_list, skeys)):
                A(lambda e, j=j, src=src: e.activation(junk[:], src, AF.Square, accum_out=ss[:, j:j + 1]),
                  r=[sk, tag + "ss"], w=[tag + "junk", tag + "ss"])
            V(lambda e: e.tensor_scalar(ss[:, 0:n], ss[:, 0:n], 1.0 / D, EPS, op0=ALU.mult, op1=ALU.add),
              r=[tag + "ss"], w=[tag + "ss"])
            A(lambda e: e.activation(rstd[:, 0:n], ss[:, 0:n], AF.Ln), r=[tag + "ss"], w=[tag + "rstd"])
            A(lambda e: e.activation(rstd[:, 0:n], rstd[:, 0:n], AF.Exp, scale=-0.5), r=[tag + "rstd"], w=[tag + "rstd"])

        def head_norm_rope(tm, src, H, gain, cs, dst, rk, wk, tag):
            sq, ss, y, t1, t2 = tm
            A(lambda e: e.activation(sq[:, 0:H, :], src, AF.Square), r=rk, w=[tag + "sq"])
            V(lambda e: e.tensor_reduce(ss[:, 0:H], sq[:, 0:H, :], axis=AX.X, op=ALU.add), r=[tag + "sq"], w=[tag + "ss"])
            V(lambda e: e.tensor_scalar(ss[:, 0:H], ss[:, 0:H], 1.0 / 64, EPS, op0=ALU.mult, op1=ALU.add),
              r=[tag + "ss"], w=[tag + "ss"])
            A(lambda e: e.activation(ss[:, 0:H], ss[:, 0:H], AF.Ln), r=[tag + "ss"], w=[tag + "ss"])
            A(lambda e: e.activation(ss[:, 0:H], ss[:, 0:H], AF.Exp, scale=-0.5), r=[tag + "ss"], w=[tag + "ss"])
            V(lambda e: e.tensor_tensor(y[:, 0:H, :], src, ss[:, 0:H].unsqueeze(2).to_broadcast([128, H, 64]), op=ALU.mult),
              r=list(rk) + [tag + "ss"], w=[tag + "y"])
            V(lambda e: e.tensor_tensor(y[:, 0:H, :], y[:, 0:H, :], gain, op=ALU.mult), r=[tag + "y", "gains"], w=[tag + "y"])
            if cs is None:
                V(lambda e: e.tensor_copy(dst, y[:, 0:H, :]), r=[tag + "y"], w=wk)
                return
            c, s = cs
            cb = c.unsqueeze(1).to_broadcast([128, H, 8])
            sb_ = s.unsqueeze(1).to_broadcast([128, H, 8])
            V(lambda e: e.tensor_tensor(t1[:, 0:H, :], y[:, 0:H, 0:8], cb, op=ALU.mult), r=[tag + "y", "trig"], w=[tag + "t1"])
            V(lambda e: e.tensor_tensor(t2[:, 0:H, :], y[:, 0:H, 8:16], sb_, op=ALU.mult), r=[tag + "y", "trig"], w=[tag + "t2"])
            V(lambda e: e.tensor_tensor(dst[:, :, 0:8], t1[:, 0:H, :], t2[:, 0:H, :], op=ALU.subtract),
              r=[tag + "t1", tag + "t2"], w=wk)
            V(lambda e: e.tensor_tensor(t1[:, 0:H, :], y[:, 0:H, 8:16], cb, op=ALU.mult), r=[tag + "y", "trig"], w=[tag + "t1"])
            V(lambda e: e.tensor_tensor(t2[:, 0:H, :], y[:, 0:H, 0:8], sb_, op=ALU.mult), r=[tag + "y", "trig"], w=[tag + "t2"])
            V(lambda e: e.tensor_tensor(dst[:, :, 8:16], t1[:, 0:H, :], t2[:, 0:H, :], op=ALU.add),
              r=[tag + "t1", tag + "t2"], w=wk)
            V(lambda e: e.tensor_copy(dst[:, :, 16:64], y[:, 0:H, 16:64]), r=[tag + "y"], w=wk)

        class _Dup2:
            def __init__(self, tab, base):
                self.tab, self.base = tab, base

            def __getitem__(self, key):
                _, a, _ = key
                return self.tab[:, self.base + a // 2, :]

        def norm_rope_b(tmb, src, A_, B_, gain, cos3, sin3, dst, rk, wk, tag):
            sq, ss, y, t1, t2 = tmb[0:5]
            H = A_ * B_
            fl = lambda ap: ap
            A(lambda e: e.activation(sq[:, 0:H, :], fl(src), AF.Square), r=rk, w=[tag + "sq"])
            V(lambda e: e.tensor_reduce(ss[:, 0:H], sq[:, 0:H, :], axis=AX.X, op=ALU.add), r=[tag + "sq"], w=[tag + "ss"])
            V(lambda e: e.tensor_scalar(ss[:, 0:H], ss[:, 0:H], 1.0 / 64, EPS, op0=ALU.mult, op1=ALU.add), r=[tag + "ss"], w=[tag + "ss"])
            A(lambda e: e.activation(ss[:, 0:H], ss[:, 0:H], AF.Ln), r=[tag + "ss"], w=[tag + "ss"])
            A(lambda e: e.activation(ss[:, 0:H], ss[:, 0:H], AF.Exp, scale=-0.5), r=[tag + "ss"], w=[tag + "ss"])
            V(lambda e: e.tensor_tensor(y[:, 0:H, :], fl(src), ss[:, 0:H].unsqueeze(2).to_broadcast([128, H, 64]), op=ALU.mult),
              r=list(rk) + [tag + "ss"], w=[tag + "y"])
            V(lambda e: e.tensor_tensor(y[:, 0:H, :], y[:, 0:H, :], gain, op=ALU.mult), r=[tag + "y", "gains"], w=[tag + "y"])
            t3, t4 = tmb[5], tmb[6]
            sl = lambda tt, a: tt[:, a * B_:(a + 1) * B_, :]
            cbs = [cos3[:, a, :].unsqueeze(1).to_broadcast([128, B_, 8]) for a in range(A_)]
            sbs = [sin3[:, a, :].unsqueeze(1).to_broadcast([128, B_, 8]) for a in range(A_)]
            for a in range(A_):
                V(lambda e, a=a: e.tensor_tensor(sl(t1, a), sl(y, a)[:, :, 0:8], cbs[a], op=ALU.mult), r=[tag + "y", "trig"], w=[(tag + "t1", a)])
            for a in range(A_):
                V(lambda e, a=a: e.tensor_tensor(sl(t2, a), sl(y, a)[:, :, 8:16], sbs[a], op=ALU.mult), r=[tag + "y", "trig"], w=[(tag + "t2", a)])
            for a in range(A_):
                V(lambda e, a=a: e.tensor_tensor(sl(t3, a), sl(y, a)[:, :, 8:16], cbs[a], op=ALU.mult), r=[tag + "y", "trig"], w=[(tag + "t3", a)])
            for a in range(A_):
                V(lambda e, a=a: e.tensor_tensor(sl(t4, a), sl(y, a)[:, :, 0:8], sbs[a], op=ALU.mult), r=[tag + "y", "trig"], w=[(tag + "t4", a)])
            for a in range(A_):
                V(lambda e, a=a, da=dst(a): e.tensor_tensor(da[:, :, 0:8], sl(t1, a), sl(t2, a), op=ALU.subtract), r=[(tag + "t1", a), (tag + "t2", a)], w=wk)
            for a in range(A_):
                V(lambda e, a=a, da=dst(a): e.tensor_tensor(da[:, :, 8:16], sl(t3, a), sl(t4, a), op=ALU.add), r=[(tag + "t3", a), (tag + "t4", a)], w=wk)
            for a in range(A_):
                V(lambda e, a=a, da=dst(a): e.tensor_copy(da[:, :, 16:64], sl(y, a)[:, :, 16:64]), r=[tag + "y"], w=wk)

        def load_w_bf(dst3, src2, nk, key):
            h = (nk + 1) // 2
            v = src2.rearrange("(k p) n -> p k n", p=128)
            P.dma("gpsimd", dst3[:, 0:h, :], v[:, 0:h, :], w=[key + "A"], semkey=key + "A")
            if nk > h:
                P.dma("gpsimd", dst3[:, h:nk, :], v[:, h:nk, :], w=[key + "B"], semkey=key + "B")

        def wkey(key, k, nk):
            return key + ("A" if k < (nk + 1) // 2 else "B")

        def ffn_phase(tag, src_d, dst_d, nblk, wg_d, wu_d, wd_d, g_d, wg_pre=None):
            with ExitStack() as ph:
                wg = wg_pre if wg_pre is not None else sbt(ph, "wg", [128, 8, FF], BF)
                wu = sbt(ph, "wu", [128, 8, FF], BF)
                wd = sbt(ph, "wd", [128, NFC, D], BF)
                gbc = sbt(ph, "gbc", [128, D], F32)
                xt = sbt(ph, "xt", [128, 4, D], F32)
                hb = [sbt(ph, "hb%d" % j, [128, D], BF) for j in range(2)]
                hT = sbt(ph, "hT", [128, 8, 512], BF)
                aT = sbt(ph, "aT", [128, NFC, 512], BF)
                sg = [sbt(ph, "sg%d" % j, [128, 512], F32) for j in range(2)]
                junk = sbt(ph, "junk", [128, D], BF)
                ss = sbt(ph, "ss", [128, 4], F32)
                rstd = sbt(ph, "rstd", [128, 4], F32)
                if wg_pre is None:
                    load_w_bf(wg, wg_d, 8, "wg")
                load_w_bf(wu, wu_d, 8, "wu")
                load_w_bf(wd, wd_d, NFC, "wd")
                P.dma("sync", gbc[:], g_d, w=["gbc"], semkey="c0")
                for n in range(nblk):
                    if n == 0:
                        for j in range(4):
                            P.dma("sync", xt[:, j, :], src_d[(n * 4 + j) * 128:(n * 4 + j + 1) * 128, :], w=[("xt", j)], semkey=("xt", j))
                    rms_stats((junk, ss, rstd), [xt[:, j, :] for j in range(4)], [("xt", j) for j in range(4)], "f")
                    for j in range(4):
                        hbj = hb[j % 2]
                        V(lambda e, j=j, hbj=hbj: e.scalar_tensor_tensor(hbj[:], xt[:, j, :], rstd[:, j:j + 1], gbc[:], op0=ALU.mult, op1=ALU.mult),
                          r=[("xt", j), "frstd", "gbc"], w=[("hb", j % 2)])
                        for half in range(2):
                            bank = 4 + half
                            for kk in range(4):
                                kc = half * 4 + kk
                                TE(lambda e, hbj=hbj, kc=kc, kk=kk, bank=bank: e.matmul(PS[bank][:, kk * 128:(kk + 1) * 128], hbj[:, kc * 128:(kc + 1) * 128], ident[:], start=True, stop=True),
                                   r=[("hb", j % 2), "ident"], w=[psk(bank)])
                            evac_copy(j + half, hT[:, half * 4:half * 4 + 4, j * 128:(j + 1) * 128],
                                      PS[bank][:].rearrange("p (k t) -> p k t", k=4),
                                      r=[psk(bank)], w=[("hT", half, j)])
                    hT_keys = [("hT", h_, j_) for h_ in range(2) for j_ in range(4)]
                    for fc in range(NFC):
                        bg, bu = fc % 2, 2 + fc % 2
                        for kc in range(8):
                            TE(lambda e, fc=fc, kc=kc, bg=bg: e.matmul(PS[bg][:], wg[:, kc, fc * 128:(fc + 1) * 128], hT[:, kc, :], start=(kc == 0), stop=(kc == 7)),
                               r=hT_keys + [wkey("wg", kc, 8)], w=[psk(bg)])
                        for kc in range(8):
                            TE(lambda e, fc=fc, kc=kc, bu=bu: e.matmul(PS[bu][:], wu[:, kc, fc * 128:(fc + 1) * 128], hT[:, kc, :], start=(kc == 0), stop=(kc == 7)),
                               r=hT_keys + [wkey("wu", kc, 8)], w=[psk(bu)])
                        sgf = sg[fc % 2]
                        A(lambda e, sgf=sgf, bg=bg: e.activation(sgf[:], PS[bg][:], AF.Silu), r=[psk(bg)], w=[("sg", fc % 2)])
                        V(lambda e, sgf=sgf, bu=bu, fc=fc: e.tensor_tensor(aT[:, fc, :], sgf[:], PS[bu][:], op=ALU.mult),
                          r=[("sg", fc % 2), psk(bu)], w=[("aT", fc)])
                    for j in range(4):
                        for half in range(2):
                            bank = 6 + half
                            for fc in range(NFC):
                                TE(lambda e, fc=fc, j=j, half=half, bank=bank: e.matmul(PS[bank][:], aT[:, fc, j * 128:(j + 1) * 128], wd[:, fc, half * 512:(half + 1) * 512], start=(fc == 0), stop=(fc == NFC - 1)),
                                   r=[("aT", fc), wkey("wd", fc, NFC)], w=[psk(bank)])
                            V(lambda e, j=j, half=half, bank=bank: e.scalar_tensor_tensor(xt[:, j, half * 512:(half + 1) * 512], PS[bank][:], 0.5, xt[:, j, half * 512:(half + 1) * 512], op0=ALU.mult, op1=ALU.add),
                              r=[psk(bank), ("xt", j)], w=[("xt", j)])
                        P.dma("sync", dst_d[(n * 4 + j) * 128:(n * 4 + j + 1) * 128, :], xt[:, j, :], r=[("xt", j)], w=[("dst", n, j)], semkey=("st", j))
                        if n + 1 < nblk:
                            P.dma("sync", xt[:, j, :], src_d[((n + 1) * 4 + j) * 128:((n + 1) * 4 + j + 1) * 128, :], w=[("xt", j)], semkey=("xt", j))
                P.barrier()
                P.flush()

        _dn = [0]

        def dump(name, src, shape, dt=F32, r=()):
            if not debug:
                return
            d = dbg_out(name, shape, dt)
            _dn[0] += 1
            P.dma("sync", d, src, r=r, semkey=("dbg", _dn[0] % 4))

        def dump_done():
            if debug:
                P.barrier()
                P.flush()

        if not quick:
            ffn_phase("f1", IN["x"], x1_all, 16, IN["wg1"], IN["wu1"], IN["wd1"], IN["g_ffn1"])
        else:
            P.dma("sync", x1_all[0:2048, :], IN["x"][0:2048, :], semkey="c0")
            P.barrier()
            P.flush()
        dump("x1", x1_all[0:1024, :], [1024, D])
        dump("x1b", x1_all[T - 512:T, :], [512, D])
        dump("cosT", cosT[:].rearrange("p a b -> p (a b)"), [128, 800])
        dump("sinT", sinT[:].rearrange("p a b -> p (a b)"), [128, 800])
        dump_done()
        if stop_after <= 1:
            return nc, DBG, None

        with ExitStack() as phkv:
          kcT = sbt(phkv, "kcT", [128, T + 16], BF)
          vcT = sbt(phkv, "vcT", [128, T + 16], BF)
          gq = sbt(phkv, "gq", [128, 64], F32)
          gkc = sbt(phkv, "gkc", [128, 64], F32)
          gksw = sbt(phkv, "gksw", [128, 2, 64], F32)
          tm = (sbt(phkv, "nsq", [128, 8, 64], F32), sbt(phkv, "nss", [128, 8], F32), sbt(phkv, "ny", [128, 8, 64], F32),
                sbt(phkv, "nt1", [128, 8, 8], F32), sbt(phkv, "nt2", [128, 8, 8], F32))
          with ExitStack() as ph:
            win = sbt(ph, "win", [128, 8, 2840], BF)
            gbc = sbt(ph, "gbc", [128, D], F32)
            xt = sbt(ph, "xt", [128, 4, D], F32)
            hb = [sbt(ph, "hb%d" % j, [128, D], BF) for j in range(4)]
            hbo = [sbt(ph, "hbo%d" % j, [128, D], BF) for j in range(2)]
            hT = sbt(ph, "hT", [128, 8, 512], BF)
            hoT = sbt(ph, "hoT", [128, 8, 256], BF)
            x1o = [sbt(ph, "x1o%d" % j, [128, D], F32) for j in range(2)]
            junk = sbt(ph, "junk", [128, D], BF)
            ss4 = sbt(ph, "ss4", [128, 4], F32)
            rstd = sbt(ph, "rstd", [128, 4], F32)
            kstg = sbt(ph, "kstg", [128, 2, 2, 64], BF)
            kTs = sbt(ph, "kTs", [128, 2, 512], BF)
            vstg = sbt(ph, "vstg", [128, 4, 2, 2, 65], BF)
            ksbs = sbt(ph, "ksbs", [128, 4, 512], BF)
            vsbs = sbt(ph, "vsbs", [128, 4, 512], BF)
            qstg = sbt(ph, "qstg", [128, 2, 4, 2, 64], BF)
            kraw = sbt(ph, "kraw", [128, 4, 4, 64], F32)
            kst16 = sbt(ph, "kst16", [128, 4, 4, 64], BF)
            qraw = sbt(ph, "qraw", [128, 2, 8, 64], F32)
            gk16 = sbt(ph, "gk16", [128, 4, 4, 64], F32)
            gq16 = sbt(ph, "gq16", [128, 16, 64], F32)
            graw = sbt(ph, "graw", [128, 2, 24], F32)
            tmb = (sbt(ph, "bsq", [128, 16, 64], F32), sbt(ph, "bss", [128, 16], F32), sbt(ph, "by", [128, 16, 64], F32),
                   sbt(ph, "bt1", [128, 16, 8], F32), sbt(ph, "bt2", [128, 16, 8], F32),
                   sbt(ph, "bt3", [128, 16, 8], F32), sbt(ph, "bt4", [128, 16, 8], F32))
            qnTs = sbt(ph, "qnTs", [128, 2, 512], BF)
            qsbs = sbt(ph, "qsbs", [128, 4, 256], BF)
            load_w_bf(win, IN["w_in"], 8, "win")
            P.dma("sync", gbc[:], IN["g_mix"], w=["gbc"], semkey="c0")
            P.dma("sync", gq[:], IN["g_q"], w=["gains"], semkey="c1")
            P.dma("sync", gkc[:], IN["g_kc"], w=["gains"], semkey="c2")
            P.dma("sync", gksw[:].rearrange("p a d -> p (a d)"), IN["g_ksw"], w=["gains"], semkey="c3")
            V(lambda e: e.memset(vstg[:], 1.0), w=["vstg"])
            for j_ in range(0 if os.environ.get("NOG16") else 4):
                for gi_ in range(2):
                    V(lambda e, j_=j_, gi_=gi_: e.tensor_copy(gk16[:, j_, 2 * gi_:2 * gi_ + 2, :], gksw[:, gi_, :].unsqueeze(1).to_broadcast([128, 2, 64])), r=["gains"], w=["gk16"])
            V(lambda e: e.tensor_copy(gq16[:], gq[:].unsqueeze(1).to_broadcast([128, 16, 64])), r=["gains"], w=["gq16"])
            V(lambda e: e.memset(kcT[:, T:T + 16], 0.0), w=["kcTpad"])
            V(lambda e: e.memset(vcT[:, T:T + 16], 0.0), w=["vcTpad"])
            if quick:
                V(lambda e: e.memset(kcT[:, 2048:T], 0.0), w=["kcTq"])
                V(lambda e: e.memset(vcT[:, 2048:T], 0.0), w=["vcTq"])
            win_keys = ["winA", "winB"]
            nblk1b = int(quick) if quick else 16
            for n in range(nblk1b):
                if n == 0:
                    for j in range(4):
                        P.dma("sync", xt[:, j, :], x1_all[(n * 4 + j) * 128:(n * 4 + j + 1) * 128, :], w=[("xt", j)], semkey=("xt", j))
                rms_stats((junk, ss4, rstd), [xt[:, j, :] for j in range(4)], [("xt", j) for j in range(4)], "m")
                for j in range(4):
                    V(lambda e, j=j: e.scalar_tensor_tensor(hb[j][:], xt[:, j, :], rstd[:, j:j + 1], gbc[:], op0=ALU.mult, op1=ALU.mult),
                      r=[("xt", j), "mrstd", "gbc"], w=[("hb", j)])
                    for half in range(2):
                        bank = 4 + half
                        for kk in range(4):
                            kc = half * 4 + kk
                            TE(lambda e, j=j, kc=kc, kk=kk, bank=bank: e.matmul(PS[bank][:, kk * 128:(kk + 1) * 128], hb[j][:, kc * 128:(kc + 1) * 128], ident[:], start=True, stop=True),
                               r=[("hb", j), "ident"], w=[psk(bank)])
                        evac_copy(j + half, hT[:, half * 4:half * 4 + 4, j * 128:(j + 1) * 128],
                                  PS[bank][:].rearrange("p (k t) -> p k t", k=4), r=[psk(bank)], w=[("hT", half, j)])
                hT_keys = [("hT", h_, j_) for h_ in range(2) for j_ in range(4)]
                for q in range(2):
                    i_own = n * 2 + q
                    V(lambda e, q=q: e.tensor_scalar(x1o[q][:], xt[:, 2 * q, :], sab[:, 0:1], None, op0=ALU.mult),
                      r=[("xt", 2 * q), "sab"], w=[("x1o", q)])
                    V(lambda e, q=q: e.scalar_tensor_tensor(x1o[q][:], xt[:, 2 * q + 1, :], sab[:, 1:2], x1o[q][:], op0=ALU.mult, op1=ALU.add),
                      r=[("xt", 2 * q + 1), "sab", ("x1o", q)], w=[("x1o", q)])
                    P.dma("sync", x1o_d[i_own * 128:(i_own + 1) * 128, :], x1o[q][:], r=[("x1o", q)], w=[("x1od", i_own)], semkey=("x1o", q))
                    V(lambda e, q=q: e.tensor_scalar(hbo[q][:], hb[2 * q][:], sab[:, 0:1], None, op0=ALU.mult),
                      r=[("hb", 2 * q), "sab"], w=[("hbo", q)])
                    V(lambda e, q=q: e.scalar_tensor_tensor(hbo[q][:], hb[2 * q + 1][:], sab[:, 1:2], hbo[q][:], op0=ALU.mult, op1=ALU.add),
                      r=[("hb", 2 * q + 1), "sab", ("hbo", q)], w=[("hbo", q)])
                    for half in range(2):
                        bank = 6 + half
                        for kk in range(4):
                            kc = half * 4 + kk
                            TE(lambda e, q=q, kc=kc, kk=kk, bank=bank: e.matmul(PS[bank][:, kk * 128:(kk + 1) * 128], hbo[q][:, kc * 128:(kc + 1) * 128], ident[:], start=True, stop=True),
                               r=[("hbo", q), "ident"], w=[psk(bank)])
                        evac_copy(q + half, hoT[:, half * 4:half * 4 + 4, q * 128:(q + 1) * 128],
                                  PS[bank][:].rearrange("p (k t) -> p k t", k=4), r=[psk(bank)], w=[("hoT", half, q)])
                hoT_keys = [("hoT", h_, q_) for h_ in range(2) for q_ in range(2)]
                if n + 1 < nblk1b:
                    for j in range(4):
                        P.dma("sync", xt[:, j, :], x1_all[((n + 1) * 4 + j) * 128:((n + 1) * 4 + j + 1) * 128, :], w=[("xt", j)], semkey=("xt", j))
                tok0 = n * 512
                for which, col, dstT in ((0, C_KC, kcT), (1, C_VC, vcT)):
                    bank = which
                    for kc in range(8):
                        TE(lambda e, kc=kc, col=col, bank=bank: e.matmul(PS[bank][:], win[:, kc, col:col + 128], hT[:, kc, :], start=(kc == 0), stop=(kc == 7)),
                           r=hT_keys + win_keys, w=[psk(bank)])
                    evac_copy(which, dstT[:, tok0:tok0 + 512], PS[bank][:], r=[psk(bank)], w=[("cT", which, n)])
                for pr in range(4):
                    bank = 2 + pr % 2
                    for kc in range(8):
                        TE(lambda e, kc=kc, pr=pr, bank=bank: e.matmul(PS[bank][:], win[:, kc, C_KSB + pr * 128:C_KSB + (pr + 1) * 128], hT[:, kc, :], start=(kc == 0), stop=(kc == 7)),
                           r=hT_keys + win_keys, w=[psk(bank)])
                    evac_copy(pr, ksbs[:, pr, :], PS[bank][:], r=[psk(bank)], w=["ksbs"])
                P.dma("sync", ksbT_d[:, :, tok0:tok0 + 512], ksbs[:], r=["ksbs"], w=[("ksbd", n)], semkey="ksbs")
                for j in range(4):
                    bank = j % 2
                    for kc in range(8):
                        TE(lambda e, kc=kc, j=j, bank=bank: e.matmul(PS[bank][:], hT[:, kc, j * 128:(j + 1) * 128], win[:, kc, C_KS:C_KS + 512], start=(kc == 0), stop=(kc == 7)),
                           r=hT_keys + win_keys, w=[psk(bank)])
                    psv = PS[bank][:].rearrange("p (a h d) -> p a h d", a=4, h=2)
                    A(lambda e, j=j, psv=psv: e.copy(vstg[:, j, 0, :, 0:64], psv[:, 1, :, :]), r=[psk(bank)], w=["vstg"])
                    A(lambda e, j=j, psv=psv: e.copy(vstg[:, j, 1, :, 0:64], psv[:, 3, :, :]), r=[psk(bank)], w=["vstg"])
                    if not os.environ.get("NOKRAW"):
                        V(lambda e, j=j, bank=bank: e.tensor_copy(kraw[:, j, 0:2, :].rearrange("p h d -> p (h d)"), PS[bank][:, 0:128]), r=[psk(bank)], w=[("kraw", j)])
                        V(lambda e, j=j, bank=bank: e.tensor_copy(kraw[:, j, 2:4, :].rearrange("p h d -> p (h d)"), PS[bank][:, 256:384]), r=[psk(bank), ("kraw", j)], w=[("kraw", j)])
                    bank2 = 6 + j % 2
                    for kc in range(8):
                        TE(lambda e, kc=kc, j=j, bank2=bank2: e.matmul(PS[bank2][:], hT[:, kc, j * 128:(j + 1) * 128], win[:, kc, C_VSB:C_VSB + 512], start=(kc == 0), stop=(kc == 7)),
                           r=hT_keys + win_keys, w=[psk(bank2)])
                    evac_copy(j, vsbs[:, j, :], PS[bank2][:], r=[psk(bank2)], w=["vsbs"])
                if os.environ.get("KSKIP"):
                    continue
                norm_rope_b(tmb, kraw[:].rearrange("p a b d -> p (a b) d"), 4, 4, gk16[:].rearrange("p a b d -> p (a b) d"), cosT[:, n * 4:n * 4 + 4, :], sinT[:, n * 4:n * 4 + 4, :],
                            lambda a: kst16[:, a, :, :], [("kraw", j_) for j_ in range(4)], ["kst16"], "b")
                for j in range(4):
                    for gi in range(2):
                        TE(lambda e, gi=gi, j=j: e.matmul(PS[2 + gi][:, j * 128:(j + 1) * 128], kst16[:, j, 2 * gi:2 * gi + 2, :].rearrange("p h d -> p (h d)"), ident[:], start=True, stop=True),
                           r=["kst16", "ident"], w=[psk(2 + gi)])
                for gi in range(2):
                    evac_copy(gi, kTs[:, gi, :], PS[2 + gi][:], r=[psk(2 + gi)], w=["kTs"])
                P.dma("sync", ksT_d[:, tok0:tok0 + 512], kTs[:, 0, :], r=["kTs"], w=[("ksd", n)], semkey="kTs")
                P.dma("sync", kwT_d[:, tok0:tok0 + 512], kTs[:, 1, :], r=["kTs"], w=[("kwd", n)], semkey="kTs2")
                P.dma("sync", vs_d[tok0:tok0 + 512, :].rearrange("(j p) c -> p j c", p=128), vstg[:, :, 0, :, :].rearrange("p j h c -> p j (h c)"),
                      r=["vstg"], w=[("vsd", n)], semkey="vstg")
                P.dma("sync", vw_d[tok0:tok0 + 512, :].rearrange("(j p) c -> p j c", p=128), vstg[:, :, 1, :, :].rearrange("p j h c -> p j (h c)"),
                      r=["vstg"], w=[("vwd", n)], semkey="vstg2")
                P.dma("sync", vsb_d[tok0:tok0 + 512, :].rearrange("(j p) c -> p j c", p=128), vsbs[:], r=["vsbs"], w=[("vsbd", n)], semkey="vsbs")
                if os.environ.get("QSKIP"):
                    continue
                for q in range(2):
                    bank = 4 + q
                    for kc in range(8):
                        TE(lambda e, kc=kc, q=q, bank=bank: e.matmul(PS[bank][:], hoT[:, kc, q * 128:(q + 1) * 128], win[:, kc, C_QN:C_QN + 512], start=(kc == 0), stop=(kc == 7)),
                           r=hoT_keys + win_keys, w=[psk(bank)])
                    A(lambda e, q=q, bank=bank: e.copy(qraw[:, q, :, :].rearrange("p h d -> p (h d)"), PS[bank][:]), r=[psk(bank)], w=[("qraw", q)])
                    for kc in range(8):
                        TE(lambda e, kc=kc, q=q: e.matmul(PS[1][:, q * 32:q * 32 + 24], hoT[:, kc, q * 128:(q + 1) * 128], win[:, kc, C_GT:C_GT + 24], start=(kc == 0), stop=(kc == 7)),
                           r=hoT_keys + win_keys, w=[psk(1)])
                cq = cosT[:, 64 + n * 2:64 + n * 2 + 2, :].unsqueeze(2).to_broadcast([128, 2, 2, 8]).rearrange("p q k d -> p (q k) d") if False else None
                norm_rope_b(tmb, qraw[:].rearrange("p q h d -> p (q h) d"), 4, 4, gq16[:], _Dup2(cosT, 64 + n * 2), _Dup2(sinT, 64 + n * 2),
                            lambda a: qstg[:, a // 2, :, a % 2, :], [("qraw", 0), ("qraw", 1)], ["qstg"], "b")
                for q in range(2):
                    for g in range(4):
                        TE(lambda e, g=g, q=q: e.matmul(PS[0][:, g * 128:(g + 1) * 128], qstg[:, q, g, :, :].rearrange("p k d -> p (k d)"), ident[:], start=True, stop=True),
                           r=["qstg", "ident"], w=[psk(0)])
                    evac_copy(q, qnTs[:, q, :], PS[0][:], r=[psk(0)], w=["qnTs"])
                for q in range(2):
                    A(lambda e, q=q: e.activation(graw[:, q, :], PS[1][:, q * 32:q * 32 + 24], AF.Exp, scale=-1.0), r=[psk(1)], w=["graw"])
                V(lambda e: e.tensor_scalar(graw[:], graw[:], 1.0, None, op0=ALU.add), r=["graw"], w=["graw"])
                V(lambda e, n=n: e.reciprocal(gates[:, n * 2:n * 2 + 2, :], graw[:]), r=["graw"], w=["gates"])
                P.dma("sync", qnT_d[:, n * 2:n * 2 + 2, :], qnTs[:], r=["qnTs"], w=[("qnd", n)], semkey="qnTs")
                for pr in range(4):
                    bank = 2 + pr % 2
                    for kc in range(8):
                        TE(lambda e, kc=kc, pr=pr, bank=bank: e.matmul(PS[bank][:, 0:256], win[:, kc, C_QS + pr * 128:C_QS + (pr + 1) * 128], hoT[:, kc, :], start=(kc == 0), stop=(kc == 7)),
                           r=hoT_keys + win_keys, w=[psk(bank)])
                    V(lambda e, pr=pr, bank=bank: e.tensor_scalar(qsbs[:, pr, :], PS[bank][:, 0:256], 0.125, None, op0=ALU.mult), r=[psk(bank)], w=["qsbs"])
                P.dma("sync", qsbT_d[:, :, n * 256:(n + 1) * 256], qsbs[:], r=["qsbs"], w=[("qsbd", n)], semkey="qsbs")
            P.barrier()
            P.flush()

          if True:
            with ExitStack() as pc:
                w1 = sbt(pc, "w1", [128, 32, 256], BF)
                w2 = sbt(pc, "w2", [128, 2, 64], BF)
                peT = sbt(pc, "peT", [128, 32], F32)
                kpe = sbt(pc, "kpe", [128, 32, 512], BF)
                gel = sbt(pc, "gel", [128, 2, 2, 512], BF)
                gx2 = sbt(pc, "gx2", [128, 512], F32)
                gu = sbt(pc, "gu", [128, 512], F32)
                cstg = sbt(pc, "cstg", [128, 2, 64], BF)
                wovs = sbt(pc, "wovs", [128, 4, 128], BF)
                P.dma("sync", wovs[:].rearrange("p c j -> p (c j)"), IN["wov"], w=["wovs"], semkey="c0")
                V(lambda e: e.memset(WVC[:], 1.0), w=["WVC"])
                for hk in range(2):
                    V(lambda e, hk=hk: e.tensor_copy(WVC[:, :, hk, 0:128], wovs[:]), r=["wovs", "WVC"], w=["WVC"])
                for which, (w1_d, w2_d, pe_d, srcT) in enumerate(((IN["cw1k"], IN["cw2k"], IN["pekT"], kcT), (IN["cw1v"], IN["cw2v"], IN["pevT"], vcT))):
                    w1v = w1_d.rearrange("(r d) n -> d r n", d=64)
                    P.dma("gpsimd", w1[0:64, :, :], w1v, w=["w1a"], semkey="w1a")
                    P.dma("gpsimd", w1[64:128, :, :], w1v, w=["w1b"], semkey="w1b")
                    P.dma("gpsimd", w2[:], w2_d.rearrange("(c p) d -> p c d", p=128), w=["w2"], semkey="w2")
                    P.dma("sync", peT[0:64, :], pe_d, w=["peTa"], semkey="c1")
                    P.dma("sync", peT[64:128, :], pe_d, w=["peTb"], semkey="c2")
                    src_win = bass.AP(srcT, 0, [[T + 16, 128], [1, 32], [16, 512]])
                    V(lambda e, src_win=src_win: e.tensor_tensor(kpe[:], src_win, peT[:].unsqueeze(2).to_broadcast([128, 32, 512]), op=ALU.add),
                      r=["peTa", "peTb"], w=["kpe"])
                    for hk in range(2):
                        lo, hi = hk * 64, hk * 64 + 64
                        for hc in range(2):
                            bank = (hk * 2 + hc) % 4
                            for r_ in range(32):
                                TE(lambda e, r_=r_, hc=hc, lo=lo, hi=hi, bank=bank: e.matmul(PS[bank][:], w1[lo:hi, r_, hc * 128:(hc + 1) * 128], kpe[lo:hi, r_, :], start=(r_ == 0), stop=(r_ == 31)),
                                   r=["w1a", "w1b", "kpe"], w=[psk(bank)])
                            A(lambda e, bank=bank: e.activation(gx2[:], PS[bank][:], AF.Square), r=[psk(bank)], w=["gx2"])
                            V(lambda e: e.tensor_scalar(gx2[:], gx2[:], 0.044715, 1.0, op0=ALU.mult, op1=ALU.add), r=["gx2"], w=["gx2"])
                            V(lambda e, bank=bank: e.tensor_tensor(gu[:], gx2[:], PS[bank][:], op=ALU.mult), r=["gx2", psk(bank)], w=["gu"])
                            A(lambda e: e.activation(gu[:], gu[:], AF.Tanh, scale=0.7978845608028654), r=["gu"], w=["gu"])
                            V(lambda e: e.tensor_scalar(gu[:], gu[:], 0.5, 0.5, op0=ALU.mult, op1=ALU.add), r=["gu"], w=["gu"])
                            V(lambda e, hk=hk, hc=hc, bank=bank: e.tensor_tensor(gel[:, hk, hc, :], gu[:], PS[bank][:], op=ALU.mult),
                              r=["gu", psk(bank)], w=[("gel", hk, hc)])
                    for cc in range(4):
                        bank = 4 + cc % 2
                        for hk in range(2):
                            for hc in range(2):
                                TE(lambda e, cc=cc, hk=hk, hc=hc, bank=bank: e.matmul(PS[bank][:, hk * 64:(hk + 1) * 64], gel[:, hk, hc, cc * 128:(cc + 1) * 128], w2[:, hc, :], start=(hc == 0), stop=(hc == 1)),
                                   r=[("gel", hk, hc), "w2"], w=[psk(bank)])
                        pv = PS[bank][:, 0:128].rearrange("p (h d) -> p h d", h=2)
                        if which == 0:
                            head_norm_rope(tm, pv, 2, gkc[:].unsqueeze(1).to_broadcast([128, 2, 64]), cs_cmp(cc), cstg[:], [psk(bank)], ["cstg"], "k")
                            TE(lambda e, cc=cc: e.matmul(PS[6][:, cc * 128:(cc + 1) * 128], cstg[:].rearrange("p h d -> p (h d)"), ident[:], start=True, stop=True),
                               r=["cstg", "ident"], w=[psk(6)])
                        else:
                            V(lambda e, cc=cc, pv=pv: e.tensor_copy(WVC[:, cc, :, 128:192], pv), r=[psk(bank), "WVC"], w=["WVC"])
                    if which == 0:
                        V(lambda e: e.tensor_copy(KCT[:], PS[6][:]), r=[psk(6)], w=["KCT"])
                P.barrier()
                P.flush()
        dump("ksT", ksT_d, [128, T], BF)
        dump("kwT", kwT_d, [128, T], BF)
        dump("vs", vs_d, [T, 130], BF)
        dump("vw", vw_d, [T, 130], BF)
        dump("ksbT", ksbT_d.rearrange("p a t -> p (a t)"), [128, 4 * T], BF)
        dump("vsb", vsb_d, [T, 512], BF)
        dump("qsbT", qsbT_d.rearrange("p a t -> p (a t)"), [128, 4 * NOWN * 128], BF)
        dump("qnT", qnT_d.rearrange("p a t -> p (a t)"), [128, NOWN * 512], BF)
        dump("x1o", x1o_d[0:512, :], [512, D])
        dump("gates", gates[:].rearrange("p a b -> p (a b)"), [128, NOWN * 24])
        dump("KCT", KCT[:], [128, 512], BF)
        dump("WVC", WVC[:].rearrange("p a b c -> p (a b c)"), [128, 1600], BF)
        dump_done()
        if stop_after <= 2:
            return nc, DBG, None

        with ExitStack() as ph:
            KSA = [sbt(ph, "KSA0", [128, T], BF), sbt(ph, "KSA1", [128, T], BF)]
            RQ = sbt(ph, "RQ", [128, 2, 512], BF)
            KWT = sbt(ph, "KWT", [128, T], BF)
            VS = sbt(ph, "VS", [128, NT, 130], BF)
            VW = sbt(ph, "VW", [128, NT, 130], BF)
            QNT = sbt(ph, "QNT", [128, NOWN, 512], BF)
            mkc = sbt(ph, "mkc", [128, 8, 128], BF)
            cms = sbt(ph, "cms", [128, 2, 128], BF)
            wms = sbt(ph, "wms", [128, 6, 128], BF)
            cms4 = sbt(ph, "cms4", [128, 2, 4, 128], BF)
            wms4 = sbt(ph, "wms4", [128, 6, 4, 128], BF)
            A2t = [sbt(ph, "A2t%d" % k, [128, 128], F32) for k in range(2)]
            A13t = [sbt(ph, "A13t%d" % k, [128, 128], F32) for k in range(2)]
            Eb = [sbt(ph, "Eb%d" % k, [128, 512], BF) for k in range(3)]
            imp = sbt(ph, "imp", [128, 128], F32)
            sc = sbt(ph, "sc", [128, 128], F32)
            sc2 = sbt(ph, "sc2", [128, 128], F32)
            m8 = sbt(ph, "m8", [128, 8], F32)
            m8b = sbt(ph, "m8b", [128, 8], F32)
            nsel = sbt(ph, "nsel", [128, 128], BF)
            nselT = sbt(ph, "nselT", [128, 4, 128], BF)
            den = sbt(ph, "den", [128, 16], F32)
            oacc = sbt(ph, "oacc", [128, 2, 4, 64], F32)
            obf = sbt(ph, "obf", [128, 512], BF)
            oTs = sbt(ph, "oTs", [128, 4, 128], BF)
            P.dma("sync", QNT[:], qnT_d, w=["QNT"], semkey=("l", 4))
            P.dma("sync", mkc[:].rearrange("p a b -> p (a b)"), IN["maskc"], w=["mkc"], semkey=("l", 6))
            P.dma("sync", cms[:].rearrange("p a b -> p (a b)"), IN["cmsel"], w=["cms"], semkey=("l", 7))
            P.dma("sync", wms[:].rearrange("p a b -> p (a b)"), IN["wmask"], w=["wms"], semkey=("l", 8))
            P.dma("sync", KSA[0][0:64, :], ksT_d[0:64, :], w=[("KSA", 0)], semkey=("l", 0))
            P.dma("sync", KSA[0][64:128, :], IN["expd2"], w=[("KSA", 0)], semkey=("l", 0))
            P.dma("sync", KWT[:], kwT_d, w=["KWT"], semkey=("l", 1))
            for q8 in range(8):
                P.dma("sync", VS[:, q8 * 8:(q8 + 1) * 8, :], vs_d[q8 * 1024:(q8 + 1) * 1024, :].rearrange("(j p) c -> p j c", p=128), w=[("VS", q8)], semkey=("lv", q8 % 4))
                P.dma("sync", VW[:, q8 * 8:(q8 + 1) * 8, :], vw_d[q8 * 1024:(q8 + 1) * 1024, :].rearrange("(j p) c -> p j c", p=128), w=[("VW", q8)], semkey=("lw", q8 % 4))
                if q8 == 0:
                    P.dma("sync", KSA[1][64:128, :], ksT_d[64:128, :], w=[("KSA", 1)], semkey=("l", 5))
                    P.dma("sync", KSA[1][0:64, :], IN["expd2"], w=[("KSA", 1)], semkey=("l", 5))
            V(lambda e: e.tensor_copy(cms4[:], cms[:].unsqueeze(2).to_broadcast([128, 2, 4, 128])), r=["cms"], w=["cms4"])
            V(lambda e: e.tensor_copy(wms4[:], wms[:].unsqueeze(2).to_broadcast([128, 6, 4, 128])), r=["wms"], w=["wms4"])
            PSC = [PS[2][:].rearrange("p (g c) -> p g c", g=2), PS[3][:].rearrange("p (g c) -> p g c", g=2)]

            def pcmp(g):
                return PSC[g // 2][:, g % 2, :]

            PSS = PS[4][:].rearrange("p (g c) -> p g c", g=4)
            PSW = PS[5][:].rearrange("p (g c) -> p g c", g=4)
            ecnt = [0]

            def stream(tiles, q_rhs, pv_rhs_fn, pacc_fn, ncol, tag, firsts=(0,)):
                n = len(tiles)

                SB3 = (0, 1, 6)

                def emit_score(t):
                    b = SB3[t % 3]
                    mm = tiles[t]["score"]
                    for k_, (lh, rh, keys) in enumerate(mm):
                        TE(lambda e, lh=lh, rh=rh, b=b, k_=k_, L=len(mm): e.matmul(PS[b][:], lh, rh, start=(k_ == 0), stop=(k_ == L - 1)),
                           r=keys, w=[psk(b)])

                emit_score(0)
                if n > 1:
                    emit_score(1)
                for t in range(n):
                    if t + 2 < n:
                        emit_score(t + 2)
                    b = SB3[t % 3]
                    ei = ecnt[0] % 3
                    ecnt[0] += 1
                    E = Eb[ei]
                    A(lambda e, E=E, b=b: e.activation(E[:], PS[b][:], AF.Exp, scale=0.125), r=[psk(b)], w=[("E", ei)])
                    mk = tiles[t].get("mask")
                    if mk is not None:
                        V(lambda e, E=E, mk=mk: e.tensor_tensor(E[:].rearrange("p (g q) -> p g q", g=4), E[:].rearrange("p (g q) -> p g q", g=4), mk, op=ALU.mult),
                          r=[("E", ei), "mkc"], w=[("E", ei)])
                    rhs, rkeys = pv_rhs_fn(t)
                    for g in range(4):
                        TE(lambda e, E=E, g=g, rhs=rhs, t=t: e.matmul(pacc_fn(g)[:, 0:ncol], E[:, g * 128:(g + 1) * 128], rhs, start=(t == 0 and g in firsts), stop=(t == n - 1)),
                           r=[("E", ei)] + rkeys, w=[(tag, g)])

            for i in range(1 if quick else NOWN):
                k2 = i % 2
                P.dma("sync", A2t[k2][:], IN["A2"][i], w=[("A2t", k2)], semkey=("A2t", k2))
                P.dma("sync", A13t[k2][:], IN["A13"][i], w=[("A13t", k2)], semkey=("A13t", k2))
                for hk in range(2):
                    lo, hi = hk * 64, hk * 64 + 64
                    qr = QNT[lo:hi, i, :]
                    ncc = i // 8 + 1
                    tiles = []
                    for cc in range(ncc):
                        tiles.append({"score": [(KCT[lo:hi, cc * 128:(cc + 1) * 128], qr, ["KCT", "QNT"])],
                                      "mask": (mkc[:, i % 8, :].unsqueeze(1).to_broadcast([128, 4, 128]) if cc == ncc - 1 else None)})
                    stream(tiles, qr, lambda t, hk=hk: (WVC[:, t, hk, 0:193], ["WVC"]), pcmp, 193, "pc", firsts=(0, 2))
                    for g in range(4):
                        pc = pcmp(g)
                        gi = (hk * 4 + g) * 3
                        V(lambda e, pc=pc, g=g: e.tensor_scalar(den[:, g:g + 1], pc[:, 192:193], 1e-30, None, op0=ALU.max), r=[("pc", g)], w=[("den", g)])
                        V(lambda e, g=g: e.reciprocal(den[:, g:g + 1], den[:, g:g + 1]), r=[("den", g)], w=[("den", g)])
                        if g == 0:
                            V(lambda e, pc=pc, g=g: e.tensor_scalar(imp[:], pc[:, 0:128], den[:, g:g + 1], None, op0=ALU.mult), r=[("pc", g), ("den", g)], w=["imp"])
                        else:
                            V(lambda e, pc=pc, g=g: e.scalar_tensor_tensor(imp[:], pc[:, 0:128], den[:, g:g + 1], imp[:], op0=ALU.mult, op1=ALU.add),
                              r=[("pc", g), ("den", g), "imp"], w=["imp"])
                        V(lambda e, g=g, gi=gi, i=i: e.tensor_tensor(den[:, 4 + g:5 + g], den[:, g:g + 1], gates[:, i, gi:gi + 1], op=ALU.mult), r=[("den", g), "gates"], w=[("den", 4 + g)])
                        V(lambda e, pc=pc, g=g, hk=hk: e.tensor_scalar(oacc[:, hk, g, :], pc[:, 128:192], den[:, 4 + g:5 + g], None, op0=ALU.mult),
                          r=[("pc", g), ("den", 4 + g)], w=[("oacc", hk, g)])
                    V(lambda e, k2=k2: e.tensor_tensor(sc[:], imp[:], A2t[k2][:], op=ALU.mult), r=["imp", ("A2t", k2)], w=["sc"])
                    V(lambda e, k2=k2: e.tensor_tensor(sc[:], sc[:], A13t[k2][:], op=ALU.add), r=["sc", ("A13t", k2)], w=["sc"])
                    V(lambda e: e.max(out=m8[:], in_=sc[:]), r=["sc"], w=["m8"])
                    V(lambda e: e.match_replace(out=sc2[:], in_to_replace=m8[:], in_values=sc[:], imm_value=-2.0), r=["m8", "sc"], w=["sc2"])
                    V(lambda e: e.max(out=m8b[:], in_=sc2[:]), r=["sc2"], w=["m8b"])
                    V(lambda e: e.tensor_scalar(sc2[:], sc[:], m8b[:, 7:8], None, op0=ALU.is_ge), r=["sc", "m8b"], w=["sc2"])
                    V(lambda e: e.tensor_scalar(nsel[:], sc2[:], -1.0, BIG, op0=ALU.add, op1=ALU.mult), r=["sc2"], w=["nsel"])
                    tiles = []
                    kts = []
                    for rel in range(6):
                        kt = 2 * i - 4 + rel
                        if kt < 0:
                            continue
                        kts.append(kt)
                        tiles.append({"score": [(KWT[lo:hi, kt * 128:(kt + 1) * 128], qr, ["KWT", "QNT"]),
                                                (ident[:], wms4[:, rel, :, :].rearrange("p g q -> p (g q)"), ["ident", "wms4"])]})
                    stream(tiles, qr, lambda t, hk=hk, kts=kts: (VW[:, kts[t], hk * 65:(hk + 1) * 65], [("VW", kts[t] // 8)]), lambda g: PSW[:, g, :], 65, "pwin")
                    olo = 64 - lo
                    for h2 in range(2):
                        TE(lambda e, h2=h2, olo=olo: e.matmul(PS[7][olo:olo + 64, h2 * 128:(h2 + 1) * 128], nsel[:, h2 * 64:(h2 + 1) * 64], ident[:], start=True, stop=True),
                           r=["nsel", "ident"], w=[psk(7)])
                    V(lambda e, lo=lo, hi=hi, i=i: e.tensor_copy(RQ[lo:hi, :, :], QNT[lo:hi, i, :].unsqueeze(1).to_broadcast([64, 2, 512])), r=["QNT"], w=["RQ"])
                    for h2 in range(2):
                        V(lambda e, h2=h2, olo=olo: e.tensor_copy(RQ[olo:olo + 64, h2, :].rearrange("p (g q) -> p g q", g=4), PS[7][olo:olo + 64, h2 * 128:(h2 + 1) * 128].unsqueeze(1).to_broadcast([64, 4, 128])),
                          r=[psk(7), "RQ"], w=["RQ"])
                    if debug and i in (0, 1, 4, 15, 31):
                        dump("sel_%d_%d" % (i, hk), sc2[:], [128, 128], r=["sc2"])
                        dump("imp_%d_%d" % (i, hk), imp[:], [128, 128], r=["imp"])
                    for g in range(4):
                        gi = (hk * 4 + g) * 3 + 1 + 1
                        c0 = 8 + 1 * 4 + g
                        V(lambda e, g=g, c0=c0: e.tensor_scalar(den[:, c0:c0 + 1], PSW[:, g, 64:65], 1e-30, None, op0=ALU.max), r=[("pwin", g)], w=[("den", c0)])
                        V(lambda e, c0=c0: e.reciprocal(den[:, c0:c0 + 1], den[:, c0:c0 + 1]), r=[("den", c0)], w=[("den", c0)])
                        V(lambda e, c0=c0, gi=gi, i=i: e.tensor_tensor(den[:, c0:c0 + 1], den[:, c0:c0 + 1], gates[:, i, gi:gi + 1], op=ALU.mult), r=[("den", c0), "gates"], w=[("den", c0)])
                        V(lambda e, g=g, c0=c0, hk=hk: e.scalar_tensor_tensor(oacc[:, hk, g, :], PSW[:, g, 0:64], den[:, c0:c0 + 1], oacc[:, hk, g, :], op0=ALU.mult, op1=ALU.add),
                          r=[("pwin", g), ("den", c0), ("oacc", hk, g)], w=[("oacc", hk, g)])
                    tiles = []
                    for kt in range(2 * i + 2):
                        mm = [(KSA[hk][:, kt * 128:(kt + 1) * 128], RQ[:, kt // 32, :], [("KSA", hk), "RQ"])]
                        if kt >= 2 * i:
                            mm.append((ident[:], cms4[:, kt - 2 * i, :, :].rearrange("p g q -> p (g q)"), ["ident", "cms4"]))
                        tiles.append({"score": mm})
                    stream(tiles, qr, lambda t, hk=hk: (VS[:, t, hk * 65:(hk + 1) * 65], [("VS", t // 8)]), lambda g: PSS[:, g, :], 65, "psel")
                    for g in range(4):
                        gi = (hk * 4 + g) * 3 + 1 + 0
                        c0 = 8 + 0 * 4 + g
                        V(lambda e, g=g, c0=c0: e.tensor_scalar(den[:, c0:c0 + 1], PSS[:, g, 64:65], 1e-30, None, op0=ALU.max), r=[("psel", g)], w=[("den", c0)])
                        V(lambda e, c0=c0: e.reciprocal(den[:, c0:c0 + 1], den[:, c0:c0 + 1]), r=[("den", c0)], w=[("den", c0)])
                        V(lambda e, c0=c0, gi=gi, i=i: e.tensor_tensor(den[:, c0:c0 + 1], den[:, c0:c0 + 1], gates[:, i, gi:gi + 1], op=ALU.mult), r=[("den", c0), "gates"], w=[("den", c0)])
                        V(lambda e, g=g, c0=c0, hk=hk: e.scalar_tensor_tensor(oacc[:, hk, g, :], PSS[:, g, 0:64], den[:, c0:c0 + 1], oacc[:, hk, g, :], op0=ALU.mult, op1=ALU.add),
                          r=[("psel", g), ("den", c0), ("oacc", hk, g)], w=[("oacc", hk, g)])
                okeys = [("oacc", hk_, g_) for hk_ in range(2) for g_ in range(4)]
                V(lambda e: e.tensor_copy(obf[:], oacc[:].rearrange("p a b c -> p (a b c)")), r=okeys, w=["obf"])
                for c in range(4):
                    TE(lambda e, c=c: e.matmul(PS[7][:, c * 128:(c + 1) * 128], obf[:, c * 128:(c + 1) * 128], ident[:], start=True, stop=True), r=["obf", "ident"], w=[psk(7)])
                V(lambda e: e.tensor_copy(oTs[:].rearrange("p a b -> p (a b)"), PS[7][:]), r=[psk(7)], w=["oTs"])
                P.dma("sync", oT_d[:, 0:4, i * 128:(i + 1) * 128], oTs[:], r=["oTs"], w=[("oTd", i)], semkey="oTs")
            P.barrier()
            P.flush()
        dump("oT_nsa", oT_d[:, 0:4, :].rearrange("p a t -> p (a t)"), [128, 4 * NOWN * 128], BF)
        dump_done()
        if stop_after <= 3:
            return nc, DBG, None

        with ExitStack() as ph:
            KSB = sbt(ph, "KSB", [128, 4, T], BF)
            VSB = sbt(ph, "VSB", [128, NT, 512], BF)
            QSB = sbt(ph, "QSB", [128, 4, NOWN * 128], BF)
            uneg = sbt(ph, "uneg", [128, 128], BF)
            nones = sbt(ph, "nones", [128, 128], BF)
            sbm = sbt(ph, "sbm", [128, 8, 512], BF)
            e1 = sbt(ph, "e1", [128, 2, 512], F32)
            spb = [sbt(ph, "spb%d" % k, [128, 2, 512], BF) for k in range(2)]
            aTb = [sbt(ph, "aTb%d" % k, [128, 2, 512], BF) for k in range(2)]
            Rb = sbt(ph, "Rb", [128, 2, 512], BF)
            osb = sbt(ph, "osb", [128, 512], BF)
            P.dma("sync", QSB[:], qsbT_d, w=["QSB"], semkey=("l", 8))
            P.dma("sync", uneg[:], IN["uneg"], w=["uneg"], semkey=("l", 9))
            P.dma("sync", nones[:], IN["negones"], w=["nones"], semkey=("l", 10))
            P.dma("sync", sbm[:].rearrange("p a b -> p (a b)"), IN["sbmask"], w=["sbm"], semkey=("l", 11))
            P.dma("sync", KSB[:, 0, :], ksbT_d[:, 0, :], w=[("KSB", 0)], semkey=("lw", 0))
            for q8 in range(8):
                P.dma("sync", VSB[:, q8 * 8:(q8 + 1) * 8, :], vsb_d[q8 * 1024:(q8 + 1) * 1024, :].rearrange("(j p) c -> p j c", p=128), w=[("VSB", q8)], semkey=("lv", q8 % 4))
            for pr in range(1, 4):
                P.dma("sync", KSB[:, pr, :], ksbT_d[:, pr, :], w=[("KSB", pr)], semkey=("lw", pr))
            PSP = [PSALL[:, 0:2, :], PSALL[:, 2:4, :], PSALL[:, 4:6, :]]
            sbq = quick and not os.environ.get("SBFULL")
            for pr in range(1 if sbq else 4):
                for g8 in range(2 if sbq else 8):
                    nkt = 8 * (g8 + 1)
                    kts = list(range(nkt - 1, -1, -1))

                    def c0_of(kt, g8=g8):
                        r = kt - 8 * g8
                        return 0 if r <= 0 else 128 * ((r - 1 + 1) // 2)

                    V(lambda e: e.memset(Rb[:], 0.0), w=["Rb"])

                    def S1(t, kts=kts, pr=pr, g8=g8):
                        kt = kts[t]
                        c0 = c0_of(kt)
                        for hh in range(2):
                            lo, hi = hh * 64, hh * 64 + 64
                            b = (t % 3) * 2 + hh
                            TE(lambda e, lo=lo, hi=hi, b=b, kt=kt, pr=pr, g8=g8, c0=c0: e.matmul(PS[b][:, c0:512], KSB[lo:hi, pr, kt * 128:(kt + 1) * 128], QSB[lo:hi, pr, g8 * 512 + c0:(g8 + 1) * 512], start=True, stop=False),
                               r=[("KSB", pr), "QSB"], w=[("psp", t % 3)])

                    def S2(t, kts=kts, g8=g8):
                        kt = kts[t]
                        c0 = c0_of(kt)
                        k2, k3 = t % 2, t % 3
                        A(lambda e, k3=k3, c0=c0: e.activation(e1[:, :, c0:512], PSP[k3][:, :, c0:512], AF.Exp), r=[("psp", k3)], w=["e1"])
                        A(lambda e, k2=k2, c0=c0: e.activation(spb[k2][:, :, c0:512], e1[:, :, c0:512], AF.Ln, bias=1.0), r=["e1"], w=[("spb", k2)])
                        if kt >= 8 * g8:
                            mk = sbm[:, kt - 8 * g8, c0:512].unsqueeze(1).to_broadcast([128, 2, 512 - c0])
                            V(lambda e, k2=k2, mk=mk, c0=c0: e.tensor_tensor(spb[k2][:, :, c0:512], spb[k2][:, :, c0:512], mk, op=ALU.mult), r=[("spb", k2), "sbm"], w=[("spb", k2)])

                    def S3(t, nkt=nkt, kts=kts):
                        k2, k3 = t % 2, t % 3
                        c0 = c0_of(kts[t])
                        for hh in range(2):
                            b = k3 * 2 + hh
                            TE(lambda e, b=b, k2=k2, hh=hh, t=t, c0=c0: e.matmul(PS[b][:, c0:512], uneg[:], spb[k2][:, hh, c0:512], start=False, stop=(t == 0)), r=[("spb", k2), "uneg"], w=[("psp", k3)])
                            if t > 0:
                                TE(lambda e, b=b, hh=hh, c0=c0: e.matmul(PS[b][:, c0:512], nones[:], Rb[:, hh, c0:512], start=False, stop=True), r=["Rb", "nones"], w=[("psp", k3)])
                        if t + 1 < nkt:
                            V(lambda e, k2=k2, c0=c0: e.tensor_tensor(Rb[:, :, c0:512], Rb[:, :, c0:512], spb[k2][:, :, c0:512], op=ALU.add), r=["Rb", ("spb", k2)], w=["Rb"])

                    def S4(t, kts=kts, g8=g8):
                        kt = kts[t]
                        c0 = c0_of(kt)
                        k2, k3 = t % 2, t % 3
                        A(lambda e, k2=k2, k3=k3, c0=c0: e.activation(aTb[k2][:, :, c0:512], PSP[k3][:, :, c0:512], AF.Exp), r=[("psp", k3)], w=[("aTb", k2)])
                        if kt >= 8 * g8:
                            mk = sbm[:, kt - 8 * g8, c0:512].unsqueeze(1).to_broadcast([128, 2, 512 - c0])
                            V(lambda e, k2=k2, mk=mk, c0=c0: e.tensor_tensor(aTb[k2][:, :, c0:512], aTb[k2][:, :, c0:512], mk, op=ALU.mult), r=[("aTb", k2), "sbm"], w=[("aTb", k2)])

                    def S5(t, kts=kts, pr=pr, nkt=nkt):
                        kt = kts[t]
                        c0 = c0_of(kt)
                        k2 = t % 2
                        for hh in range(2):
                            h = pr * 2 + hh
                            TE(lambda e, hh=hh, k2=k2, kt=kt, h=h, t=t, nkt=nkt, c0=c0: e.matmul(PS[6 + hh][hh * 64:hh * 64 + 64, c0:512], VSB[:, kt, h * 64:(h + 1) * 64], aTb[k2][:, hh, c0:512], start=(t == 0), stop=(t == nkt - 1)),
                               r=[("aTb", k2), ("VSB", kt // 8)], w=[psk(6 + hh)])

                    S1(0)
                    S1(1)
                    S2(0)
                    S3(0)
                    for s_ in range(nkt):
                        if s_ + 2 < nkt:
                            S1(s_ + 2)
                        if s_ + 1 < nkt:
                            S2(s_ + 1)
                            S3(s_ + 1)
                        S4(s_)
                        S5(s_)
                    for hh in range(2):
                        evac_copy(hh, osb[hh * 64:hh * 64 + 64, :], PS[6 + hh][hh * 64:hh * 64 + 64, :], r=[psk(6 + hh)], w=["osb"])
                    P.dma("sync", oT_d[:, 4 + pr, g8 * 512:(g8 + 1) * 512], osb[:], r=["osb"], w=[("oTd", pr, g8)], semkey="osb")
            P.barrier()
            P.flush()
        dump("oT_sb", oT_d[:, 4:8, :].rearrange("p a t -> p (a t)"), [128, 4 * NOWN * 128], BF)
        dump_done()
        if stop_after <= 4:
            return nc, DBG, None

        pre2 = top.enter_context(ExitStack())
        wg2_pre = sbt(pre2, "wg2pre", [128, 8, FF], BF)
        with ExitStack() as ph:
            wout = sbt(ph, "wout", [128, 8, D], BF)
            mwq = sbt(ph, "mwq", [128, 8, 256], BF)
            mwk = sbt(ph, "mwk", [128, 8, 256], BF)
            mwv = sbt(ph, "mwv", [128, 8, 256], BF)
            mwo = sbt(ph, "mwo", [128, 2, D], BF)
            gbc = sbt(ph, "gbc", [128, D], F32)
            gkv = sbt(ph, "gkv", [128, D], F32)
            gmq = sbt(ph, "gmq", [128, 64], F32)
            gmk = sbt(ph, "gmk", [128, 64], F32)
            mt_ = sbt(ph, "mt", [128, 2, D], F32)
            mb = [sbt(ph, "mb%d" % k, [128, D], BF) for k in range(2)]
            memT = sbt(ph, "memT", [128, 8, 256], BF)
            KMT = sbt(ph, "KMT", [128, 2, 256], BF)
            VM = sbt(ph, "VM", [128, 2, 4, 65], BF)
            kmstg = sbt(ph, "kmstg", [128, 4, 64], BF)
            xt = sbt(ph, "xt", [128, 4, D], F32)
            oTb = sbt(ph, "oTb", [128, 8, 512], BF)
            hb = [sbt(ph, "hb%d" % k, [128, D], BF) for k in range(2)]
            hT = sbt(ph, "hT", [128, 8, 512], BF)
            qstg = sbt(ph, "qstg", [128, 4, 64], BF)
            qmT = sbt(ph, "qmT", [128, 2, 512], BF)
            Em = [sbt(ph, "Em%d" % k, [128, 512], BF) for k in range(2)]
            om = sbt(ph, "om", [128, 4, 256], BF)
            omT = sbt(ph, "omT", [128, 2, 512], BF)
            rd = sbt(ph, "rd", [128, 16], F32)
            junk = sbt(ph, "junk", [128, D], BF)
            ss4 = sbt(ph, "ss4", [128, 4], F32)
            rstd = sbt(ph, "rstd", [128, 4], F32)
            tm = (sbt(ph, "nsq", [128, 8, 64], F32), sbt(ph, "nss", [128, 8], F32), sbt(ph, "ny", [128, 8, 64], F32),
                  sbt(ph, "nt1", [128, 8, 8], F32), sbt(ph, "nt2", [128, 8, 8], F32))
            load_w_bf(wout, IN["w_out"], 8, "wout")
            load_w_bf(mwq, IN["mwq"], 8, "mwq")
            load_w_bf(mwk, IN["mwk"], 8, "mwk")
            load_w_bf(mwv, IN["mwv"], 8, "mwv")
            load_w_bf(mwo, IN["mwo"], 2, "mwo")
            load_w_bf(wg2_pre, IN["wg2"], 8, "wg")
            P.dma("sync", gbc[:], IN["g_memx"], w=["gbc"], semkey="c0")
            P.dma("sync", gkv[:], IN["g_memkv"], w=["gkv"], semkey="c1")
            P.dma("sync", gmq[:], IN["g_mq"], w=["gains"], semkey="c2")
            P.dma("sync", gmk[:], IN["g_mk"], w=["gains"], semkey="c3")
            P.dma("sync", mt_[:], IN["mem"].rearrange("(j p) d -> p j d", p=128), w=["mt"], semkey="c4")
            V(lambda e: e.memset(VM[:], 1.0), w=["VM"])
            rms_stats((junk, ss4, rstd), [mt_[:, j, :] for j in range(2)], ["mt", "mt"], "mm")
            for j in range(2):
                V(lambda e, j=j: e.scalar_tensor_tensor(mb[j][:], mt_[:, j, :], rstd[:, j:j + 1], gkv[:], op0=ALU.mult, op1=ALU.mult), r=["mt", "mmrstd", "gkv"], w=[("mb", j)])
                for half in range(2):
                    bank = 4 + half
                    for kk in range(4):
                        kc = half * 4 + kk
                        TE(lambda e, j=j, kc=kc, kk=kk, bank=bank: e.matmul(PS[bank][:, kk * 128:(kk + 1) * 128], mb[j][:, kc * 128:(kc + 1) * 128], ident[:], start=True, stop=True),
                           r=[("mb", j), "ident"], w=[psk(bank)])
                    evac_copy(j + half, memT[:, half * 4:half * 4 + 4, j * 128:(j + 1) * 128], PS[bank][:].rearrange("p (k t) -> p k t", k=4), r=[psk(bank)], w=[("memT", half, j)])
            mT_keys = [("memT", h_, j_) for h_ in range(2) for j_ in range(2)]
            for j in range(2):
                for kc in range(8):
                    TE(lambda e, j=j, kc=kc: e.matmul(PS[0][:, 0:256], memT[:, kc, j * 128:(j + 1) * 128], mwk[:, kc, :], start=(kc == 0), stop=(kc == 7)), r=mT_keys + ["mwkA", "mwkB"], w=[psk(0)])
                head_norm_rope(tm, PS[0][:, 0:256].rearrange("p (h d) -> p h d", h=4), 4, gmk[:].unsqueeze(1).to_broadcast([128, 4, 64]), None, kmstg[:], [psk(0)], ["kmstg"], "k")
                for pr in range(2):
                    TE(lambda e, pr=pr: e.matmul(PS[1][:, pr * 128:(pr + 1) * 128], kmstg[:, 2 * pr:2 * pr + 2, :].rearrange("p h d -> p (h d)"), ident[:], start=True, stop=True), r=["kmstg", "ident"], w=[psk(1)])
                V(lambda e, j=j: e.tensor_copy(KMT[:, :, j * 128:(j + 1) * 128], PS[1][:, 0:256].rearrange("p (a t) -> p a t", a=2)), r=[psk(1)], w=["KMT"])
                for kc in range(8):
                    TE(lambda e, j=j, kc=kc: e.matmul(PS[2][:, 0:256], memT[:, kc, j * 128:(j + 1) * 128], mwv[:, kc, :], start=(kc == 0), stop=(kc == 7)), r=mT_keys + ["mwvA", "mwvB"], w=[psk(2)])
                V(lambda e, j=j: e.tensor_copy(VM[:, j, :, 0:64], PS[2][:, 0:256].rearrange("p (h d) -> p h d", h=4)), r=[psk(2), "VM"], w=["VM"])
            for n in range(8):
                for j in range(4):
                    P.dma("sync", xt[:, j, :], x1o_d[(n * 4 + j) * 128:(n * 4 + j + 1) * 128, :], w=[("xt", j)], semkey=("xt", j))
                P.dma("sync", oTb[:], oT_d[:, :, n * 512:(n + 1) * 512], w=["oTb"], semkey="oTb")
                for j in range(4):
                    for half in range(2):
                        bank = 6 + half
                        for c in range(8):
                            TE(lambda e, j=j, half=half, c=c, bank=bank: e.matmul(PS[bank][:], oTb[:, c, j * 128:(j + 1) * 128], wout[:, c, half * 512:(half + 1) * 512], start=(c == 0), stop=(c == 7)),
                               r=["oTb", "woutA", "woutB"], w=[psk(bank)])
                        V(lambda e, j=j, half=half, bank=bank: e.tensor_tensor(xt[:, j, half * 512:(half + 1) * 512], xt[:, j, half * 512:(half + 1) * 512], PS[bank][:], op=ALU.add),
                          r=[psk(bank), ("xt", j)], w=[("xt", j)])
                if debug and n == 0:
                    for j in range(4):
                        dump("x2_%d" % j, xt[:, j, :], [128, D], r=[("xt", j)])
                rms_stats((junk, ss4, rstd), [xt[:, j, :] for j in range(4)], [("xt", j) for j in range(4)], "x")
                for j in range(4):
                    hbj = hb[j % 2]
                    V(lambda e, j=j, hbj=hbj: e.scalar_tensor_tensor(hbj[:], xt[:, j, :], rstd[:, j:j + 1], gbc[:], op0=ALU.mult, op1=ALU.mult), r=[("xt", j), "xrstd", "gbc"], w=[("hb", j % 2)])
                    for half in range(2):
                        bank = 4 + half
                        for kk in range(4):
                            kc = half * 4 + kk
                            TE(lambda e, hbj=hbj, kc=kc, kk=kk, bank=bank: e.matmul(PS[bank][:, kk * 128:(kk + 1) * 128], hbj[:, kc * 128:(kc + 1) * 128], ident[:], start=True, stop=True),
                               r=[("hb", j % 2), "ident"], w=[psk(bank)])
                        evac_copy(j + half, hT[:, half * 4:half * 4 + 4, j * 128:(j + 1) * 128], PS[bank][:].rearrange("p (k t) -> p k t", k=4), r=[psk(bank)], w=[("hT", half, j)])
                hT_keys = [("hT", h_, j_) for h_ in range(2) for j_ in range(4)]
                for j in range(4):
                    for kc in range(8):
                        TE(lambda e, j=j, kc=kc: e.matmul(PS[0][:, 0:256], hT[:, kc, j * 128:(j + 1) * 128], mwq[:, kc, :], start=(kc == 0), stop=(kc == 7)), r=hT_keys + ["mwqA", "mwqB"], w=[psk(0)])
                    head_norm_rope(tm, PS[0][:, 0:256].rearrange("p (h d) -> p h d", h=4), 4, gmq[:].unsqueeze(1).to_broadcast([128, 4, 64]), None, qstg[:], [psk(0)], ["qstg"], "k")
                    for pr in range(2):
                        TE(lambda e, pr=pr: e.matmul(PS[1][:, pr * 128:(pr + 1) * 128], qstg[:, 2 * pr:2 * pr + 2, :].rearrange("p h d -> p (h d)"), ident[:], start=True, stop=True), r=["qstg", "ident"], w=[psk(1)])
                    V(lambda e, j=j: e.tensor_copy(qmT[:, :, j * 128:(j + 1) * 128], PS[1][:, 0:256].rearrange("p (a t) -> p a t", a=2)), r=[psk(1)], w=[("qmT", j)])
                qm_keys = [("qmT", j_) for j_ in range(4)]
                PSO = [PS[4][:].rearrange("p (h c) -> p h c", h=4), PS[5][:].rearrange("p (h c) -> p h c", h=4),
                       PS[6][:].rearrange("p (h c) -> p h c", h=4), PS[7][:].rearrange("p (h c) -> p h c", h=4)]
                cnt_e = 0
                for h in range(4):
                    pr, lo = h // 2, (h % 2) * 64
                    for mt2 in range(2):
                        b = 2 + cnt_e % 2
                        E = Em[cnt_e % 2]
                        ek = ("Em", cnt_e % 2)
                        cnt_e += 1
                        TE(lambda e, pr=pr, lo=lo, mt2=mt2, b=b: e.matmul(PS[b][:], KMT[lo:lo + 64, pr, mt2 * 128:(mt2 + 1) * 128], qmT[lo:lo + 64, pr, :], start=True, stop=True), r=["KMT"] + qm_keys, w=[psk(b)])
                        A(lambda e, E=E, b=b: e.activation(E[:], PS[b][:], AF.Exp, scale=0.125), r=[psk(b)], w=[ek])
                        for j in range(4):
                            TE(lambda e, E=E, j=j, h=h, mt2=mt2: e.matmul(PSO[j][:, h, 0:65], E[:, j * 128:(j + 1) * 128], VM[:, mt2, h, :], start=(mt2 == 0), stop=(mt2 == 1)), r=[ek, "VM"], w=[psk(4 + j)])
                for j in range(4):
                    for h in range(4):
                        V(lambda e, j=j, h=h: e.reciprocal(rd[:, j * 4 + h:j * 4 + h + 1], PSO[j][:, h, 64:65]), r=[psk(4 + j)], w=[("rd", j, h)])
                        V(lambda e, j=j, h=h: e.tensor_scalar(om[:, j, h * 64:(h + 1) * 64], PSO[j][:, h, 0:64], rd[:, j * 4 + h:j * 4 + h + 1], None, op0=ALU.mult), r=[psk(4 + j), ("rd", j, h)], w=[("om", j)])
                for j in range(4):
                    for c in range(2):
                        TE(lambda e, j=j, c=c: e.matmul(PS[0][:, c * 128:(c + 1) * 128], om[:, j, c * 128:(c + 1) * 128], ident[:], start=True, stop=True), r=[("om", j), "ident"], w=[psk(0)])
                    V(lambda e, j=j: e.tensor_copy(omT[:, :, j * 128:(j + 1) * 128], PS[0][:, 0:256].rearrange("p (a t) -> p a t", a=2)), r=[psk(0)], w=[("omT", j)])
                omT_keys = [("omT", j_) for j_ in range(4)]
                for j in range(4):
                    for half in range(2):
                        bank = 2 + half
                        for c in range(2):
                            TE(lambda e, j=j, half=half, c=c, bank=bank: e.matmul(PS[bank][:], omT[:, c, j * 128:(j + 1) * 128], mwo[:, c, half * 512:(half + 1) * 512], start=(c == 0), stop=(c == 1)),
                               r=omT_keys + ["mwoA", "mwoB"], w=[psk(bank)])
                        V(lambda e, j=j, half=half, bank=bank: e.tensor_tensor(xt[:, j, half * 512:(half + 1) * 512], xt[:, j, half * 512:(half + 1) * 512], PS[bank][:], op=ALU.add),
                          r=[psk(bank), ("xt", j)], w=[("xt", j)])
                    P.dma("sync", x3o_d[(n * 4 + j) * 128:(n * 4 + j + 1) * 128, :], xt[:, j, :], r=[("xt", j)], w=[("x3d", n, j)], semkey=("st", j))
            P.barrier()
            P.flush()
        dump("x3", x3o_d[0:512, :], [512, D])
        dump_done()
        if stop_after <= 5:
            return nc, DBG, None

        ffn_phase("f2", x3o_d, out_d, 8, IN["wg2"], IN["wu2"], IN["wd2"], IN["g_ffn2"], wg_pre=wg2_pre)
        print("n_instr", P.n_instr, "n_ops", len(P.ops), "dma_sems", len(P.dma_sem))
        return nc, DBG, None


def _consts():
    c = {}
    c["freqs"] = np.broadcast_to((500000.0 ** (-np.arange(0, 16, 2, dtype=np.float32) / 16)).astype(np.float32)[None, :], (128, 8)).copy()
    c["ident"] = np.eye(128, dtype=np.float32).astype(NPBF)
    jj, ss_ = np.meshgrid(np.arange(128), np.arange(128), indexing="ij")
    c["uneg"] = (-(jj >= ss_).astype(np.float32)).astype(NPBF)
    c["negones"] = (-np.ones((128, 128), np.float32)).astype(NPBF)
    ex = np.zeros((64, 64, 128), np.float32)
    for kt in range(64):
        ex[(2 * kt) % 64, kt, 0:64] = 1.0
        ex[(2 * kt + 1) % 64, kt, 64:128] = 1.0
    c["expd2"] = ex.reshape(64, 64 * 128).astype(NPBF)
    n = np.arange(512)[:, None]
    j = np.arange(128)[None, :]
    ov = np.minimum(n * 16 + 32, j * 64 + 64) - np.maximum(n * 16, j * 64)
    ov = np.clip(ov, 0, None).astype(np.float32) / 32.0
    ov[511, :] = 0.0
    c["wov"] = ov.reshape(4, 128, 128).transpose(1, 0, 2).reshape(128, 512).astype(NPBF)
    return c


def _core_tables(p):
    t = {}
    sab = np.zeros((128, 2), np.float32)
    sab[:, p] = 1.0
    t["sab"] = sab
    nl = np.arange(128)[:, None]
    ql = np.arange(128)[None, :]
    mc = np.zeros((128, 8, 128), np.float32)
    for a in range(8):
        mc[:, a, :] = (16 * nl + 31 <= 256 * a + 128 * p + ql)
    t["maskc"] = mc.reshape(128, 1024).astype(NPBF)
    A2 = np.zeros((NOWN, 128, 128), np.float32)
    A13 = np.zeros((NOWN, 128, 128), np.float32)
    jb = np.arange(128)[None, :]
    for i in range(NOWN):
        tq = 128 * (2 * i + p) + np.arange(128)[:, None]
        cur = tq // 64
        valid = (jb * 64 <= tq)
        big = np.zeros((128, 128), np.float32)
        big = np.where(jb == 0, 1e4, big)
        big = np.where(jb == cur - 1, 2e4, big)
        big = np.where(jb == cur, 3e4, big)
        forced = big > 0
        A2[i] = (valid & ~forced)
        A13[i] = np.where(valid & forced, big, np.where(valid, 0.0, -1.0))
    t["A2"] = A2
    t["A13"] = A13
    kl = np.arange(128)[:, None]
    causal = np.where(kl <= ql, 0.0, -BIG).astype(np.float32)
    cm = np.zeros((128, 2, 128), np.float32)
    if p == 0:
        cm[:, 0, :] = causal
        cm[:, 1, :] = -BIG
    else:
        cm[:, 0, :] = 0.0
        cm[:, 1, :] = causal
    t["cmsel"] = cm.reshape(128, 256).astype(NPBF)
    wm = np.zeros((128, 6, 128), np.float32)
    for rel in range(6):
        dd = 128 * (p + 4 - rel) + ql - kl
        wm[:, rel, :] = np.where((dd >= 0) & (dd < 512), 0.0, -BIG)
    t["wmask"] = wm.reshape(128, 768).astype(NPBF)
    sm = np.zeros((128, 8, 4, 128), np.float32)
    for r in range(8):
        for m in range(4):
            sm[:, r, m, :] = (128 * (r - 2 * m - p) + kl < ql)
    t["sbmask"] = sm.reshape(128, 8 * 512).astype(NPBF)
    return t


def _prep_inputs(inp):
    consts = _consts()
    f = lambda k: np.ascontiguousarray(np.asarray(inp[k])[0], dtype=np.float32)
    rep = lambda v, n=128: np.ascontiguousarray(np.broadcast_to(np.asarray(v, np.float32)[None, :], (n, v.shape[-1])))
    shared = {
        "wg1": f("ffn1_wg"), "wu1": f("ffn1_wu"), "wd1": f("ffn1_wd"),
        "wg2": f("ffn2_wg"), "wu2": f("ffn2_wu"), "wd2": f("ffn2_wd"),
        "w_in": f("w_in"), "w_out": f("w_out"),
        "mwq": f("mem_wq"), "mwk": f("mem_wk"), "mwv": f("mem_wv"), "mwo": f("mem_wo"),
        "cw1k": f("cmp_w1_k"), "cw1v": f("cmp_w1_v"), "cw2k": f("cmp_w2_k"), "cw2v": f("cmp_w2_v"),
        "pekT": np.ascontiguousarray(f("cmp_pos_k").T), "pevT": np.ascontiguousarray(f("cmp_pos_v").T),
        "g_ffn1": rep(f("ffn1_norm")), "g_mix": rep(f("mix_norm")), "g_memx": rep(f("mem_x_norm")),
        "g_memkv": rep(f("mem_kv_norm")), "g_ffn2": rep(f("ffn2_norm")),
        "g_q": rep(f("nsa_q_norm")), "g_kc": rep(f("nsa_kc_norm")),
        "g_ksw": np.ascontiguousarray(np.concatenate([rep(f("nsa_ks_norm")), rep(f("nsa_kw_norm"))], axis=1)),
        "g_mq": rep(f("mem_q_norm")), "g_mk": rep(f("mem_k_norm")),
    }
    shared.update(consts)
    x = np.asarray(inp["x"], np.float32)
    mem = np.asarray(inp["mem"], np.float32)
    pos = np.asarray(inp["positions"]).astype(np.int32)
    tabs = [_core_tables(0), _core_tables(1)]
    in_maps = []
    for c in range(8):
        b, p = c // 2, c % 2
        m = dict(shared)
        m.update(tabs[p])
        m["x"] = np.ascontiguousarray(x[b])
        m["mem"] = np.ascontiguousarray(mem[b])
        pt = pos[b].reshape(64, 128)
        own = pt[p::2]
        cmp_pos = np.zeros(512, np.int32)
        cmp_pos[:511] = pos[b][np.arange(511) * 16 + 31]
        m["pos_cat"] = np.ascontiguousarray(np.concatenate([pt.T, own.T, cmp_pos.reshape(4, 128).T], axis=1).astype(np.int32))
        in_maps.append(m)
    return in_maps


def kernel(**inputs):
    in_maps = _prep_inputs(inputs)
    nc, _, _ = build_nc()
    res = run_bass_kernel_spmd(nc, in_maps, core_ids=list(range(8)))
    out = np.zeros((4, T, D), np.float32)
    for c in range(8):
        b, p = c // 2, c % 2
        o = np.asarray(res.results[c]["out"]).reshape(NOWN, 128, D)
        out[b].reshape(64, 128, D)[p::2] = o
    return out
```

```python
import os
import numpy as np
import ml_dtypes
from contextlib import ExitStack
import concourse.bass as bass
import concourse.mybir as mybir
from concourse.bass_utils import run_bass_kernel_spmd

F32 = mybir.dt.float32
BF = mybir.dt.bfloat16
I32 = mybir.dt.int32
AF = mybir.ActivationFunctionType
ALU = mybir.AluOpType
AX = mybir.AxisListType
NPBF = ml_dtypes.bfloat16

T = 8192
D = 1024
FF = 2816
NFC = 22
NT = 64
NOWN = 32
EPS = 1e-6
BIG = 30000.0
TWO_PI = 6.283185307179586
PI = 3.141592653589793
C_QN, C_KC, C_VC, C_KS, C_VS, C_KW, C_VW, C_GT, C_QS, C_KSB, C_VSB = 0, 512, 640, 768, 896, 1024, 1152, 1280, 1304, 1816, 2328

COMPUTE = ("tensor", "scalar", "vector", "gpsimd")


class Op:
    __slots__ = ("eng", "fn", "deps", "is_dma", "semkey", "token", "has_dep", "idx")

    def __init__(self, eng, fn, is_dma=False, semkey=None):
        self.eng = eng
        self.fn = fn
        self.deps = []
        self.is_dma = is_dma
        self.semkey = semkey
        self.token = None
        self.has_dep = False
        self.idx = -1


class Prog:
    def __init__(self, nc, stack):
        self.nc = nc
        self.ops = []
        self.emitted = 0
        self.state = {}
        self.eng_sem = {}
        self.eng_cnt = {}
        for e in COMPUTE:
            self.eng_sem[e] = stack.enter_context(nc.semaphore("sem_" + e))
            self.eng_cnt[e] = 0
        self.dma_sem = {}
        self.dma_cnt = {}
        self.stack = stack
        self.waited = {e: {} for e in ("sync",) + COMPUTE}
        self.last_op = {}
        self.pending_dma = []
        self.last_dma = {}
        self.n_instr = 0

    def add(self, eng, fn, r=(), w=(), dma=False, semkey=None):
        op = Op(eng, fn, is_dma=dma, semkey=semkey)
        op.idx = len(self.ops)
        deps = {}
        for k in r:
            st = self.state.get(k)
            if st is not None:
                for d in st[0]:
                    deps[d.idx] = (d, "raw")
                if isinstance(k, tuple) and k[0] == "ps":
                    for d in st[1]:
                        if d.eng != eng and d.idx not in deps:
                            deps[d.idx] = (d, "rar")
        for k in w:
            st = self.state.get(k)
            if st is not None:
                for d in st[0]:
                    if d.idx not in deps:
                        deps[d.idx] = (d, "waw")
                for d in st[1]:
                    if d.idx not in deps:
                        deps[d.idx] = (d, "war")
        for d, kind in deps.values():
            if (not d.is_dma) and (not dma) and d.eng == eng:
                if eng == "tensor":
                    continue
            op.deps.append(d)
            d.has_dep = True
        for k in r:
            st = self.state.setdefault(k, [[], []])
            st[1].append(op)
        for k in w:
            self.state[k] = [[op], []]
        self.ops.append(op)
        if dma:
            prev = self.last_dma.get(semkey)
            if prev is not None and prev not in op.deps:
                op.deps.append(prev)
            self.last_dma[semkey] = op
            op.has_dep = True
            self.pending_dma.append(op)
        else:
            self.last_op[eng] = op
        return op

    def dma(self, eng, out, in_, r=(), w=(), semkey=None):
        assert semkey is not None
        return self.add(eng, lambda e: e.dma_start(out=out, in_=in_), r=r, w=w, dma=True, semkey=semkey)

    def barrier(self):
        lasts = list(self.last_op.values())
        dmas = list(self.pending_dma)
        self.pending_dma = []
        for e in ("sync",) + COMPUTE:
            op = Op(e, None)
            op.idx = len(self.ops)
            for d in lasts + dmas:
                if d.eng == e and not d.is_dma:
                    continue
                op.deps.append(d)
                d.has_dep = True
            self.ops.append(op)
        self.state = {}

    def flush(self):
        nc = self.nc
        for op in self.ops[self.emitted:]:
            e = getattr(nc, op.eng)
            wt = self.waited[op.eng]
            for d in op.deps:
                sem, val = d.token
                key = id(sem)
                if wt.get(key, 0) >= val:
                    continue
                e.wait_ge(sem, val)
                self.n_instr += 1
                wt[key] = val
            if op.fn is None:
                continue
            ins = op.fn(e)
            self.n_instr += 1
            if op.is_dma:
                sk = op.semkey
                if sk not in self.dma_sem:
                    self.dma_sem[sk] = self.stack.enter_context(nc.semaphore("dsem_%d" % len(self.dma_sem)))
                    self.dma_cnt[sk] = 0
                ins.then_inc(self.dma_sem[sk], 16)
                self.dma_cnt[sk] += 16
                op.token = (self.dma_sem[sk], self.dma_cnt[sk])
            elif op.has_dep:
                self.eng_cnt[op.eng] += 1
                ins.then_inc(self.eng_sem[op.eng], 1)
                op.token = (self.eng_sem[op.eng], self.eng_cnt[op.eng])
        self.emitted = len(self.ops)


INPUT_SPECS = [
    ("x", [T, D], F32), ("mem", [256, D], F32), ("pos_cat", [128, 100], I32), ("freqs", [128, 8], F32),
    ("wg1", [D, FF], F32), ("wu1", [D, FF], F32), ("wd1", [FF, D], F32),
    ("wg2", [D, FF], F32), ("wu2", [D, FF], F32), ("wd2", [FF, D], F32),
    ("w_in", [D, 2840], F32), ("w_out", [D, D], F32),
    ("mwq", [D, 256], F32), ("mwk", [D, 256], F32), ("mwv", [D, 256], F32), ("mwo", [256, D], F32),
    ("cw1k", [2048, 256], F32), ("cw1v", [2048, 256], F32), ("cw2k", [256, 64], F32), ("cw2v", [256, 64], F32),
    ("pekT", [64, 32], F32), ("pevT", [64, 32], F32),
    ("g_ffn1", [128, D], F32), ("g_mix", [128, D], F32), ("g_memx", [128, D], F32), ("g_memkv", [128, D], F32),
    ("g_ffn2", [128, D], F32),
    ("g_q", [128, 64], F32), ("g_kc", [128, 64], F32), ("g_ksw", [128, 128], F32),
    ("g_mq", [128, 64], F32), ("g_mk", [128, 64], F32),
    ("sab", [128, 2], F32), ("maskc", [128, 8 * 128], BF), ("A2", [NOWN, 128, 128], F32), ("A13", [NOWN, 128, 128], F32),
    ("cmsel", [128, 2 * 128], BF), ("wmask", [128, 6 * 128], BF), ("sbmask", [128, 8 * 512], BF),
    ("ident", [128, 128], BF), ("uneg", [128, 128], BF), ("negones", [128, 128], BF),
    ("expd2", [64, T], BF), ("wov", [128, 4 * 128], BF),
]


def build_nc(debug=False, stop_after=99, quick=False):
    nc = bass.Bass("TRN2", target_bir_lowering=False)
    IN = {}
    for name, shape, dt in INPUT_SPECS:
        IN[name] = nc.dram_tensor(name, shape, dt, kind="ExternalInput").ap()
    out_d = nc.dram_tensor("out", [NOWN * 128, D], F32, kind="ExternalOutput").ap()
    DBG = {}

    def dbg_out(name, shape, dt=F32):
        DBG[name] = nc.dram_tensor("dbg_" + name, shape, dt, kind="ExternalOutput").ap()
        return DBG[name]

    x1_all = nc.dram_tensor("x1_all", [T, D], F32).ap()
    x1o_d = nc.dram_tensor("x1o_d", [NOWN * 128, D], F32).ap()
    x3o_d = nc.dram_tensor("x3o_d", [NOWN * 128, D], F32).ap()
    ksT_d = nc.dram_tensor("ksT_d", [128, T], BF).ap()
    kwT_d = nc.dram_tensor("kwT_d", [128, T], BF).ap()
    vs_d = nc.dram_tensor("vs_d", [T, 130], BF).ap()
    vw_d = nc.dram_tensor("vw_d", [T, 130], BF).ap()
    ksbT_d = nc.dram_tensor("ksbT_d", [128, 4, T], BF).ap()
    vsb_d = nc.dram_tensor("vsb_d", [T, 512], BF).ap()
    qsbT_d = nc.dram_tensor("qsbT_d", [128, 4, NOWN * 128], BF).ap()
    qnT_d = nc.dram_tensor("qnT_d", [128, NOWN, 512], BF).ap()
    oT_d = nc.dram_tensor("oT_d", [128, 8, NOWN * 128], BF).ap()

    with ExitStack() as top:
        P = Prog(nc, top)

        _cnt = [0]

        def sbt(st, name, shape, dt):
            _cnt[0] += 1
            return st.enter_context(nc.sbuf_tensor("s%d_%s" % (_cnt[0], name), shape, dt))

        PSALL = top.enter_context(nc.psum_tensor("psall", [128, 8, 512], F32))
        PS = [PSALL[:, i, :] for i in range(8)]

        def psk(b):
            return ("ps", b)

        V = lambda fn, r=(), w=(): P.add("vector", fn, r, w)
        A = lambda fn, r=(), w=(): P.add("scalar", fn, r, w)
        G = lambda fn, r=(), w=(): P.add("gpsimd", fn, r, w)
        TE = lambda fn, r=(), w=(): P.add("tensor", fn, r, w)

        ident = sbt(top, "ident", [128, 128], BF)
        gates = sbt(top, "gates", [128, NOWN, 24], F32)
        cosT = sbt(top, "cosT", [128, 100, 8], F32)
        sinT = sbt(top, "sinT", [128, 100, 8], F32)
        KCT = sbt(top, "KCT", [128, 512], BF)
        WVC = sbt(top, "WVC", [128, 4, 2, 200], BF)
        sab = sbt(top, "sab", [128, 2], F32)
        sabb = sbt(top, "sabb", [128, 2], BF)
        epsb = sbt(top, "epsb", [128, 1], F32)

        P.dma("sync", ident[:], IN["ident"], w=["ident"], semkey="c0")
        P.dma("sync", sab[:], IN["sab"], w=["sab"], semkey="c1")
        V(lambda e: e.tensor_copy(sabb[:], sab[:]), r=["sab"], w=["sabb"])
        V(lambda e: e.memset(epsb[:], 0.0), w=["epsb"])

        def evac_copy(i, out, in_, r, w):
            if i % 2 == 0:
                return A(lambda e: e.copy(out, in_), r=r, w=w)
            return V(lambda e: e.tensor_copy(out, in_), r=r, w=w)

        with ExitStack() as ph:
            posi = sbt(ph, "posi", [128, 100], I32)
            posf = sbt(ph, "posf", [128, 100], F32)
            frq = sbt(ph, "frq", [128, 8], F32)
            ang = sbt(ph, "ang", [128, 100, 8], F32)
            tmpa = sbt(ph, "tmpa", [128, 100, 8], F32)
            ki = sbt(ph, "ki", [128, 100, 8], I32)
            kf = sbt(ph, "kf", [128, 100, 8], F32)
            P.dma("sync", posi[:], IN["pos_cat"], w=["posi"], semkey="c0")
            P.dma("sync", frq[:], IN["freqs"], w=["frq"], semkey="c1")
            V(lambda e: e.tensor_copy(posf[:], posi[:]), r=["posi"], w=["posf"])
            V(lambda e: e.tensor_tensor(ang[:], posf[:].unsqueeze(2).to_broadcast([128, 100, 8]),
                                        frq[:].unsqueeze(1).to_broadcast([128, 100, 8]), op=ALU.mult),
              r=["posf", "frq"], w=["ang"])

            def sin_of(dst, shift):
                V(lambda e: e.tensor_scalar(tmpa[:], ang[:], shift, None, op0=ALU.add), r=["ang"], w=["tmpa"])
                V(lambda e: e.tensor_scalar(kf[:], tmpa[:], 1.0 / TWO_PI, None, op0=ALU.mult), r=["tmpa"], w=["kf"])
                V(lambda e: e.tensor_copy(ki[:], kf[:]), r=["kf"], w=["ki"])
                V(lambda e: e.tensor_copy(kf[:], ki[:]), r=["ki"], w=["kf"])
                V(lambda e: e.scalar_tensor_tensor(tmpa[:], kf[:], -TWO_PI, tmpa[:], op0=ALU.mult, op1=ALU.add),
                  r=["kf", "tmpa"], w=["tmpa"])
                V(lambda e: e.tensor_scalar(kf[:], tmpa[:], 0.0, TWO_PI, op0=ALU.is_lt, op1=ALU.mult), r=["tmpa"], w=["kf"])
                V(lambda e: e.tensor_tensor(tmpa[:], tmpa[:], kf[:], op=ALU.add), r=["tmpa", "kf"], w=["tmpa"])
                V(lambda e: e.tensor_scalar(tmpa[:], tmpa[:], TWO_PI, 0.0, op0=ALU.min, op1=ALU.max), r=["tmpa"], w=["tmpa"])
                A(lambda e: e.activation(dst, tmpa[:], AF.Sin, scale=-1.0, bias=PI), r=["tmpa"], w=["trig"])

            sin_of(sinT[:], 0.0)
            sin_of(cosT[:], PI / 2)
            P.barrier()
            P.flush()

        def cs_all(j):
            return cosT[:, j, :], sinT[:, j, :]

        def cs_own(i):
            return cosT[:, 64 + i, :], sinT[:, 64 + i, :]

        def cs_cmp(c):
            return cosT[:, 96 + c, :], sinT[:, 96 + c, :]

        def rms_stats(st_tiles, src_list, skeys, tag):
            junk, ss, rstd = st_tiles
            n = len(src_list)
            V(lambda e: e.memset(ss[:, 0:n], 0.0), w=[tag + "ss"])
            for j, (src, sk) in enumerate(zip(src_list, skeys)):
                A(lambda e, j=j, src=src: e.activation(junk[:], src, AF.Square, accum_out=ss[:, j:j + 1]),
                  r=[sk, tag + "ss"], w=[tag + "junk", tag + "ss"])
            V(lambda e: e.tensor_scalar(ss[:, 0:n], ss[:, 0:n], 1.0 / D, EPS, op0=ALU.mult, op1=ALU.add),
              r=[tag + "ss"], w=[tag + "ss"])
            A(lambda e: e.activation(rstd[:, 0:n], ss[:, 0:n], AF.Ln), r=[tag + "ss"], w=[tag + "rstd"])
            A(lambda e: e.activation(rstd[:, 0:n], rstd[:, 0:n], AF.Exp, scale=-0.5), r=[tag + "rstd"], w=[tag + "rstd"])

        def head_norm_rope(tm, src, H, gain, cs, dst, rk, wk, tag):
            sq, ss, y, t1, t2 = tm
            A(lambda e: e.activation(sq[:, 0:H, :], src, AF.Square), r=rk, w=[tag + "sq"])
            V(lambda e: e.tensor_reduce(ss[:, 0:H], sq[:, 0:H, :], axis=AX.X, op=ALU.add), r=[tag + "sq"], w=[tag + "ss"])
            V(lambda e: e.tensor_scalar(ss[:, 0:H], ss[:, 0:H], 1.0 / 64, EPS, op0=ALU.mult, op1=ALU.add),
              r=[tag + "ss"], w=[tag + "ss"])
            A(lambda e: e.activation(ss[:, 0:H], ss[:, 0:H], AF.Ln), r=[tag + "ss"], w=[tag + "ss"])
            A(lambda e: e.activation(ss[:, 0:H], ss[:, 0:H], AF.Exp, scale=-0.5), r=[tag + "ss"], w=[tag + "ss"])
            V(lambda e: e.tensor_tensor(y[:, 0:H, :], src, ss[:, 0:H].unsqueeze(2).to_broadcast([128, H, 64]), op=ALU.mult),
              r=list(rk) + [tag + "ss"], w=[tag + "y"])
            V(lambda e: e.tensor_tensor(y[:, 0:H, :], y[:, 0:H, :], gain, op=ALU.mult), r=[tag + "y", "gains"], w=[tag + "y"])
            if cs is None:
                V(lambda e: e.tensor_copy(dst, y[:, 0:H, :]), r=[tag + "y"], w=wk)
                return
            c, s = cs
            cb = c.unsqueeze(1).to_broadcast([128, H, 8])
            sb_ = s.unsqueeze(1).to_broadcast([128, H, 8])
            V(lambda e: e.tensor_tensor(t1[:, 0:H, :], y[:, 0:H, 0:8], cb, op=ALU.mult), r=[tag + "y", "trig"], w=[tag + "t1"])
            V(lambda e: e.tensor_tensor(t2[:, 0:H, :], y[:, 0:H, 8:16], sb_, op=ALU.mult), r=[tag + "y", "trig"], w=[tag + "t2"])
            V(lambda e: e.tensor_tensor(dst[:, :, 0:8], t1[:, 0:H, :], t2[:, 0:H, :], op=ALU.subtract),
              r=[tag + "t1", tag + "t2"], w=wk)
            V(lambda e: e.tensor_tensor(t1[:, 0:H, :], y[:, 0:H, 8:16], cb, op=ALU.mult), r=[tag + "y", "trig"], w=[tag + "t1"])
            V(lambda e: e.tensor_tensor(t2[:, 0:H, :], y[:, 0:H, 0:8], sb_, op=ALU.mult), r=[tag + "y", "trig"], w=[tag + "t2"])
            V(lambda e: e.tensor_tensor(dst[:, :, 8:16], t1[:, 0:H, :], t2[:, 0:H, :], op=ALU.add),
              r=[tag + "t1", tag + "t2"], w=wk)
            V(lambda e: e.tensor_copy(dst[:, :, 16:64], y[:, 0:H, 16:64]), r=[tag + "y"], w=wk)

        class _Dup2:
            def __init__(self, tab, base):
                self.tab, self.base = tab, base

            def __getitem__(self, key):
                _, a, _ = key
                return self.tab[:, self.base + a // 2, :]

        def norm_rope_b(tmb, src, A_, B_, gain, cos3, sin3, dst, rk, wk, tag):
            sq, ss, y, t1, t2 = tmb[0:5]
            H = A_ * B_
            fl = lambda ap: ap
            A(lambda e: e.activation(sq[:, 0:H, :], fl(src), AF.Square), r=rk, w=[tag + "sq"])
            V(lambda e: e.tensor_reduce(ss[:, 0:H], sq[:, 0:H, :], axis=AX.X, op=ALU.add), r=[tag + "sq"], w=[tag + "ss"])
            V(lambda e: e.tensor_scalar(ss[:, 0:H], ss[:, 0:H], 1.0 / 64, EPS, op0=ALU.mult, op1=ALU.add), r=[tag + "ss"], w=[tag + "ss"])
            A(lambda e: e.activation(ss[:, 0:H], ss[:, 0:H], AF.Ln), r=[tag + "ss"], w=[tag + "ss"])
            A(lambda e: e.activation(ss[:, 0:H], ss[:, 0:H], AF.Exp, scale=-0.5), r=[tag + "ss"], w=[tag + "ss"])
            V(lambda e: e.tensor_tensor(y[:, 0:H, :], fl(src), ss[:, 0:H].unsqueeze(2).to_broadcast([128, H, 64]), op=ALU.mult),
              r=list(rk) + [tag + "ss"], w=[tag + "y"])
            V(lambda e: e.tensor_tensor(y[:, 0:H, :], y[:, 0:H, :], gain, op=ALU.mult), r=[tag + "y", "gains"], w=[tag + "y"])
            t3, t4 = tmb[5], tmb[6]
            sl = lambda tt, a: tt[:, a * B_:(a + 1) * B_, :]
            cbs = [cos3[:, a, :].unsqueeze(1).to_broadcast([128, B_, 8]) for a in range(A_)]
            sbs = [sin3[:, a, :].unsqueeze(1).to_broadcast([128, B_, 8]) for a in range(A_)]
            for a in range(A_):
                V(lambda e, a=a: e.tensor_tensor(sl(t1, a), sl(y, a)[:, :, 0:8], cbs[a], op=ALU.mult), r=[tag + "y", "trig"], w=[(tag + "t1", a)])
            for a in range(A_):
                V(lambda e, a=a: e.tensor_tensor(sl(t2, a), sl(y, a)[:, :, 8:16], sbs[a], op=ALU.mult), r=[tag + "y", "trig"], w=[(tag + "t2", a)])
            for a in range(A_):
                V(lambda e, a=a: e.tensor_tensor(sl(t3, a), sl(y, a)[:, :, 8:16], cbs[a], op=ALU.mult), r=[tag + "y", "trig"], w=[(tag + "t3", a)])
            for a in range(A_):
                V(lambda e, a=a: e.tensor_tensor(sl(t4, a), sl(y, a)[:, :, 0:8], sbs[a], op=ALU.mult), r=[tag + "y", "trig"], w=[(tag + "t4", a)])
            for a in range(A_):
                V(lambda e, a=a, da=dst(a): e.tensor_tensor(da[:, :, 0:8], sl(t1, a), sl(t2, a), op=ALU.subtract), r=[(tag + "t1", a), (tag + "t2", a)], w=wk)
            for a in range(A_):
                V(lambda e, a=a, da=dst(a): e.tensor_tensor(da[:, :, 8:16], sl(t3, a), sl(t4, a), op=ALU.add), r=[(tag + "t3", a), (tag + "t4", a)], w=wk)
            for a in range(A_):
                V(lambda e, a=a, da=dst(a): e.tensor_copy(da[:, :, 16:64], sl(y, a)[:, :, 16:64]), r=[tag + "y"], w=wk)

        def load_w_bf(dst3, src2, nk, key):
            h = (nk + 1) // 2
            v = src2.rearrange("(k p) n -> p k n", p=128)
            P.dma("gpsimd", dst3[:, 0:h, :], v[:, 0:h, :], w=[key + "A"], semkey=key + "A")
            if nk > h:
                P.dma("gpsimd", dst3[:, h:nk, :], v[:, h:nk, :], w=[key + "B"], semkey=key + "B")

        def wkey(key, k, nk):
            return key + ("A" if k < (nk + 1) // 2 else "B")

        def ffn_phase(tag, src_d, dst_d, nblk, wg_d, wu_d, wd_d, g_d, wg_pre=None):
            with ExitStack() as ph:
                wg = wg_pre if wg_pre is not None else sbt(ph, "wg", [128, 8, FF], BF)
                wu = sbt(ph, "wu", [128, 8, FF], BF)
                wd = sbt(ph, "wd", [128, NFC, D], BF)
                gbc = sbt(ph, "gbc", [128, D], F32)
                xt = sbt(ph, "xt", [128, 4, D], F32)
                hb = [sbt(ph, "hb%d" % j, [128, D], BF) for j in range(2)]
                hT = sbt(ph, "hT", [128, 8, 512], BF)
                aT = sbt(ph, "aT", [128, NFC, 512], BF)
                sg = [sbt(ph, "sg%d" % j, [128, 512], F32) for j in range(2)]
                junk = sbt(ph, "junk", [128, D], BF)
                ss = sbt(ph, "ss", [128, 4], F32)
                rstd = sbt(ph, "rstd", [128, 4], F32)
                if wg_pre is None:
                    load_w_bf(wg, wg_d, 8, "wg")
                load_w_bf(wu, wu_d, 8, "wu")
                load_w_bf(wd, wd_d, NFC, "wd")
                P.dma("sync", gbc[:], g_d, w=["gbc"], semkey="c0")
                for n in range(nblk):
                    if n == 0:
                        for j in range(4):
                            P.dma("sync", xt[:, j, :], src_d[(n * 4 + j) * 128:(n * 4 + j + 1) * 128, :], w=[("xt", j)], semkey=("xt", j))
                    rms_stats((junk, ss, rstd), [xt[:, j, :] for j in range(4)], [("xt", j) for j in range(4)], "f")
                    for j in range(4):
                        hbj = hb[j % 2]
                        V(lambda e, j=j, hbj=hbj: e.scalar_tensor_tensor(hbj[:], xt[:, j, :], rstd[:, j:j + 1], gbc[:], op0=ALU.mult, op1=ALU.mult),
                          r=[("xt", j), "frstd", "gbc"], w=[("hb", j % 2)])
                        for half in range(2):
                            bank = 4 + half
                            for kk in range(4):
                                kc = half * 4 + kk
                                TE(lambda e, hbj=hbj, kc=kc, kk=kk, bank=bank: e.matmul(PS[bank][:, kk * 128:(kk + 1) * 128], hbj[:, kc * 128:(kc + 1) * 128], ident[:], start=True, stop=True),
                                   r=[("hb", j % 2), "ident"], w=[psk(bank)])
                            evac_copy(j + half, hT[:, half * 4:half * 4 + 4, j * 128:(j + 1) * 128],
                                      PS[bank][:].rearrange("p (k t) -> p k t", k=4),
                                      r=[psk(bank)], w=[("hT", half, j)])
                    hT_keys = [("hT", h_, j_) for h_ in range(2) for j_ in range(4)]
                    for fc in range(NFC):
                        bg, bu = fc % 2, 2 + fc % 2
                        for kc in range(8):
                            TE(lambda e, fc=fc, kc=kc, bg=bg: e.matmul(PS[bg][:], wg[:, kc, fc * 128:(fc + 1) * 128], hT[:, kc, :], start=(kc == 0), stop=(kc == 7)),
                               r=hT_keys + [wkey("wg", kc, 8)], w=[psk(bg)])
                        for kc in range(8):
                            TE(lambda e, fc=fc, kc=kc, bu=bu: e.matmul(PS[bu][:], wu[:, kc, fc * 128:(fc + 1) * 128], hT[:, kc, :], start=(kc == 0), stop=(kc == 7)),
                               r=hT_keys + [wkey("wu", kc, 8)], w=[psk(bu)])
                        sgf = sg[fc % 2]
                        A(lambda e, sgf=sgf, bg=bg: e.activation(sgf[:], PS[bg][:], AF.Silu), r=[psk(bg)], w=[("sg", fc % 2)])
                        V(lambda e, sgf=sgf, bu=bu, fc=fc: e.tensor_tensor(aT[:, fc, :], sgf[:], PS[bu][:], op=ALU.mult),
                          r=[("sg", fc % 2), psk(bu)], w=[("aT", fc)])
                    for j in range(4):
                        for half in range(2):
                            bank = 6 + half
                            for fc in range(NFC):
                                TE(lambda e, fc=fc, j=j, half=half, bank=bank: e.matmul(PS[bank][:], aT[:, fc, j * 128:(j + 1) * 128], wd[:, fc, half * 512:(half + 1) * 512], start=(fc == 0), stop=(fc == NFC - 1)),
                                   r=[("aT", fc), wkey("wd", fc, NFC)], w=[psk(bank)])
                            V(lambda e, j=j, half=half, bank=bank: e.scalar_tensor_tensor(xt[:, j, half * 512:(half + 1) * 512], PS[bank][:], 0.5, xt[:, j, half * 512:(half + 1) * 512], op0=ALU.mult, op1=ALU.add),
                              r=[psk(bank), ("xt", j)], w=[("xt", j)])
                        P.dma("sync", dst_d[(n * 4 + j) * 128:(n * 4 + j + 1) * 128, :], xt[:, j, :], r=[("xt", j)], w=[("dst", n, j)], semkey=("st", j))
                        if n + 1 < nblk:
                            P.dma("sync", xt[:, j, :], src_d[((n + 1) * 4 + j) * 128:((n + 1) * 4 + j + 1) * 128, :], w=[("xt", j)], semkey=("xt", j))
                P.barrier()
                P.flush()

        _dn = [0]

        def dump(name, src, shape, dt=F32, r=()):
            if not debug:
                return
            d = dbg_out(name, shape, dt)
            _dn[0] += 1
            P.dma("sync", d, src, r=r, semkey=("dbg", _dn[0] % 4))

        def dump_done():
            if debug:
                P.barrier()
                P.flush()

        if not quick:
            ffn_phase("f1", IN["x"], x1_all, 16, IN["wg1"], IN["wu1"], IN["wd1"], IN["g_ffn1"])
        else:
            P.dma("sync", x1_all[0:2048, :], IN["x"][0:2048, :], semkey="c0")
            P.barrier()
            P.flush()
        dump("x1", x1_all[0:1024, :], [1024, D])
        dump("x1b", x1_all[T - 512:T, :], [512, D])
        dump("cosT", cosT[:].rearrange("p a b -> p (a b)"), [128, 800])
        dump("sinT", sinT[:].rearrange("p a b -> p (a b)"), [128, 800])
        dump_done()
        if stop_after <= 1:
            return nc, DBG, None

        with ExitStack() as phkv:
          kcT = sbt(phkv, "kcT", [128, T + 16], BF)
          vcT = sbt(phkv, "vcT", [128, T + 16], BF)
          gq = sbt(phkv, "gq", [128, 64], F32)
          gkc = sbt(phkv, "gkc", [128, 64], F32)
          gksw = sbt(phkv, "gksw", [128, 2, 64], F32)
          tm = (sbt(phkv, "nsq", [128, 8, 64], F32), sbt(phkv, "nss", [128, 8], F32), sbt(phkv, "ny", [128, 8, 64], F32),
                sbt(phkv, "nt1", [128, 8, 8], F32), sbt(phkv, "nt2", [128, 8, 8], F32))
          with ExitStack() as ph:
            win = sbt(ph, "win", [128, 8, 2840], BF)
            gbc = sbt(ph, "gbc", [128, D], F32)
            xt = sbt(ph, "xt", [128, 4, D], F32)
            hb = [sbt(ph, "hb%d" % j, [128, D], BF) for j in range(4)]
            hbo = [sbt(ph, "hbo%d" % j, [128, D], BF) for j in range(2)]
            hT = sbt(ph, "hT", [128, 8, 512], BF)
            hoT = sbt(ph, "hoT", [128, 8, 256], BF)
            x1o = [sbt(ph, "x1o%d" % j, [128, D], F32) for j in range(2)]
            junk = sbt(ph, "junk", [128, D], BF)
            ss4 = sbt(ph, "ss4", [128, 4], F32)
            rstd = sbt(ph, "rstd", [128, 4], F32)
            kstg = sbt(ph, "kstg", [128, 2, 2, 64], BF)
            kTs = sbt(ph, "kTs", [128, 2, 512], BF)
            vstg = sbt(ph, "vstg", [128, 4, 2, 2, 65], BF)
            ksbs = sbt(ph, "ksbs", [128, 4, 512], BF)
            vsbs = sbt(ph, "vsbs", [128, 4, 512], BF)
            qstg = sbt(ph, "qstg", [128, 2, 4, 2, 64], BF)
            kraw = sbt(ph, "kraw", [128, 4, 4, 64], F32)
            kst16 = sbt(ph, "kst16", [128, 4, 4, 64], BF)
            qraw = sbt(ph, "qraw", [128, 2, 8, 64], F32)
            gk16 = sbt(ph, "gk16", [128, 4, 4, 64], F32)
            gq16 = sbt(ph, "gq16", [128, 16, 64], F32)
            graw = sbt(ph, "graw", [128, 2, 24], F32)
            tmb = (sbt(ph, "bsq", [128, 16, 64], F32), sbt(ph, "bss", [128, 16], F32), sbt(ph, "by", [128, 16, 64], F32),
                   sbt(ph, "bt1", [128, 16, 8], F32), sbt(ph, "bt2", [128, 16, 8], F32),
                   sbt(ph, "bt3", [128, 16, 8], F32), sbt(ph, "bt4", [128, 16, 8], F32))
            qnTs = sbt(ph, "qnTs", [128, 2, 512], BF)
            qsbs = sbt(ph, "qsbs", [128, 4, 256], BF)
            load_w_bf(win, IN["w_in"], 8, "win")
            P.dma("sync", gbc[:], IN["g_mix"], w=["gbc"], semkey="c0")
            P.dma("sync", gq[:], IN["g_q"], w=["gains"], semkey="c1")
            P.dma("sync", gkc[:], IN["g_kc"], w=["gains"], semkey="c2")
            P.dma("sync", gksw[:].rearrange("p a d -> p (a d)"), IN["g_ksw"], w=["gains"], semkey="c3")
            V(lambda e: e.memset(vstg[:], 1.0), w=["vstg"])
            for j_ in range(0 if os.environ.get("NOG16") else 4):
                for gi_ in range(2):
                    V(lambda e, j_=j_, gi_=gi_: e.tensor_copy(gk16[:, j_, 2 * gi_:2 * gi_ + 2, :], gksw[:, gi_, :].unsqueeze(1).to_broadcast([128, 2, 64])), r=["gains"], w=["gk16"])
            V(lambda e: e.tensor_copy(gq16[:], gq[:].unsqueeze(1).to_broadcast([128, 16, 64])), r=["gains"], w=["gq16"])
            V(lambda e: e.memset(kcT[:, T:T + 16], 0.0), w=["kcTpad"])
            V(lambda e: e.memset(vcT[:, T:T + 16], 0.0), w=["vcTpad"])
            if quick:
                V(lambda e: e.memset(kcT[:, 2048:T], 0.0), w=["kcTq"])
                V(lambda e: e.memset(vcT[:, 2048:T], 0.0), w=["vcTq"])
            win_keys = ["winA", "winB"]
            nblk1b = int(quick) if quick else 16
            for n in range(nblk1b):
                if n == 0:
                    for j in range(4):
                        P.dma("sync", xt[:, j, :], x1_all[(n * 4 + j) * 128:(n * 4 + j + 1) * 128, :], w=[("xt", j)], semkey=("xt", j))
                rms_stats((junk, ss4, rstd), [xt[:, j, :] for j in range(4)], [("xt", j) for j in range(4)], "m")
                for j in range(4):
                    V(lambda e, j=j: e.scalar_tensor_tensor(hb[j][:], xt[:, j, :], rstd[:, j:j + 1], gbc[:], op0=ALU.mult, op1=ALU.mult),
                      r=[("xt", j), "mrstd", "gbc"], w=[("hb", j)])
                    for half in range(2):
                        bank = 4 + half
                        for kk in range(4):
                            kc = half * 4 + kk
                            TE(lambda e, j=j, kc=kc, kk=kk, bank=bank: e.matmul(PS[bank][:, kk * 128:(kk + 1) * 128], hb[j][:, kc * 128:(kc + 1) * 128], ident[:], start=True, stop=True),
                               r=[("hb", j), "ident"], w=[psk(bank)])
                        evac_copy(j + half, hT[:, half * 4:half * 4 + 4, j * 128:(j + 1) * 128],
                                  PS[bank][:].rearrange("p (k t) -> p k t", k=4), r=[psk(bank)], w=[("hT", half, j)])
                hT_keys = [("hT", h_, j_) for h_ in range(2) for j_ in range(4)]
                for q in range(2):
                    i_own = n * 2 + q
                    V(lambda e, q=q: e.tensor_scalar(x1o[q][:], xt[:, 2 * q, :], sab[:, 0:1], None, op0=ALU.mult),
                      r=[("xt", 2 * q), "sab"], w=[("x1o", q)])
                    V(lambda e, q=q: e.scalar_tensor_tensor(x1o[q][:], xt[:, 2 * q + 1, :], sab[:, 1:2], x1o[q][:], op0=ALU.mult, op1=ALU.add),
                      r=[("xt", 2 * q + 1), "sab", ("x1o", q)], w=[("x1o", q)])
                    P.dma("sync", x1o_d[i_own * 128:(i_own + 1) * 128, :], x1o[q][:], r=[("x1o", q)], w=[("x1od", i_own)], semkey=("x1o", q))
                    V(lambda e, q=q: e.tensor_scalar(hbo[q][:], hb[2 * q][:], sab[:, 0:1], None, op0=ALU.mult),
                      r=[("hb", 2 * q), "sab"], w=[("hbo", q)])
                    V(lambda e, q=q: e.scalar_tensor_tensor(hbo[q][:], hb[2 * q + 1][:], sab[:, 1:2], hbo[q][:], op0=ALU.mult, op1=ALU.add),
                      r=[("hb", 2 * q + 1), "sab", ("hbo", q)], w=[("hbo", q)])
                    for half in range(2):
                        bank = 6 + half
                        for kk in range(4):
                            kc = half * 4 + kk
                            TE(lambda e, q=q, kc=kc, kk=kk, bank=bank: e.matmul(PS[bank][:, kk * 128:(kk + 1) * 128], hbo[q][:, kc * 128:(kc + 1) * 128], ident[:], start=True, stop=True),
                               r=[("hbo", q), "ident"], w=[psk(bank)])
                        evac_copy(q + half, hoT[:, half * 4:half * 4 + 4, q * 128:(q + 1) * 128],
                                  PS[bank][:].rearrange("p (k t) -> p k t", k=4), r=[psk(bank)], w=[("hoT", half, q)])
                hoT_keys = [("hoT", h_, q_) for h_ in range(2) for q_ in range(2)]
                if n + 1 < nblk1b:
                    for j in range(4):
                        P.dma("sync", xt[:, j, :], x1_all[((n + 1) * 4 + j) * 128:((n + 1) * 4 + j + 1) * 128, :], w=[("xt", j)], semkey=("xt", j))
                tok0 = n * 512
                for which, col, dstT in ((0, C_KC, kcT), (1, C_VC, vcT)):
                    bank = which
                    for kc in range(8):
                        TE(lambda e, kc=kc, col=col, bank=bank: e.matmul(PS[bank][:], win[:, kc, col:col + 128], hT[:, kc, :], start=(kc == 0), stop=(kc == 7)),
                           r=hT_keys + win_keys, w=[psk(bank)])
                    evac_copy(which, dstT[:, tok0:tok0 + 512], PS[bank][:], r=[psk(bank)], w=[("cT", which, n)])
                for pr in range(4):
                    bank = 2 + pr % 2
                    for kc in range(8):
                        TE(lambda e, kc=kc, pr=pr, bank=bank: e.matmul(PS[bank][:], win[:, kc, C_KSB + pr * 128:C_KSB + (pr + 1) * 128], hT[:, kc, :], start=(kc == 0), stop=(kc == 7)),
                           r=hT_keys + win_keys, w=[psk(bank)])
                    evac_copy(pr, ksbs[:, pr, :], PS[bank][:], r=[psk(bank)], w=["ksbs"])
                P.dma("sync", ksbT_d[:, :, tok0:tok0 + 512], ksbs[:], r=["ksbs"], w=[("ksbd", n)], semkey="ksbs")
                for j in range(4):
                    bank = j % 2
                    for kc in range(8):
                        TE(lambda e, kc=kc, j=j, bank=bank: e.matmul(PS[bank][:], hT[:, kc, j * 128:(j + 1) * 128], win[:, kc, C_KS:C_KS + 512], start=(kc == 0), stop=(kc == 7)),
                           r=hT_keys + win_keys, w=[psk(bank)])
                    psv = PS[bank][:].rearrange("p (a h d) -> p a h d", a=4, h=2)
                    A(lambda e, j=j, psv=psv: e.copy(vstg[:, j, 0, :, 0:64], psv[:, 1, :, :]), r=[psk(bank)], w=["vstg"])
                    A(lambda e, j=j, psv=psv: e.copy(vstg[:, j, 1, :, 0:64], psv[:, 3, :, :]), r=[psk(bank)], w=["vstg"])
                    if not os.environ.get("NOKRAW"):
                        V(lambda e, j=j, bank=bank: e.tensor_copy(kraw[:, j, 0:2, :].rearrange("p h d -> p (h d)"), PS[bank][:, 0:128]), r=[psk(bank)], w=[("kraw", j)])
                        V(lambda e, j=j, bank=bank: e.tensor_copy(kraw[:, j, 2:4, :].rearrange("p h d -> p (h d)"), PS[bank][:, 256:384]), r=[psk(bank), ("kraw", j)], w=[("kraw", j)])
                    bank2 = 6 + j % 2
                    for kc in range(8):
                        TE(lambda e, kc=kc, j=j, bank2=bank2: e.matmul(PS[bank2][:], hT[:, kc, j * 128:(j + 1) * 128], win[:, kc, C_VSB:C_VSB + 512], start=(kc == 0), stop=(kc == 7)),
                           r=hT_keys + win_keys, w=[psk(bank2)])
                    evac_copy(j, vsbs[:, j, :], PS[bank2][:], r=[psk(bank2)], w=["vsbs"])
                if os.environ.get("KSKIP"):
                    continue
                norm_rope_b(tmb, kraw[:].rearrange("p a b d -> p (a b) d"), 4, 4, gk16[:].rearrange("p a b d -> p (a b) d"), cosT[:, n * 4:n * 4 + 4, :], sinT[:, n * 4:n * 4 + 4, :],
                            lambda a: kst16[:, a, :, :], [("kraw", j_) for j_ in range(4)], ["kst16"], "b")
                for j in range(4):
                    for gi in range(2):
                        TE(lambda e, gi=gi, j=j: e.matmul(PS[2 + gi][:, j * 128:(j + 1) * 128], kst16[:, j, 2 * gi:2 * gi + 2, :].rearrange("p h d -> p (h d)"), ident[:], start=True, stop=True),
                           r=["kst16", "ident"], w=[psk(2 + gi)])
                for gi in range(2):
                    evac_copy(gi, kTs[:, gi, :], PS[2 + gi][:], r=[psk(2 + gi)], w=["kTs"])
                P.dma("sync", ksT_d[:, tok0:tok0 + 512], kTs[:, 0, :], r=["kTs"], w=[("ksd", n)], semkey="kTs")
                P.dma("sync", kwT_d[:, tok0:tok0 + 512], kTs[:, 1, :], r=["kTs"], w=[("kwd", n)], semkey="kTs2")
                P.dma("sync", vs_d[tok0:tok0 + 512, :].rearrange("(j p) c -> p j c", p=128), vstg[:, :, 0, :, :].rearrange("p j h c -> p j (h c)"),
                      r=["vstg"], w=[("vsd", n)], semkey="vstg")
                P.dma("sync", vw_d[tok0:tok0 + 512, :].rearrange("(j p) c -> p j c", p=128), vstg[:, :, 1, :, :].rearrange("p j h c -> p j (h c)"),
                      r=["vstg"], w=[("vwd", n)], semkey="vstg2")
                P.dma("sync", vsb_d[tok0:tok0 + 512, :].rearrange("(j p) c -> p j c", p=128), vsbs[:], r=["vsbs"], w=[("vsbd", n)], semkey="vsbs")
                if os.environ.get("QSKIP"):
                    continue
                for q in range(2):
                    bank = 4 + q
                    for kc in range(8):
                        TE(lambda e, kc=kc, q=q, bank=bank: e.matmul(PS[bank][:], hoT[:, kc, q * 128:(q + 1) * 128], win[:, kc, C_QN:C_QN + 512], start=(kc == 0), stop=(kc == 7)),
                           r=hoT_keys + win_keys, w=[psk(bank)])
                    A(lambda e, q=q, bank=bank: e.copy(qraw[:, q, :, :].rearrange("p h d -> p (h d)"), PS[bank][:]), r=[psk(bank)], w=[("qraw", q)])
                    for kc in range(8):
                        TE(lambda e, kc=kc, q=q: e.matmul(PS[1][:, q * 32:q * 32 + 24], hoT[:, kc, q * 128:(q + 1) * 128], win[:, kc, C_GT:C_GT + 24], start=(kc == 0), stop=(kc == 7)),
                           r=hoT_keys + win_keys, w=[psk(1)])
                cq = cosT[:, 64 + n * 2:64 + n * 2 + 2, :].unsqueeze(2).to_broadcast([128, 2, 2, 8]).rearrange("p q k d -> p (q k) d") if False else None
                norm_rope_b(tmb, qraw[:].rearrange("p q h d -> p (q h) d"), 4, 4, gq16[:], _Dup2(cosT, 64 + n * 2), _Dup2(sinT, 64 + n * 2),
                            lambda a: qstg[:, a // 2, :, a % 2, :], [("qraw", 0), ("qraw", 1)], ["qstg"], "b")
                for q in range(2):
                    for g in range(4):
                        TE(lambda e, g=g, q=q: e.matmul(PS[0][:, g * 128:(g + 1) * 128], qstg[:, q, g, :, :].rearrange("p k d -> p (k d)"), ident[:], start=True, stop=True),
                           r=["qstg", "ident"], w=[psk(0)])
                    evac_copy(q, qnTs[:, q, :], PS[0][:], r=[psk(0)], w=["qnTs"])
                for q in range(2):
                    A(lambda e, q=q: e.activation(graw[:, q, :], PS[1][:, q * 32:q * 32 + 24], AF.Exp, scale=-1.0), r=[psk(1)], w=["graw"])
                V(lambda e: e.tensor_scalar(graw[:], graw[:], 1.0, None, op0=ALU.add), r=["graw"], w=["graw"])
                V(lambda e, n=n: e.reciprocal(gates[:, n * 2:n * 2 + 2, :], graw[:]), r=["graw"], w=["gates"])
                P.dma("sync", qnT_d[:, n * 2:n * 2 + 2, :], qnTs[:], r=["qnTs"], w=[("qnd", n)], semkey="qnTs")
                for pr in range(4):
                    bank = 2 + pr % 2
                    for kc in range(8):
                        TE(lambda e, kc=kc, pr=pr, bank=bank: e.matmul(PS[bank][:, 0:256], win[:, kc, C_QS + pr * 128:C_QS + (pr + 1) * 128], hoT[:, kc, :], start=(kc == 0), stop=(kc == 7)),
                           r=hoT_keys + win_keys, w=[psk(bank)])
                    V(lambda e, pr=pr, bank=bank: e.tensor_scalar(qsbs[:, pr, :], PS[bank][:, 0:256], 0.125, None, op0=ALU.mult), r=[psk(bank)], w=["qsbs"])
                P.dma("sync", qsbT_d[:, :, n * 256:(n + 1) * 256], qsbs[:], r=["qsbs"], w=[("qsbd", n)], semkey="qsbs")
            P.barrier()
            P.flush()

          if True:
            with ExitStack() as pc:
                w1 = sbt(pc, "w1", [128, 32, 256], BF)
                w2 = sbt(pc, "w2", [128, 2, 64], BF)
                peT = sbt(pc, "peT", [128, 32], F32)
                kpe = sbt(pc, "kpe", [128, 32, 512], BF)
                gel = sbt(pc, "gel", [128, 2, 2, 512], BF)
                gx2 = sbt(pc, "gx2", [128, 512], F32)
                gu = sbt(pc, "gu", [128, 512], F32)
                cstg = sbt(pc, "cstg", [128, 2, 64], BF)
                wovs = sbt(pc, "wovs", [128, 4, 128], BF)
                P.dma("sync", wovs[:].rearrange("p c j -> p (c j)"), IN["wov"], w=["wovs"], semkey="c0")
                V(lambda e: e.memset(WVC[:], 1.0), w=["WVC"])
                for hk in range(2):
                    V(lambda e, hk=hk: e.tensor_copy(WVC[:, :, hk, 0:128], wovs[:]), r=["wovs", "WVC"], w=["WVC"])
                for which, (w1_d, w2_d, pe_d, srcT) in enumerate(((IN["cw1k"], IN["cw2k"], IN["pekT"], kcT), (IN["cw1v"], IN["cw2v"], IN["pevT"], vcT))):
                    w1v = w1_d.rearrange("(r d) n -> d r n", d=64)
                    P.dma("gpsimd", w1[0:64, :, :], w1v, w=["w1a"], semkey="w1a")
                    P.dma("gpsimd", w1[64:128, :, :], w1v, w=["w1b"], semkey="w1b")
                    P.dma("gpsimd", w2[:], w2_d.rearrange("(c p) d -> p c d", p=128), w=["w2"], semkey="w2")
                    P.dma("sync", peT[0:64, :], pe_d, w=["peTa"], semkey="c1")
                    P.dma("sync", peT[64:128, :], pe_d, w=["peTb"], semkey="c2")
                    src_win = bass.AP(srcT, 0, [[T + 16, 128], [1, 32], [16, 512]])
                    V(lambda e, src_win=src_win: e.tensor_tensor(kpe[:], src_win, peT[:].unsqueeze(2).to_broadcast([128, 32, 512]), op=ALU.add),
                      r=["peTa", "peTb"], w=["kpe"])
                    for hk in range(2):
                        lo, hi = hk * 64, hk * 64 + 64
                        for hc in range(2):
                            bank = (hk * 2 + hc) % 4
                            for r_ in range(32):
                                TE(lambda e, r_=r_, hc=hc, lo=lo, hi=hi, bank=bank: e.matmul(PS[bank][:], w1[lo:hi, r_, hc * 128:(hc + 1) * 128], kpe[lo:hi, r_, :], start=(r_ == 0), stop=(r_ == 31)),
                                   r=["w1a", "w1b", "kpe"], w=[psk(bank)])
                            A(lambda e, bank=bank: e.activation(gx2[:], PS[bank][:], AF.Square), r=[psk(bank)], w=["gx2"])
                            V(lambda e: e.tensor_scalar(gx2[:], gx2[:], 0.044715, 1.0, op0=ALU.mult, op1=ALU.add), r=["gx2"], w=["gx2"])
                            V(lambda e, bank=bank: e.tensor_tensor(gu[:], gx2[:], PS[bank][:], op=ALU.mult), r=["gx2", psk(bank)], w=["gu"])
                            A(lambda e: e.activation(gu[:], gu[:], AF.Tanh, scale=0.7978845608028654), r=["gu"], w=["gu"])
                            V(lambda e: e.tensor_scalar(gu[:], gu[:], 0.5, 0.5, op0=ALU.mult, op1=ALU.add), r=["gu"], w=["gu"])
                            V(lambda e, hk=hk, hc=hc, bank=bank: e.tensor_tensor(gel[:, hk, hc, :], gu[:], PS[bank][:], op=ALU.mult),
                              r=["gu", psk(bank)], w=[("gel", hk, hc)])
                    for cc in range(4):
                        bank = 4 + cc % 2
                        for hk in range(2):
                            for hc in range(2):
                                TE(lambda e, cc=cc, hk=hk, hc=hc, bank=bank: e.matmul(PS[bank][:, hk * 64:(hk + 1) * 64], gel[:, hk, hc, cc * 128:(cc + 1) * 128], w2[:, hc, :], start=(hc == 0), stop=(hc == 1)),
                                   r=[("gel", hk, hc), "w2"], w=[psk(bank)])
                        pv = PS[bank][:, 0:128].rearrange("p (h d) -> p h d", h=2)
                        if which == 0:
                            head_norm_rope(tm, pv, 2, gkc[:].unsqueeze(1).to_broadcast([128, 2, 64]), cs_cmp(cc), cstg[:], [psk(bank)], ["cstg"], "k")
                            TE(lambda e, cc=cc: e.matmul(PS[6][:, cc * 128:(cc + 1) * 128], cstg[:].rearrange("p h d -> p (h d)"), ident[:], start=True, stop=True),
                               r=["cstg", "ident"], w=[psk(6)])
                        else:
                            V(lambda e, cc=cc, pv=pv: e.tensor_copy(WVC[:, cc, :, 128:192], pv), r=[psk(bank), "WVC"], w=["WVC"])
                    if which == 0:
                        V(lambda e: e.tensor_copy(KCT[:], PS[6][:]), r=[psk(6)], w=["KCT"])
                P.barrier()
                P.flush()
        dump("ksT", ksT_d, [128, T], BF)
        dump("kwT", kwT_d, [128, T], BF)
        dump("vs", vs_d, [T, 130], BF)
        dump("vw", vw_d, [T, 130], BF)
        dump("ksbT", ksbT_d.rearrange("p a t -> p (a t)"), [128, 4 * T], BF)
        dump("vsb", vsb_d, [T, 512], BF)
        dump("qsbT", qsbT_d.rearrange("p a t -> p (a t)"), [128, 4 * NOWN * 128], BF)
        dump("qnT", qnT_d.rearrange("p a t -> p (a t)"), [128, NOWN * 512], BF)
        dump("x1o", x1o_d[0:512, :], [512, D])
        dump("gates", gates[:].rearrange("p a b -> p (a b)"), [128, NOWN * 24])
        dump("KCT", KCT[:], [128, 512], BF)
        dump("WVC", WVC[:].rearrange("p a b c -> p (a b c)"), [128, 1600], BF)
        dump_done()
        if stop_after <= 2:
            return nc, DBG, None

        with ExitStack() as ph:
            KSA = [sbt(ph, "KSA0", [128, T], BF), sbt(ph, "KSA1", [128, T], BF)]
            RQ = sbt(ph, "RQ", [128, 2, 512], BF)
            KWT = sbt(ph, "KWT", [128, T], BF)
            VS = sbt(ph, "VS", [128, NT, 130], BF)
            VW = sbt(ph, "VW", [128, NT, 130], BF)
            QNT = sbt(ph, "QNT", [128, NOWN, 512], BF)
            mkc = sbt(ph, "mkc", [128, 8, 128], BF)
            cms = sbt(ph, "cms", [128, 2, 128], BF)
            wms = sbt(ph, "wms", [128, 6, 128], BF)
            cms4 = sbt(ph, "cms4", [128, 2, 4, 128], BF)
            wms4 = sbt(ph, "wms4", [128, 6, 4, 128], BF)
            A2t = [sbt(ph, "A2t%d" % k, [128, 128], F32) for k in range(2)]
            A13t = [sbt(ph, "A13t%d" % k, [128, 128], F32) for k in range(2)]
            Eb = [sbt(ph, "Eb%d" % k, [128, 512], BF) for k in range(3)]
            imp = sbt(ph, "imp", [128, 128], F32)
            sc = sbt(ph, "sc", [128, 128], F32)
            sc2 = sbt(ph, "sc2", [128, 128], F32)
            m8 = sbt(ph, "m8", [128, 8], F32)
            m8b = sbt(ph, "m8b", [128, 8], F32)
            nsel = sbt(ph, "nsel", [128, 128], BF)
            nselT = sbt(ph, "nselT", [128, 4, 128], BF)
            den = sbt(ph, "den", [128, 16], F32)
            oacc = sbt(ph, "oacc", [128, 2, 4, 64], F32)
            obf = sbt(ph, "obf", [128, 512], BF)
            oTs = sbt(ph, "oTs", [128, 4, 128], BF)
            P.dma("sync", QNT[:], qnT_d, w=["QNT"], semkey=("l", 4))
            P.dma("sync", mkc[:].rearrange("p a b -> p (a b)"), IN["maskc"], w=["mkc"], semkey=("l", 6))
            P.dma("sync", cms[:].rearrange("p a b -> p (a b)"), IN["cmsel"], w=["cms"], semkey=("l", 7))
            P.dma("sync", wms[:].rearrange("p a b -> p (a b)"), IN["wmask"], w=["wms"], semkey=("l", 8))
            P.dma("sync", KSA[0][0:64, :], ksT_d[0:64, :], w=[("KSA", 0)], semkey=("l", 0))
            P.dma("sync", KSA[0][64:128, :], IN["expd2"], w=[("KSA", 0)], semkey=("l", 0))
            P.dma("sync", KWT[:], kwT_d, w=["KWT"], semkey=("l", 1))
            for q8 in range(8):
                P.dma("sync", VS[:, q8 * 8:(q8 + 1) * 8, :], vs_d[q8 * 1024:(q8 + 1) * 1024, :].rearrange("(j p) c -> p j c", p=128), w=[("VS", q8)], semkey=("lv", q8 % 4))
                P.dma("sync", VW[:, q8 * 8:(q8 + 1) * 8, :], vw_d[q8 * 1024:(q8 + 1) * 1024, :].rearrange("(j p) c -> p j c", p=128), w=[("VW", q8)], semkey=("lw", q8 % 4))
                if q8 == 0:
                    P.dma("sync", KSA[1][64:128, :], ksT_d[64:128, :], w=[("KSA", 1)], semkey=("l", 5))
                    P.dma("sync", KSA[1][0:64, :], IN["expd2"], w=[("KSA", 1)], semkey=("l", 5))
            V(lambda e: e.tensor_copy(cms4[:], cms[:].unsqueeze(2).to_broadcast([128, 2, 4, 128])), r=["cms"], w=["cms4"])
            V(lambda e: e.tensor_copy(wms4[:], wms[:].unsqueeze(2).to_broadcast([128, 6, 4, 128])), r=["wms"], w=["wms4"])
            PSC = [PS[2][:].rearrange("p (g c) -> p g c", g=2), PS[3][:].rearrange("p (g c) -> p g c", g=2)]

            def pcmp(g):
                return PSC[g // 2][:, g % 2, :]

            PSS = PS[4][:].rearrange("p (g c) -> p g c", g=4)
            PSW = PS[5][:].rearrange("p (g c) -> p g c", g=4)
            ecnt = [0]

            def stream(tiles, q_rhs, pv_rhs_fn, pacc_fn, ncol, tag, firsts=(0,)):
                n = len(tiles)

                SB3 = (0, 1, 6)

                def emit_score(t):
                    b = SB3[t % 3]
                    mm = tiles[t]["score"]
                    for k_, (lh, rh, keys) in enumerate(mm):
                        TE(lambda e, lh=lh, rh=rh, b=b, k_=k_, L=len(mm): e.matmul(PS[b][:], lh, rh, start=(k_ == 0), stop=(k_ == L - 1)),
                           r=keys, w=[psk(b)])

                emit_score(0)
                if n > 1:
                    emit_score(1)
                for t in range(n):
                    if t + 2 < n:
                        emit_score(t + 2)
                    b = SB3[t % 3]
                    ei = ecnt[0] % 3
                    ecnt[0] += 1
                    E = Eb[ei]
                    A(lambda e, E=E, b=b: e.activation(E[:], PS[b][:], AF.Exp, scale=0.125), r=[psk(b)], w=[("E", ei)])
                    mk = tiles[t].get("mask")
                    if mk is not None:
                        V(lambda e, E=E, mk=mk: e.tensor_tensor(E[:].rearrange("p (g q) -> p g q", g=4), E[:].rearrange("p (g q) -> p g q", g=4), mk, op=ALU.mult),
                          r=[("E", ei), "mkc"], w=[("E", ei)])
                    rhs, rkeys = pv_rhs_fn(t)
                    for g in range(4):
                        TE(lambda e, E=E, g=g, rhs=rhs, t=t: e.matmul(pacc_fn(g)[:, 0:ncol], E[:, g * 128:(g + 1) * 128], rhs, start=(t == 0 and g in firsts), stop=(t == n - 1)),
                           r=[("E", ei)] + rkeys, w=[(tag, g)])

            for i in range(1 if quick else NOWN):
                k2 = i % 2
                P.dma("sync", A2t[k2][:], IN["A2"][i], w=[("A2t", k2)], semkey=("A2t", k2))
                P.dma("sync", A13t[k2][:], IN["A13"][i], w=[("A13t", k2)], semkey=("A13t", k2))
                for hk in range(2):
                    lo, hi = hk * 64, hk * 64 + 64
                    qr = QNT[lo:hi, i, :]
                    ncc = i // 8 + 1
                    tiles = []
                    for cc in range(ncc):
                        tiles.append({"score": [(KCT[lo:hi, cc * 128:(cc + 1) * 128], qr, ["KCT", "QNT"])],
                                      "mask": (mkc[:, i % 8, :].unsqueeze(1).to_broadcast([128, 4, 128]) if cc == ncc - 1 else None)})
                    stream(tiles, qr, lambda t, hk=hk: (WVC[:, t, hk, 0:193], ["WVC"]), pcmp, 193, "pc", firsts=(0, 2))
                    for g in range(4):
                        pc = pcmp(g)
                        gi = (hk * 4 + g) * 3
                        V(lambda e, pc=pc, g=g: e.tensor_scalar(den[:, g:g + 1], pc[:, 192:193], 1e-30, None, op0=ALU.max), r=[("pc", g)], w=[("den", g)])
                        V(lambda e, g=g: e.reciprocal(den[:, g:g + 1], den[:, g:g + 1]), r=[("den", g)], w=[("den", g)])
                        if g == 0:
                            V(lambda e, pc=pc, g=g: e.tensor_scalar(imp[:], pc[:, 0:128], den[:, g:g + 1], None, op0=ALU.mult), r=[("pc", g), ("den", g)], w=["imp"])
                        else:
                            V(lambda e, pc=pc, g=g: e.scalar_tensor_tensor(imp[:], pc[:, 0:128], den[:, g:g + 1], imp[:], op0=ALU.mult, op1=ALU.add),
                              r=[("pc", g), ("den", g), "imp"], w=["imp"])
                        V(lambda e, g=g, gi=gi, i=i: e.tensor_tensor(den[:, 4 + g:5 + g], den[:, g:g + 1], gates[:, i, gi:gi + 1], op=ALU.mult), r=[("den", g), "gates"], w=[("den", 4 + g)])
                        V(lambda e, pc=pc, g=g, hk=hk: e.tensor_scalar(oacc[:, hk, g, :], pc[:, 128:192], den[:, 4 + g:5 + g], None, op0=ALU.mult),
                          r=[("pc", g), ("den", 4 + g)], w=[("oacc", hk, g)])
                    V(lambda e, k2=k2: e.tensor_tensor(sc[:], imp[:], A2t[k2][:], op=ALU.mult), r=["imp", ("A2t", k2)], w=["sc"])
                    V(lambda e, k2=k2: e.tensor_tensor(sc[:], sc[:], A13t[k2][:], op=ALU.add), r=["sc", ("A13t", k2)], w=["sc"])
                    V(lambda e: e.max(out=m8[:], in_=sc[:]), r=["sc"], w=["m8"])
                    V(lambda e: e.match_replace(out=sc2[:], in_to_replace=m8[:], in_values=sc[:], imm_value=-2.0), r=["m8", "sc"], w=["sc2"])
                    V(lambda e: e.max(out=m8b[:], in_=sc2[:]), r=["sc2"], w=["m8b"])
                    V(lambda e: e.tensor_scalar(sc2[:], sc[:], m8b[:, 7:8], None, op0=ALU.is_ge), r=["sc", "m8b"], w=["sc2"])
                    V(lambda e: e.tensor_scalar(nsel[:], sc2[:], -1.0, BIG, op0=ALU.add, op1=ALU.mult), r=["sc2"], w=["nsel"])
                    tiles = []
                    kts = []
                    for rel in range(6):
                        kt = 2 * i - 4 + rel
                        if kt < 0:
                            continue
                        kts.append(kt)
                        tiles.append({"score": [(KWT[lo:hi, kt * 128:(kt + 1) * 128], qr, ["KWT", "QNT"]),
                                                (ident[:], wms4[:, rel, :, :].rearrange("p g q -> p (g q)"), ["ident", "wms4"])]})
                    stream(tiles, qr, lambda t, hk=hk, kts=kts: (VW[:, kts[t], hk * 65:(hk + 1) * 65], [("VW", kts[t] // 8)]), lambda g: PSW[:, g, :], 65, "pwin")
                    olo = 64 - lo
                    for h2 in range(2):
                        TE(lambda e, h2=h2, olo=olo: e.matmul(PS[7][olo:olo + 64, h2 * 128:(h2 + 1) * 128], nsel[:, h2 * 64:(h2 + 1) * 64], ident[:], start=True, stop=True),
                           r=["nsel", "ident"], w=[psk(7)])
                    V(lambda e, lo=lo, hi=hi, i=i: e.tensor_copy(RQ[lo:hi, :, :], QNT[lo:hi, i, :].unsqueeze(1).to_broadcast([64, 2, 512])), r=["QNT"], w=["RQ"])
                    for h2 in range(2):
                        V(lambda e, h2=h2, olo=olo: e.tensor_copy(RQ[olo:olo + 64, h2, :].rearrange("p (g q) -> p g q", g=4), PS[7][olo:olo + 64, h2 * 128:(h2 + 1) * 128].unsqueeze(1).to_broadcast([64, 4, 128])),
                          r=[psk(7), "RQ"], w=["RQ"])
                    if debug and i in (0, 1, 4, 15, 31):
                        dump("sel_%d_%d" % (i, hk), sc2[:], [128, 128], r=["sc2"])
                        dump("imp_%d_%d" % (i, hk), imp[:], [128, 128], r=["imp"])
                    for g in range(4):
                        gi = (hk * 4 + g) * 3 + 1 + 1
                        c0 = 8 + 1 * 4 + g
                        V(lambda e, g=g, c0=c0: e.tensor_scalar(den[:, c0:c0 + 1], PSW[:, g, 64:65], 1e-30, None, op0=ALU.max), r=[("pwin", g)], w=[("den", c0)])
                        V(lambda e, c0=c0: e.reciprocal(den[:, c0:c0 + 1], den[:, c0:c0 + 1]), r=[("den", c0)], w=[("den", c0)])
                        V(lambda e, c0=c0, gi=gi, i=i: e.tensor_tensor(den[:, c0:c0 + 1], den[:, c0:c0 + 1], gates[:, i, gi:gi + 1], op=ALU.mult), r=[("den", c0), "gates"], w=[("den", c0)])
                        V(lambda e, g=g, c0=c0, hk=hk: e.scalar_tensor_tensor(oacc[:, hk, g, :], PSW[:, g, 0:64], den[:, c0:c0 + 1], oacc[:, hk, g, :], op0=ALU.mult, op1=ALU.add),
                          r=[("pwin", g), ("den", c0), ("oacc", hk, g)], w=[("oacc", hk, g)])
                    tiles = []
                    for kt in range(2 * i + 2):
                        mm = [(KSA[hk][:, kt * 128:(kt + 1) * 128], RQ[:, kt // 32, :], [("KSA", hk), "RQ"])]
                        if kt >= 2 * i:
                            mm.append((ident[:], cms4[:, kt - 2 * i, :, :].rearrange("p g q -> p (g q)"), ["ident", "cms4"]))
                        tiles.append({"score": mm})
                    stream(tiles, qr, lambda t, hk=hk: (VS[:, t, hk * 65:(hk + 1) * 65], [("VS", t // 8)]), lambda g: PSS[:, g, :], 65, "psel")
                    for g in range(4):
                        gi = (hk * 4 + g) * 3 + 1 + 0
                        c0 = 8 + 0 * 4 + g
                        V(lambda e, g=g, c0=c0: e.tensor_scalar(den[:, c0:c0 + 1], PSS[:, g, 64:65], 1e-30, None, op0=ALU.max), r=[("psel", g)], w=[("den", c0)])
                        V(lambda e, c0=c0: e.reciprocal(den[:, c0:c0 + 1], den[:, c0:c0 + 1]), r=[("den", c0)], w=[("den", c0)])
                        V(lambda e, c0=c0, gi=gi, i=i: e.tensor_tensor(den[:, c0:c0 + 1], den[:, c0:c0 + 1], gates[:, i, gi:gi + 1], op=ALU.mult), r=[("den", c0), "gates"], w=[("den", c0)])
                        V(lambda e, g=g, c0=c0, hk=hk: e.scalar_tensor_tensor(oacc[:, hk, g, :], PSS[:, g, 0:64], den[:, c0:c0 + 1], oacc[:, hk, g, :], op0=ALU.mult, op1=ALU.add),
                          r=[("psel", g), ("den", c0), ("oacc", hk, g)], w=[("oacc", hk, g)])
                okeys = [("oacc", hk_, g_) for hk_ in range(2) for g_ in range(4)]
                V(lambda e: e.tensor_copy(obf[:], oacc[:].rearrange("p a b c -> p (a b c)")), r=okeys, w=["obf"])
                for c in range(4):
                    TE(lambda e, c=c: e.matmul(PS[7][:, c * 128:(c + 1) * 128], obf[:, c * 128:(c + 1) * 128], ident[:], start=True, stop=True), r=["obf", "ident"], w=[psk(7)])
                V(lambda e: e.tensor_copy(oTs[:].rearrange("p a b -> p (a b)"), PS[7][:]), r=[psk(7)], w=["oTs"])
                P.dma("sync", oT_d[:, 0:4, i * 128:(i + 1) * 128], oTs[:], r=["oTs"], w=[("oTd", i)], semkey="oTs")
            P.barrier()
            P.flush()
        dump("oT_nsa", oT_d[:, 0:4, :].rearrange("p a t -> p (a t)"), [128, 4 * NOWN * 128], BF)
        dump_done()
        if stop_after <= 3:
            return nc, DBG, None

        with ExitStack() as ph:
            KSB = sbt(ph, "KSB", [128, 4, T], BF)
            VSB = sbt(ph, "VSB", [128, NT, 512], BF)
            QSB = sbt(ph, "QSB", [128, 4, NOWN * 128], BF)
            uneg = sbt(ph, "uneg", [128, 128], BF)
            nones = sbt(ph, "nones", [128, 128], BF)
            sbm = sbt(ph, "sbm", [128, 8, 512], BF)
            e1 = sbt(ph, "e1", [128, 2, 512], F32)
            spb = [sbt(ph, "spb%d" % k, [128, 2, 512], BF) for k in range(2)]
            aTb = [sbt(ph, "aTb%d" % k, [128, 2, 512], BF) for k in range(2)]
            Rb = sbt(ph, "Rb", [128, 2, 512], BF)
            osb = sbt(ph, "osb", [128, 512], BF)
            P.dma("sync", QSB[:], qsbT_d, w=["QSB"], semkey=("l", 8))
            P.dma("sync", uneg[:], IN["uneg"], w=["uneg"], semkey=("l", 9))
            P.dma("sync", nones[:], IN["negones"], w=["nones"], semkey=("l", 10))
            P.dma("sync", sbm[:].rearrange("p a b -> p (a b)"), IN["sbmask"], w=["sbm"], semkey=("l", 11))
            P.dma("sync", KSB[:, 0, :], ksbT_d[:, 0, :], w=[("KSB", 0)], semkey=("lw", 0))
            for q8 in range(8):
                P.dma("sync", VSB[:, q8 * 8:(q8 + 1) * 8, :], vsb_d[q8 * 1024:(q8 + 1) * 1024, :].rearrange("(j p) c -> p j c", p=128), w=[("VSB", q8)], semkey=("lv", q8 % 4))
            for pr in range(1, 4):
                P.dma("sync", KSB[:, pr, :], ksbT_d[:, pr, :], w=[("KSB", pr)], semkey=("lw", pr))
            PSP = [PSALL[:, 0:2, :], PSALL[:, 2:4, :], PSALL[:, 4:6, :]]
            sbq = quick and not os.environ.get("SBFULL")
            for pr in range(1 if sbq else 4):
                for g8 in range(2 if sbq else 8):
                    nkt = 8 * (g8 + 1)
                    kts = list(range(nkt - 1, -1, -1))

                    def c0_of(kt, g8=g8):
                        r = kt - 8 * g8
                        return 0 if r <= 0 else 128 * ((r - 1 + 1) // 2)

                    V(lambda e: e.memset(Rb[:], 0.0), w=["Rb"])

                    def S1(t, kts=kts, pr=pr, g8=g8):
                        kt = kts[t]
                        c0 = c0_of(kt)
                        for hh in range(2):
                            lo, hi = hh * 64, hh * 64 + 64
                            b = (t % 3) * 2 + hh
                            TE(lambda e, lo=lo, hi=hi, b=b, kt=kt, pr=pr, g8=g8, c0=c0: e.matmul(PS[b][:, c0:512], KSB[lo:hi, pr, kt * 128:(kt + 1) * 128], QSB[lo:hi, pr, g8 * 512 + c0:(g8 + 1) * 512], start=True, stop=False),
                               r=[("KSB", pr), "QSB"], w=[("psp", t % 3)])

                    def S2(t, kts=kts, g8=g8):
                        kt = kts[t]
                        c0 = c0_of(kt)
                        k2, k3 = t % 2, t % 3
                        A(lambda e, k3=k3, c0=c0: e.activation(e1[:, :, c0:512], PSP[k3][:, :, c0:512], AF.Exp), r=[("psp", k3)], w=["e1"])
                        A(lambda e, k2=k2, c0=c0: e.activation(spb[k2][:, :, c0:512], e1[:, :, c0:512], AF.Ln, bias=1.0), r=["e1"], w=[("spb", k2)])
                        if kt >= 8 * g8:
                            mk = sbm[:, kt - 8 * g8, c0:512].unsqueeze(1).to_broadcast([128, 2, 512 - c0])
                            V(lambda e, k2=k2, mk=mk, c0=c0: e.tensor_tensor(spb[k2][:, :, c0:512], spb[k2][:, :, c0:512], mk, op=ALU.mult), r=[("spb", k2), "sbm"], w=[("spb", k2)])

                    def S3(t, nkt=nkt, kts=kts):
                        k2, k3 = t % 2, t % 3
                        c0 = c0_of(kts[t])
                        for hh in range(2):
                            b = k3 * 2 + hh
                            TE(lambda e, b=b, k2=k2, hh=hh, t=t, c0=c0: e.matmul(PS[b][:, c0:512], uneg[:], spb[k2][:, hh, c0:512], start=False, stop=(t == 0)), r=[("spb", k2), "uneg"], w=[("psp", k3)])
                            if t > 0:
                                TE(lambda e, b=b, hh=hh, c0=c0: e.matmul(PS[b][:, c0:512], nones[:], Rb[:, hh, c0:512], start=False, stop=True), r=["Rb", "nones"], w=[("psp", k3)])
                        if t + 1 < nkt:
                            V(lambda e, k2=k2, c0=c0: e.tensor_tensor(Rb[:, :, c0:512], Rb[:, :, c0:512], spb[k2][:, :, c0:512], op=ALU.add), r=["Rb", ("spb", k2)], w=["Rb"])

                    def S4(t, kts=kts, g8=g8):
                        kt = kts[t]
                        c0 = c0_of(kt)
                        k2, k3 = t % 2, t % 3
                        A(lambda e, k2=k2, k3=k3, c0=c0: e.activation(aTb[k2][:, :, c0:512], PSP[k3][:, :, c0:512], AF.Exp), r=[("psp", k3)], w=[("aTb", k2)])
                        if kt >= 8 * g8:
                            mk = sbm[:, kt - 8 * g8, c0:512].unsqueeze(1).to_broadcast([128, 2, 512 - c0])
                            V(lambda e, k2=k2, mk=mk, c0=c0: e.tensor_tensor(aTb[k2][:, :, c0:512], aTb[k2][:, :, c0:512], mk, op=ALU.mult), r=[("aTb", k2), "sbm"], w=[("aTb", k2)])

                    def S5(t, kts=kts, pr=pr, nkt=nkt):
                        kt = kts[t]
                        c0 = c0_of(kt)
                        k2 = t % 2
                        for hh in range(2):
                            h = pr * 2 + hh
                            TE(lambda e, hh=hh, k2=k2, kt=kt, h=h, t=t, nkt=nkt, c0=c0: e.matmul(PS[6 + hh][hh * 64:hh * 64 + 64, c0:512], VSB[:, kt, h * 64:(h + 1) * 64], aTb[k2][:, hh, c0:512], start=(t == 0), stop=(t == nkt - 1)),
                               r=[("aTb", k2), ("VSB", kt // 8)], w=[psk(6 + hh)])

                    S1(0)
                    S1(1)
                    S2(0)
                    S3(0)
                    for s_ in range(nkt):
                        if s_ + 2 < nkt:
                            S1(s_ + 2)
                        if s_ + 1 < nkt:
                            S2(s_ + 1)
                            S3(s_ + 1)
                        S4(s_)
                        S5(s_)
                    for hh in range(2):
                        evac_copy(1, osb[hh * 64:hh * 64 + 64, :], PS[6 + hh][hh * 64:hh * 64 + 64, :], r=[psk(6 + hh)], w=["osb"])
                    P.dma("sync", oT_d[:, 4 + pr, g8 * 512:(g8 + 1) * 512], osb[:], r=["osb"], w=[("oTd", pr, g8)], semkey="osb")
            P.barrier()
            P.flush()
        dump("oT_sb", oT_d[:, 4:8, :].rearrange("p a t -> p (a t)"), [128, 4 * NOWN * 128], BF)
        dump_done()
        if stop_after <= 4:
            return nc, DBG, None

        pre2 = top.enter_context(ExitStack())
        wg2_pre = sbt(pre2, "wg2pre", [128, 8, FF], BF)
        with ExitStack() as ph:
            wout = sbt(ph, "wout", [128, 8, D], BF)
            mwq = sbt(ph, "mwq", [128, 8, 256], BF)
            mwk = sbt(ph, "mwk", [128, 8, 256], BF)
            mwv = sbt(ph, "mwv", [128, 8, 256], BF)
            mwo = sbt(ph, "mwo", [128, 2, D], BF)
            gbc = sbt(ph, "gbc", [128, D], F32)
            gkv = sbt(ph, "gkv", [128, D], F32)
            gmq = sbt(ph, "gmq", [128, 64], F32)
            gmk = sbt(ph, "gmk", [128, 64], F32)
            mt_ = sbt(ph, "mt", [128, 2, D], F32)
            mb = [sbt(ph, "mb%d" % k, [128, D], BF) for k in range(2)]
            memT = sbt(ph, "memT", [128, 8, 256], BF)
            KMT = sbt(ph, "KMT", [128, 2, 256], BF)
            VM = sbt(ph, "VM", [128, 2, 4, 65], BF)
            kmstg = sbt(ph, "kmstg", [128, 4, 64], BF)
            xt = sbt(ph, "xt", [128, 4, D], F32)
            oTb = sbt(ph, "oTb", [128, 8, 512], BF)
            hb = [sbt(ph, "hb%d" % k, [128, D], BF) for k in range(2)]
            hT = sbt(ph, "hT", [128, 8, 512], BF)
            qstg = sbt(ph, "qstg", [128, 4, 64], BF)
            qmT = sbt(ph, "qmT", [128, 2, 512], BF)
            Em = [sbt(ph, "Em%d" % k, [128, 512], BF) for k in range(2)]
            om = sbt(ph, "om", [128, 4, 256], BF)
            omT = sbt(ph, "omT", [128, 2, 512], BF)
            rd = sbt(ph, "rd", [128, 16], F32)
            junk = sbt(ph, "junk", [128, D], BF)
            ss4 = sbt(ph, "ss4", [128, 4], F32)
            rstd = sbt(ph, "rstd", [128, 4], F32)
            tm = (sbt(ph, "nsq", [128, 8, 64], F32), sbt(ph, "nss", [128, 8], F32), sbt(ph, "ny", [128, 8, 64], F32),
                  sbt(ph, "nt1", [128, 8, 8], F32), sbt(ph, "nt2", [128, 8, 8], F32))
            load_w_bf(wout, IN["w_out"], 8, "wout")
            load_w_bf(mwq, IN["mwq"], 8, "mwq")
            load_w_bf(mwk, IN["mwk"], 8, "mwk")
            load_w_bf(mwv, IN["mwv"], 8, "mwv")
            load_w_bf(mwo, IN["mwo"], 2, "mwo")
            load_w_bf(wg2_pre, IN["wg2"], 8, "wg")
            P.dma("sync", gbc[:], IN["g_memx"], w=["gbc"], semkey="c0")
            P.dma("sync", gkv[:], IN["g_memkv"], w=["gkv"], semkey="c1")
            P.dma("sync", gmq[:], IN["g_mq"], w=["gains"], semkey="c2")
            P.dma("sync", gmk[:], IN["g_mk"], w=["gains"], semkey="c3")
            P.dma("sync", mt_[:], IN["mem"].rearrange("(j p) d -> p j d", p=128), w=["mt"], semkey="c4")
            V(lambda e: e.memset(VM[:], 1.0), w=["VM"])
            rms_stats((junk, ss4, rstd), [mt_[:, j, :] for j in range(2)], ["mt", "mt"], "mm")
            for j in range(2):
                V(lambda e, j=j: e.scalar_tensor_tensor(mb[j][:], mt_[:, j, :], rstd[:, j:j + 1], gkv[:], op0=ALU.mult, op1=ALU.mult), r=["mt", "mmrstd", "gkv"], w=[("mb", j)])
                for half in range(2):
                    bank = 4 + half
                    for kk in range(4):
                        kc = half * 4 + kk
                        TE(lambda e, j=j, kc=kc, kk=kk, bank=bank: e.matmul(PS[bank][:, kk * 128:(kk + 1) * 128], mb[j][:, kc * 128:(kc + 1) * 128], ident[:], start=True, stop=True),
                           r=[("mb", j), "ident"], w=[psk(bank)])
                    evac_copy(j + half, memT[:, half * 4:half * 4 + 4, j * 128:(j + 1) * 128], PS[bank][:].rearrange("p (k t) -> p k t", k=4), r=[psk(bank)], w=[("memT", half, j)])
            mT_keys = [("memT", h_, j_) for h_ in range(2) for j_ in range(2)]
            for j in range(2):
                for kc in range(8):
                    TE(lambda e, j=j, kc=kc: e.matmul(PS[0][:, 0:256], memT[:, kc, j * 128:(j + 1) * 128], mwk[:, kc, :], start=(kc == 0), stop=(kc == 7)), r=mT_keys + ["mwkA", "mwkB"], w=[psk(0)])
                head_norm_rope(tm, PS[0][:, 0:256].rearrange("p (h d) -> p h d", h=4), 4, gmk[:].unsqueeze(1).to_broadcast([128, 4, 64]), None, kmstg[:], [psk(0)], ["kmstg"], "k")
                for pr in range(2):
                    TE(lambda e, pr=pr: e.matmul(PS[1][:, pr * 128:(pr + 1) * 128], kmstg[:, 2 * pr:2 * pr + 2, :].rearrange("p h d -> p (h d)"), ident[:], start=True, stop=True), r=["kmstg", "ident"], w=[psk(1)])
                V(lambda e, j=j: e.tensor_copy(KMT[:, :, j * 128:(j + 1) * 128], PS[1][:, 0:256].rearrange("p (a t) -> p a t", a=2)), r=[psk(1)], w=["KMT"])
                for kc in range(8):
                    TE(lambda e, j=j, kc=kc: e.matmul(PS[2][:, 0:256], memT[:, kc, j * 128:(j + 1) * 128], mwv[:, kc, :], start=(kc == 0), stop=(kc == 7)), r=mT_keys + ["mwvA", "mwvB"], w=[psk(2)])
                V(lambda e, j=j: e.tensor_copy(VM[:, j, :, 0:64], PS[2][:, 0:256].rearrange("p (h d) -> p h d", h=4)), r=[psk(2), "VM"], w=["VM"])
            for n in range(8):
                for j in range(4):
                    P.dma("sync", xt[:, j, :], x1o_d[(n * 4 + j) * 128:(n * 4 + j + 1) * 128, :], w=[("xt", j)], semkey=("xt", j))
                P.dma("sync", oTb[:], oT_d[:, :, n * 512:(n + 1) * 512], w=["oTb"], semkey="oTb")
                for j in range(4):
                    for half in range(2):
                        bank = 6 + half
                        for c in range(8):
                            TE(lambda e, j=j, half=half, c=c, bank=bank: e.matmul(PS[bank][:], oTb[:, c, j * 128:(j + 1) * 128], wout[:, c, half * 512:(half + 1) * 512], start=(c == 0), stop=(c == 7)),
                               r=["oTb", "woutA", "woutB"], w=[psk(bank)])
                        V(lambda e, j=j, half=half, bank=bank: e.tensor_tensor(xt[:, j, half * 512:(half + 1) * 512], xt[:, j, half * 512:(half + 1) * 512], PS[bank][:], op=ALU.add),
                          r=[psk(bank), ("xt", j)], w=[("xt", j)])
                if debug and n == 0:
                    for j in range(4):
                        dump("x2_%d" % j, xt[:, j, :], [128, D], r=[("xt", j)])
                rms_stats((junk, ss4, rstd), [xt[:, j, :] for j in range(4)], [("xt", j) for j in range(4)], "x")
                for j in range(4):
                    hbj = hb[j % 2]
                    V(lambda e, j=j, hbj=hbj: e.scalar_tensor_tensor(hbj[:], xt[:, j, :], rstd[:, j:j + 1], gbc[:], op0=ALU.mult, op1=ALU.mult), r=[("xt", j), "xrstd", "gbc"], w=[("hb", j % 2)])
                    for half in range(2):
                        bank = 4 + half
                        for kk in range(4):
                            kc = half * 4 + kk
                            TE(lambda e, hbj=hbj, kc=kc, kk=kk, bank=bank: e.matmul(PS[bank][:, kk * 128:(kk + 1) * 128], hbj[:, kc * 128:(kc + 1) * 128], ident[:], start=True, stop=True),
                               r=[("hb", j % 2), "ident"], w=[psk(bank)])
                        evac_copy(j + half, hT[:, half * 4:half * 4 + 4, j * 128:(j + 1) * 128], PS[bank][:].rearrange("p (k t) -> p k t", k=4), r=[psk(bank)], w=[("hT", half, j)])
                hT_keys = [("hT", h_, j_) for h_ in range(2) for j_ in range(4)]
                for j in range(4):
                    for kc in range(8):
                        TE(lambda e, j=j, kc=kc: e.matmul(PS[0][:, 0:256], hT[:, kc, j * 128:(j + 1) * 128], mwq[:, kc, :], start=(kc == 0), stop=(kc == 7)), r=hT_keys + ["mwqA", "mwqB"], w=[psk(0)])
                    head_norm_rope(tm, PS[0][:, 0:256].rearrange("p (h d) -> p h d", h=4), 4, gmq[:].unsqueeze(1).to_broadcast([128, 4, 64]), None, qstg[:], [psk(0)], ["qstg"], "k")
                    for pr in range(2):
                        TE(lambda e, pr=pr: e.matmul(PS[1][:, pr * 128:(pr + 1) * 128], qstg[:, 2 * pr:2 * pr + 2, :].rearrange("p h d -> p (h d)"), ident[:], start=True, stop=True), r=["qstg", "ident"], w=[psk(1)])
                    V(lambda e, j=j: e.tensor_copy(qmT[:, :, j * 128:(j + 1) * 128], PS[1][:, 0:256].rearrange("p (a t) -> p a t", a=2)), r=[psk(1)], w=[("qmT", j)])
                qm_keys = [("qmT", j_) for j_ in range(4)]
                PSO = [PS[4][:].rearrange("p (h c) -> p h c", h=4), PS[5][:].rearrange("p (h c) -> p h c", h=4),
                       PS[6][:].rearrange("p (h c) -> p h c", h=4), PS[7][:].rearrange("p (h c) -> p h c", h=4)]
                cnt_e = 0
                for h in range(4):
                    pr, lo = h // 2, (h % 2) * 64
                    for mt2 in range(2):
                        b = 2 + cnt_e % 2
                        E = Em[cnt_e % 2]
                        ek = ("Em", cnt_e % 2)
                        cnt_e += 1
                        TE(lambda e, pr=pr, lo=lo, mt2=mt2, b=b: e.matmul(PS[b][:], KMT[lo:lo + 64, pr, mt2 * 128:(mt2 + 1) * 128], qmT[lo:lo + 64, pr, :], start=True, stop=True), r=["KMT"] + qm_keys, w=[psk(b)])
                        A(lambda e, E=E, b=b: e.activation(E[:], PS[b][:], AF.Exp, scale=0.125), r=[psk(b)], w=[ek])
                        for j in range(4):
                            TE(lambda e, E=E, j=j, h=h, mt2=mt2: e.matmul(PSO[j][:, h, 0:65], E[:, j * 128:(j + 1) * 128], VM[:, mt2, h, :], start=(mt2 == 0), stop=(mt2 == 1)), r=[ek, "VM"], w=[psk(4 + j)])
                for j in range(4):
                    for h in range(4):
                        V(lambda e, j=j, h=h: e.reciprocal(rd[:, j * 4 + h:j * 4 + h + 1], PSO[j][:, h, 64:65]), r=[psk(4 + j)], w=[("rd", j, h)])
                        V(lambda e, j=j, h=h: e.tensor_scalar(om[:, j, h * 64:(h + 1) * 64], PSO[j][:, h, 0:64], rd[:, j * 4 + h:j * 4 + h + 1], None, op0=ALU.mult), r=[psk(4 + j), ("rd", j, h)], w=[("om", j)])
                for j in range(4):
                    for c in range(2):
                        TE(lambda e, j=j, c=c: e.matmul(PS[0][:, c * 128:(c + 1) * 128], om[:, j, c * 128:(c + 1) * 128], ident[:], start=True, stop=True), r=[("om", j), "ident"], w=[psk(0)])
                    V(lambda e, j=j: e.tensor_copy(omT[:, :, j * 128:(j + 1) * 128], PS[0][:, 0:256].rearrange("p (a t) -> p a t", a=2)), r=[psk(0)], w=[("omT", j)])
                omT_keys = [("omT", j_) for j_ in range(4)]
                for j in range(4):
                    for half in range(2):
                        bank = 2 + half
                        for c in range(2):
                            TE(lambda e, j=j, half=half, c=c, bank=bank: e.matmul(PS[bank][:], omT[:, c, j * 128:(j + 1) * 128], mwo[:, c, half * 512:(half + 1) * 512], start=(c == 0), stop=(c == 1)),
                               r=omT_keys + ["mwoA", "mwoB"], w=[psk(bank)])
                        V(lambda e, j=j, half=half, bank=bank: e.tensor_tensor(xt[:, j, half * 512:(half + 1) * 512], xt[:, j, half * 512:(half + 1) * 512], PS[bank][:], op=ALU.add),
                          r=[psk(bank), ("xt", j)], w=[("xt", j)])
                    P.dma("sync", x3o_d[(n * 4 + j) * 128:(n * 4 + j + 1) * 128, :], xt[:, j, :], r=[("xt", j)], w=[("x3d", n, j)], semkey=("st", j))
            P.barrier()
            P.flush()
        dump("x3", x3o_d[0:512, :], [512, D])
        dump_done()
        if stop_after <= 5:
            return nc, DBG, None

        ffn_phase("f2", x3o_d, out_d, 8, IN["wg2"], IN["wu2"], IN["wd2"], IN["g_ffn2"], wg_pre=wg2_pre)
        print("n_instr", P.n_instr, "n_ops", len(P.ops), "dma_sems", len(P.dma_sem))
        return nc, DBG, None


def _consts():
    c = {}
    c["freqs"] = np.broadcast_to((500000.0 ** (-np.arange(0, 16, 2, dtype=np.float32) / 16)).astype(np.float32)[None, :], (128, 8)).copy()
    c["ident"] = np.eye(128, dtype=np.float32).astype(NPBF)
    jj, ss_ = np.meshgrid(np.arange(128), np.arange(128), indexing="ij")
    c["uneg"] = (-(jj >= ss_).astype(np.float32)).astype(NPBF)
    c["negones"] = (-np.ones((128, 128), np.float32)).astype(NPBF)
    ex = np.zeros((64, 64, 128), np.float32)
    for kt in range(64):
        ex[(2 * kt) % 64, kt, 0:64] = 1.0
        ex[(2 * kt + 1) % 64, kt, 64:128] = 1.0
    c["expd2"] = ex.reshape(64, 64 * 128).astype(NPBF)
    n = np.arange(512)[:, None]
    j = np.arange(128)[None, :]
    ov = np.minimum(n * 16 + 32, j * 64 + 64) - np.maximum(n * 16, j * 64)
    ov = np.clip(ov, 0, None).astype(np.float32) / 32.0
    ov[511, :] = 0.0
    c["wov"] = ov.reshape(4, 128, 128).transpose(1, 0, 2).reshape(128, 512).astype(NPBF)
    return c


def _core_tables(p):
    t = {}
    sab = np.zeros((128, 2), np.float32)
    sab[:, p] = 1.0
    t["sab"] = sab
    nl = np.arange(128)[:, None]
    ql = np.arange(128)[None, :]
    mc = np.zeros((128, 8, 128), np.float32)
    for a in range(8):
        mc[:, a, :] = (16 * nl + 31 <= 256 * a + 128 * p + ql)
    t["maskc"] = mc.reshape(128, 1024).astype(NPBF)
    A2 = np.zeros((NOWN, 128, 128), np.float32)
    A13 = np.zeros((NOWN, 128, 128), np.float32)
    jb = np.arange(128)[None, :]
    for i in range(NOWN):
        tq = 128 * (2 * i + p) + np.arange(128)[:, None]
        cur = tq // 64
        valid = (jb * 64 <= tq)
        big = np.zeros((128, 128), np.float32)
        big = np.where(jb == 0, 1e4, big)
        big = np.where(jb == cur - 1, 2e4, big)
        big = np.where(jb == cur, 3e4, big)
        forced = big > 0
        A2[i] = (valid & ~forced)
        A13[i] = np.where(valid & forced, big, np.where(valid, 0.0, -1.0))
    t["A2"] = A2
    t["A13"] = A13
    kl = np.arange(128)[:, None]
    causal = np.where(kl <= ql, 0.0, -BIG).astype(np.float32)
    cm = np.zeros((128, 2, 128), np.float32)
    if p == 0:
        cm[:, 0, :] = causal
        cm[:, 1, :] = -BIG
    else:
        cm[:, 0, :] = 0.0
        cm[:, 1, :] = causal
    t["cmsel"] = cm.reshape(128, 256).astype(NPBF)
    wm = np.zeros((128, 6, 128), np.float32)
    for rel in range(6):
        dd = 128 * (p + 4 - rel) + ql - kl
        wm[:, rel, :] = np.where((dd >= 0) & (dd < 512), 0.0, -BIG)
    t["wmask"] = wm.reshape(128, 768).astype(NPBF)
    sm = np.zeros((128, 8, 4, 128), np.float32)
    for r in range(8):
        for m in range(4):
            sm[:, r, m, :] = (128 * (r - 2 * m - p) + kl < ql)
    t["sbmask"] = sm.reshape(128, 8 * 512).astype(NPBF)
    return t


def _prep_inputs(inp):
    consts = _consts()
    f = lambda k: np.ascontiguousarray(np.asarray(inp[k])[0], dtype=np.float32)
    rep = lambda v, n=128: np.ascontiguousarray(np.broadcast_to(np.asarray(v, np.float32)[None, :], (n, v.shape[-1])))
    shared = {
        "wg1": f("ffn1_wg"), "wu1": f("ffn1_wu"), "wd1": f("ffn1_wd"),
        "wg2": f("ffn2_wg"), "wu2": f("ffn2_wu"), "wd2": f("ffn2_wd"),
        "w_in": f("w_in"), "w_out": f("w_out"),
        "mwq": f("mem_wq"), "mwk": f("mem_wk"), "mwv": f("mem_wv"), "mwo": f("mem_wo"),
        "cw1k": f("cmp_w1_k"), "cw1v": f("cmp_w1_v"), "cw2k": f("cmp_w2_k"), "cw2v": f("cmp_w2_v"),
        "pekT": np.ascontiguousarray(f("cmp_pos_k").T), "pevT": np.ascontiguousarray(f("cmp_pos_v").T),
        "g_ffn1": rep(f("ffn1_norm")), "g_mix": rep(f("mix_norm")), "g_memx": rep(f("mem_x_norm")),
        "g_memkv": rep(f("mem_kv_norm")), "g_ffn2": rep(f("ffn2_norm")),
        "g_q": rep(f("nsa_q_norm")), "g_kc": rep(f("nsa_kc_norm")),
        "g_ksw": np.ascontiguousarray(np.concatenate([rep(f("nsa_ks_norm")), rep(f("nsa_kw_norm"))], axis=1)),
        "g_mq": rep(f("mem_q_norm")), "g_mk": rep(f("mem_k_norm")),
    }
    shared.update(consts)
    x = np.asarray(inp["x"], np.float32)
    mem = np.asarray(inp["mem"], np.float32)
    pos = np.asarray(inp["positions"]).astype(np.int32)
    tabs = [_core_tables(0), _core_tables(1)]
    in_maps = []
    for c in range(8):
        b, p = c // 2, c % 2
        m = dict(shared)
        m.update(tabs[p])
        m["x"] = np.ascontiguousarray(x[b])
        m["mem"] = np.ascontiguousarray(mem[b])
        pt = pos[b].reshape(64, 128)
        own = pt[p::2]
        cmp_pos = np.zeros(512, np.int32)
        cmp_pos[:511] = pos[b][np.arange(511) * 16 + 31]
        m["pos_cat"] = np.ascontiguousarray(np.concatenate([pt.T, own.T, cmp_pos.reshape(4, 128).T], axis=1).astype(np.int32))
        in_maps.append(m)
    return in_maps


def kernel(**inputs):
    in_maps = _prep_inputs(inputs)
    nc, _, _ = build_nc()
    res = run_bass_kernel_spmd(nc, in_maps, core_ids=list(range(8)))
    out = np.zeros((4, T, D), np.float32)
    for c in range(8):
        b, p = c // 2, c % 2
        o = np.asarray(res.results[c]["out"]).reshape(NOWN, 128, D)
        out[b].reshape(64, 128, D)[p::2] = o
    return out
```
